# Optimizing a Trainium2 kernel written in Bass

```python
import jax, jax.numpy as jnp
from jax import lax
import numpy as np

D_MODEL = 1024
BATCH = 16
SEQ = 2048
DEPTH = 4

HEAD_DIM = 64
CONV_CH = D_MODEL // 2
SB_HEADS = (D_MODEL // 2) // HEAD_DIM
FOX_HEADS = D_MODEL // HEAD_DIM
CONV_WIDTH = 31
D_FF = 256 * ((8 * D_MODEL // 3 + 255) // 256)
Q_BLOCK = 128
N_EVEN = (DEPTH + 1) // 2
N_ODD = DEPTH // 2
EPS = 1e-6
AB_IN = 2 * CONV_CH + 3 * SB_HEADS * HEAD_DIM
C_IN = 3 * FOX_HEADS * HEAD_DIM + FOX_HEADS

kernel_name = "hybrid_conv_stickbreak_fox_macaron"


def rms_norm(x, g):
    x32 = x.astype(jnp.float32)
    y = x32 * lax.rsqrt(jnp.mean(x32 * x32, axis=-1, keepdims=True) + EPS)
    return (y * g.astype(jnp.float32)).astype(x.dtype)


def layer_norm(x, g, b):
    x32 = x.astype(jnp.float32)
    mu = jnp.mean(x32, axis=-1, keepdims=True)
    xc = x32 - mu
    y = xc * lax.rsqrt(jnp.mean(xc * xc, axis=-1, keepdims=True) + EPS)
    return (y * g.astype(jnp.float32) + b.astype(jnp.float32)).astype(x.dtype)


def swiglu_ffn(h, w_in, w_out):
    g, u = jnp.split(h @ w_in, 2, axis=-1)
    return (jax.nn.silu(g) * u) @ w_out


def causal_depthwise_conv(u, w, b):
    y = lax.conv_general_dilated(
        u, w[:, None, :].astype(u.dtype), window_strides=(1,),
        padding=[(CONV_WIDTH - 1, 0)], dimension_numbers=('NWC', 'WIO', 'NWC'),
        feature_group_count=u.shape[-1])
    return y + b


def stick_breaking_attention(q, k, v):
    seq = q.shape[2]
    scale = HEAD_DIM ** -0.5
    outs = []
    for i in range(seq // Q_BLOCK):
        t0, t1 = i * Q_BLOCK, (i + 1) * Q_BLOCK
        z = jnp.einsum('bhqd,bhkd->bhqk', q[:, :, t0:t1], k[:, :, :t1]).astype(jnp.float32) * scale
        strict = jnp.arange(t1)[None, :] < jnp.arange(t0, t1)[:, None]
        log_fail = jnp.where(strict, -jax.nn.softplus(z), 0.0)
        between = lax.cumsum(log_fail, axis=3, reverse=True) - log_fail
        w = jnp.where(strict, jnp.exp(jax.nn.log_sigmoid(z) + between), 0.0)
        outs.append(jnp.einsum('bhqk,bhkd->bhqd', w.astype(v.dtype), v[:, :, :t1]))
    return jnp.concatenate(outs, axis=2)


def forgetting_attention(q, k, v, log_f):
    seq = q.shape[2]
    scale = HEAD_DIM ** -0.5
    cum = lax.cumsum(log_f, axis=2)
    outs = []
    for i in range(seq // Q_BLOCK):
        t0, t1 = i * Q_BLOCK, (i + 1) * Q_BLOCK
        z = jnp.einsum('bhqd,bhkd->bhqk', q[:, :, t0:t1], k[:, :, :t1]).astype(jnp.float32) * scale
        z = z + cum[:, :, t0:t1, None] - cum[:, :, None, :t1]
        causal = jnp.arange(t1)[None, :] <= jnp.arange(t0, t1)[:, None]
        p = jax.nn.softmax(jnp.where(causal, z, -jnp.inf), axis=-1)
        outs.append(jnp.einsum('bhqk,bhkd->bhqd', p.astype(v.dtype), v[:, :, :t1]))
    return jnp.concatenate(outs, axis=2)


def conv_sb_mixer(h, w_in, conv_w, conv_b, ln_g, ln_b, w_out):
    b, s, _ = h.shape
    proj = h @ w_in
    a_val = proj[..., :CONV_CH]
    a_gate = proj[..., CONV_CH:2 * CONV_CH]
    qkv = proj[..., 2 * CONV_CH:]
    u = a_val * jax.nn.sigmoid(a_gate)
    u = causal_depthwise_conv(u, conv_w, conv_b)
    u = jax.nn.silu(layer_norm(u, ln_g, ln_b))
    qkv = qkv.reshape(b, s, 3, SB_HEADS, HEAD_DIM).transpose(2, 0, 3, 1, 4)
    o = stick_breaking_attention(qkv[0], qkv[1], qkv[2])
    o = o.transpose(0, 2, 1, 3).reshape(b, s, SB_HEADS * HEAD_DIM)
    return jnp.concatenate([u, o], axis=-1) @ w_out


def fox_mixer(h, w_in, f_bias, q_g, k_g, w_out):
    b, s, _ = h.shape
    proj = h @ w_in
    hd = FOX_HEADS * HEAD_DIM
    qkv = proj[..., :3 * hd].reshape(b, s, 3, FOX_HEADS, HEAD_DIM).transpose(2, 0, 3, 1, 4)
    q = rms_norm(qkv[0], q_g)
    k = rms_norm(qkv[1], k_g)
    log_f = jax.nn.log_sigmoid((proj[..., 3 * hd:] + f_bias).astype(jnp.float32)).transpose(0, 2, 1)
    o = forgetting_attention(q, k, qkv[2], log_f)
    o = o.transpose(0, 2, 1, 3).reshape(b, s, hd)
    return o @ w_out


def setup_inputs(seed: int = 0) -> dict:
    key = jax.random.key(seed)
    ks = jax.random.split(key, 20)
    f32 = jnp.float32

    def w(k, shape, fan_in):
        return jax.random.normal(k, shape, f32) * (fan_in ** -0.5)

    def gain(k, shape):
        return 1.0 + 0.02 * jax.random.normal(k, shape, f32)

    def bias(k, shape):
        return 0.02 * jax.random.normal(k, shape, f32)

    return {
        "x": jax.random.normal(ks[0], (BATCH, SEQ, D_MODEL), f32),
        "ffn1_norm": gain(ks[1], (DEPTH, D_MODEL)),
        "ffn1_w_in": w(ks[2], (DEPTH, D_MODEL, 2 * D_FF), D_MODEL),
        "ffn1_w_out": w(ks[3], (DEPTH, D_FF, D_MODEL), D_FF),
        "mix_norm": gain(ks[4], (DEPTH, D_MODEL)),
        "ffn2_norm": gain(ks[5], (DEPTH, D_MODEL)),
        "ffn2_w_in": w(ks[6], (DEPTH, D_MODEL, 2 * D_FF), D_MODEL),
        "ffn2_w_out": w(ks[7], (DEPTH, D_FF, D_MODEL), D_FF),
        "ab_w_in": w(ks[8], (N_EVEN, D_MODEL, AB_IN), D_MODEL),
        "conv_w": w(ks[9], (N_EVEN, CONV_WIDTH, CONV_CH), CONV_WIDTH),
        "conv_b": bias(ks[10], (N_EVEN, CONV_CH)),
        "conv_ln_g": gain(ks[11], (N_EVEN, CONV_CH)),
        "conv_ln_b": bias(ks[12], (N_EVEN, CONV_CH)),
        "ab_w_out": w(ks[13], (N_EVEN, CONV_CH + SB_HEADS * HEAD_DIM, D_MODEL), CONV_CH + SB_HEADS * HEAD_DIM),
        "fox_w_in": w(ks[14], (N_ODD, D_MODEL, C_IN), D_MODEL),
        "fox_f_bias": jax.random.uniform(ks[15], (N_ODD, FOX_HEADS), f32, 1.0, 4.0),
        "fox_q_norm": gain(ks[16], (N_ODD, HEAD_DIM)),
        "fox_k_norm": gain(ks[17], (N_ODD, HEAD_DIM)),
        "fox_w_out": w(ks[18], (N_ODD, FOX_HEADS * HEAD_DIM, D_MODEL), FOX_HEADS * HEAD_DIM),
    }


def reference(x, ffn1_norm, ffn1_w_in, ffn1_w_out, mix_norm, ffn2_norm, ffn2_w_in, ffn2_w_out,
              ab_w_in, conv_w, conv_b, conv_ln_g, conv_ln_b, ab_w_out,
              fox_w_in, fox_f_bias, fox_q_norm, fox_k_norm, fox_w_out):
    for layer in range(DEPTH):
        x = x + 0.5 * swiglu_ffn(rms_norm(x, ffn1_norm[layer]), ffn1_w_in[layer], ffn1_w_out[layer])
        h = rms_norm(x, mix_norm[layer])
        if layer % 2 == 0:
            e = layer // 2
            x = x + conv_sb_mixer(h, ab_w_in[e], conv_w[e], conv_b[e], conv_ln_g[e], conv_ln_b[e], ab_w_out[e])
        else:
            o = layer // 2
            x = x + fox_mixer(h, fox_w_in[o], fox_f_bias[o], fox_q_norm[o], fox_k_norm[o], fox_w_out[o])
        x = x + 0.5 * swiglu_ffn(rms_norm(x, ffn2_norm[layer]), ffn2_w_in[layer], ffn2_w_out[layer])
    return x
```

```python
import contextlib
import numpy as np
import concourse.bass as bass
import concourse.mybir as mybir
from concourse.bass_utils import run_bass_kernel_spmd

F32 = mybir.dt.float32
BF16 = mybir.dt.bfloat16
AF = mybir.ActivationFunctionType
ALU = mybir.AluOpType
ENGS = ["sync", "tensor", "scalar", "vector", "gpsimd"]

D = 1024
T = 2048
DFF = 2816
NJ = DFF // 128
DEPTH = 4
CW = 31
EPS = 1e-6
NCORES = 8


class Ev:
    __slots__ = ("key", "val")

    def __init__(self, key, val):
        self.key = key
        self.val = val


class Buf:
    __slots__ = ("w", "r")

    def __init__(self):
        self.w = None
        self.r = {}


def I(meth, **kw):
    return (meth, kw)


class Prog:
    def __init__(self):
        self.q = {e: [] for e in ENGS}
        self.cnt = {}
        self.seen = {e: {} for e in ENGS}
        self.nops = {e: 0 for e in ENGS}

    def _wait(self, eng, key, val):
        if self.seen[eng].get(key, 0) >= val:
            return
        self.seen[eng][key] = val
        self.q[eng].append(("wait", key, val))

    def _waits(self, eng, reads, writes):
        for b in reads:
            if b.w is not None:
                self._wait(eng, b.w.key, b.w.val)
        for b in writes:
            if b.w is not None:
                self._wait(eng, b.w.key, b.w.val)
            for ev in b.r.values():
                self._wait(eng, ev.key, ev.val)

    def _mark(self, ev, reads, writes):
        for b in reads:
            o = b.r.get(ev.key)
            if o is None or o.val < ev.val:
                b.r[ev.key] = ev
        for b in writes:
            b.w = ev
            b.r = {}

    def op(self, eng, ins, reads=(), writes=(), key=None, inc=1):
        key = key or eng
        self._waits(eng, reads, writes)
        self.cnt[key] = self.cnt.get(key, 0) + inc
        ev = Ev(key, self.cnt[key])
        self.q[eng].append(("op", ins, key, inc))
        self._mark(ev, reads, writes)
        self.nops[eng] += 1
        return ev

    def group(self, eng, inss, reads=(), writes=(), key=None):
        key = key or eng
        self._waits(eng, reads, writes)
        self.cnt[key] = self.cnt.get(key, 0) + 1
        ev = Ev(key, self.cnt[key])
        for f in inss[:-1]:
            self.q[eng].append(("op", f, None, 0))
        self.q[eng].append(("op", inss[-1], key, 1))
        self._mark(ev, reads, writes)
        self.nops[eng] += len(inss)
        return ev

    def dma(self, eng, ins, reads=(), writes=(), key=None):
        return self.op(eng, ins, reads, writes, key=key, inc=16)

    def barrier(self):
        for e in ENGS:
            for k, v in self.cnt.items():
                self._wait(e, k, v)

    def final_wait(self, eng, bufs):
        self._waits(eng, bufs, ())

    def replay(self, eng_name, eng, sems):
        pend = []
        for it in self.q[eng_name]:
            if it[0] == "wait":
                pend.append(it)
            else:
                for w in pend[:-1]:
                    eng.wait_ge(sems[w[1]], w[2])
                ins = getattr(eng, it[1][0])(**it[1][1])
                if pend:
                    ins._wait_ge(sems[pend[-1][1]], pend[-1][2])
                pend = []
                if it[2] is not None:
                    ins.then_inc(sems[it[2]], it[3])
        for w in pend:
            eng.wait_ge(sems[w[1]], w[2])


class Arena:
    def __init__(self, tile, nwords):
        self.tile = tile
        self.n = nwords
        self.off = 0

    def reset(self):
        self.off = 0

    def alloc(self, shape, dt):
        n = int(np.prod(shape))
        nw = n if dt == F32 else (n + 1) // 2
        nw = (nw + 7) // 8 * 8
        assert self.off + nw <= self.n, f"arena overflow {self.off}+{nw}>{self.n}"
        v = self.tile[:, self.off:self.off + nw]
        self.off += nw
        if dt != F32:
            v = v.bitcast(dt)
        v = v[:, 0:n]
        if len(shape) == 2:
            v = v.rearrange("p (a b) -> p a b", b=shape[1])
        elif len(shape) == 3:
            v = v.rearrange("p (a b c) -> p a b c", b=shape[1], c=shape[2])
        return v


def cst_layout():
    off = {}
    c = 0
    for L in range(DEPTH):
        for nm in ("n1", "nm", "n2"):
            off[(nm, L)] = c
            c += 8
    for e in range(2):
        off[("cw", e)] = c
        c += 4 * CW
        for nm in ("cb", "lg", "lb"):
            off[(nm, e)] = c
            c += 4
    for o in range(2):
        for nm in ("gq", "gk", "fb"):
            off[(nm, o)] = c
            c += 1
    return off, c


CO, NCST = cst_layout()


class MK:
    def __init__(self, nseq=2, layers=(0, 1, 2, 3), parts=("f1", "mix", "f2")):
        self.nseq = nseq
        self.layers = layers
        self.parts = parts
        self.nc = bass.Bass("TRN2", target_bir_lowering=False)
        self.P = Prog()
        self.slotn = {}

    def wslot(self, name, nslots):
        n = self.slotn.get(name, 0)
        self.slotn[name] = n + 1
        return n % nslots

    def load_w(self, name, tiles, bufs, src, maxlast=None):
        s = self.wslot(name, len(tiles))
        t = tiles[s]
        nd = len(t.shape)
        if nd == 3:
            o = t.rearrange("p a b -> p (a b)")
        else:
            o = t
        kw = dict(out=o, in_=src)
        if maxlast:
            kw["max_dma_last_dim"] = maxlast
        self.P.dma("gpsimd", I("dma_start", **kw), writes=[bufs[s]], key=f"d_{name}{s}")
        return t, bufs[s]

    def norm_block(self, blk, gcol, dst, dst_sl, B_dst):
        P = self.P
        s = self.wslot("nrm", 2)
        sq, B_sq = self.sq[s], self.B_sq[s]
        rstd, B_rstd = self.rstd[s], self.B_rstd[s]
        pn, B_pn = self.bank[6], self.B_bank[6]
        t0 = blk * 512
        for kc in range(8):
            P.op("scalar", I("activation", out=sq[:, kc, :], in_=self.xT[:, kc, t0:t0 + 512], func=AF.Square),
                 reads=[self.B_x[kc][blk]], writes=[B_sq])
        P.group("tensor", [I("matmul", out=pn[:, :], lhsT=self.ones_d[:, :], rhs=sq[:, kc, :], start=(kc == 0), stop=(kc == 7))
                           for kc in range(8)], reads=[B_sq, self.B_c], writes=[B_pn])
        P.op("scalar", I("activation", out=rstd[:, :], in_=pn[:, :], func=AF.Sqrt, bias=self.epsc[:, 0:1], scale=1.0),
             reads=[B_pn, self.B_c], writes=[B_rstd])
        P.op("vector", I("reciprocal", out=rstd[:, :], in_=rstd[:, :]), reads=[B_rstd], writes=[B_rstd])
        for kc in range(8):
            P.op("vector", I("scalar_tensor_tensor", out=dst[:, kc, dst_sl], in0=self.xT[:, kc, t0:t0 + 512],
                             scalar=self.cst[:, gcol + kc:gcol + kc + 1], in1=rstd[:, :], op0=ALU.mult, op1=ALU.mult),
                 reads=[self.B_x[kc][blk], B_rstd, self.B_c], writes=[B_dst[kc]])

    def ffn(self, L, which):
        P = self.P
        A = self.arena
        P.barrier()
        A.reset()
        gcol = CO[("n1" if which == 0 else "n2", L)]
        fidx = L * 2 + which
        hT = A.alloc([8, 1024], BF16)
        aT = A.alloc([NJ, 1024], BF16)
        self.sq = [A.alloc([8, 512], BF16) for _ in range(2)]
        self.rstd = [A.alloc([512], F32) for _ in range(2)]
        self.B_sq = [Buf(), Buf()]
        self.B_rstd = [Buf(), Buf()]
        sg = [A.alloc([512], F32) for _ in range(2)]
        wi = [A.alloc([8, 256], BF16) for _ in range(3)]
        wo = [A.alloc([NJ, 128], BF16) for _ in range(2)]
        B_sg = [Buf(), Buf()]
        B_wi = [Buf() for _ in range(3)]
        B_wo = [Buf() for _ in range(2)]
        bank, B_bank = self.bank, self.B_bank
        for tb in range(2):
            B_h = [[Buf() for _ in range(2)] for _ in range(8)]
            B_a = [[Buf() for _ in range(2)] for _ in range(NJ)]
            for sub in range(2):
                self.norm_block(tb * 2 + sub, gcol, hT, slice(sub * 512, (sub + 1) * 512), [B_h[kc][sub] for kc in range(8)])
            for j in range(NJ):
                w, B_w = self.load_w("wi", wi, B_wi, self.win_d[fidx * NJ + j, :, :])
                for sub in range(2):
                    b = (j * 2 + sub) % 2
                    tsl = slice(sub * 512, (sub + 1) * 512)
                    hreads = [B_w] + [B_h[kc][sub] for kc in range(8)]
                    P.group("tensor", [I("matmul", out=bank[b][:, :], lhsT=w[:, kc, 0:128], rhs=hT[:, kc, tsl], start=(kc == 0), stop=(kc == 7))
                                       for kc in range(8)], reads=hreads, writes=[B_bank[b]])
                    P.group("tensor", [I("matmul", out=bank[2 + b][:, :], lhsT=w[:, kc, 128:256], rhs=hT[:, kc, tsl], start=(kc == 0), stop=(kc == 7))
                                       for kc in range(8)], reads=hreads, writes=[B_bank[2 + b]])
                    P.op("scalar", I("activation", out=sg[b][:, :], in_=bank[b][:, :], func=AF.Silu),
                         reads=[B_bank[b]], writes=[B_sg[b]])
                    P.op("vector", I("tensor_tensor", out=aT[:, j, tsl], in0=sg[b][:, :], in1=bank[2 + b][:, :], op=ALU.mult),
                         reads=[B_sg[b], B_bank[2 + b]], writes=[B_a[j][sub]])
            for m in range(8):
                w, B_w = self.load_w("wo", wo, B_wo, self.wout_d[fidx * 8 + m, :, :], maxlast=4096)
                for sub in range(2):
                    blk = tb * 2 + sub
                    b = 4 + (m * 2 + sub) % 2
                    tsl = slice(sub * 512, (sub + 1) * 512)
                    P.group("tensor", [I("matmul", out=bank[b][:, :], lhsT=w[:, kc, :], rhs=aT[:, kc, tsl], start=(kc == 0), stop=(kc == NJ - 1))
                                       for kc in range(NJ)], reads=[B_w] + [B_a[kc][sub] for kc in range(NJ)], writes=[B_bank[b]])
                    xs = self.xT[:, m, blk * 512:(blk + 1) * 512]
                    P.op("vector", I("scalar_tensor_tensor", out=xs, in0=bank[b][:, :], scalar=0.5, in1=xs, op0=ALU.mult, op1=ALU.add),
                         reads=[B_bank[b]], writes=[self.B_x[m][blk]])

    def mixer_even(self, L):
        P = self.P
        A = self.arena
        e = L // 2
        P.barrier()
        A.reset()
        bank, B_bank = self.bank, self.B_bank
        cst = self.cst
        TP = T + CW - 1
        wc = [A.alloc([8, 128], BF16) for _ in range(4)]
        B_wc = [Buf() for _ in range(4)]
        uraw = A.alloc([4 * TP], F32)
        upad = uraw.rearrange("p (c t) -> p c t", c=4)
        ubf = uraw.bitcast(BF16).rearrange("p (c t) -> p c t", c=4)

        def ufin(c, blk):
            o = 2 * (CW - 1 + blk * 512)
            return ubf[:, c, o:o + 512]

        B_up = [[Buf() for _ in range(4)] for _ in range(4)]
        B_pad = [Buf() for _ in range(4)]
        acc = [A.alloc([512], F32) for _ in range(2)]
        B_acc = [Buf(), Buf()]
        hT = A.alloc([8, T], BF16)
        B_h = [[Buf() for _ in range(4)] for _ in range(8)]
        mark = A.off
        self.sq = [A.alloc([8, 512], BF16) for _ in range(2)]
        self.rstd = [A.alloc([512], F32) for _ in range(2)]
        self.B_sq = [Buf(), Buf()]
        self.B_rstd = [Buf(), Buf()]
        tmp = [A.alloc([512], F32) for _ in range(2)]
        B_tmp = [Buf(), Buf()]
        for blk in range(4):
            self.norm_block(blk, CO[("nm", L)], hT, slice(blk * 512, (blk + 1) * 512), [B_h[kc][blk] for kc in range(8)])
        hall = lambda blk: [B_h[kc][blk] for kc in range(8)]

        for c in range(4):
            P.op("gpsimd", I("memset", ap=upad[:, c, 0:CW - 1], constant=0.0), writes=[B_pad[c]])
        for c in range(4):
            wv, B_wv = self.load_w("wc", wc, B_wc, self.abin_d[e * 20 + c, :, :])
            wg, B_wg = self.load_w("wc", wc, B_wc, self.abin_d[e * 20 + 4 + c, :, :])
            for blk in range(4):
                b = blk % 2
                tsl = slice(blk * 512, (blk + 1) * 512)
                P.group("tensor", [I("matmul", out=bank[b][:, :], lhsT=wv[:, kc, :], rhs=hT[:, kc, tsl], start=(kc == 0), stop=(kc == 7))
                                   for kc in range(8)], reads=[B_wv] + hall(blk), writes=[B_bank[b]])
                P.group("tensor", [I("matmul", out=bank[2 + b][:, :], lhsT=wg[:, kc, :], rhs=hT[:, kc, tsl], start=(kc == 0), stop=(kc == 7))
                                   for kc in range(8)], reads=[B_wg] + hall(blk), writes=[B_bank[2 + b]])
                P.op("scalar", I("activation", out=tmp[b][:, :], in_=bank[2 + b][:, :], func=AF.Sigmoid),
                     reads=[B_bank[2 + b]], writes=[B_tmp[b]])
                P.op("vector", I("tensor_tensor", out=upad[:, c, CW - 1 + blk * 512:CW - 1 + (blk + 1) * 512], in0=tmp[b][:, :], in1=bank[b][:, :], op=ALU.mult),
                     reads=[B_tmp[b], B_bank[b]], writes=[B_up[c][blk]])

        cw0 = CO[("cw", e)]
        cb0 = CO[("cb", e)]

        def conv_items():
            i = 0
            for blk in (3, 2, 1, 0):
                base = blk * 512
                for c in range(4):
                    a, B_a = acc[i % 2], B_acc[i % 2]
                    i += 1
                    rd = [B_up[c][blk], B_up[c][blk - 1] if blk > 0 else B_pad[c], self.B_c]
                    wcol = lambda k: cst[:, cw0 + c * CW + k:cw0 + c * CW + k + 1]
                    P.op("vector", I("tensor_scalar", out=a[:, :], in0=upad[:, c, base:base + 512], scalar1=wcol(0),
                                     scalar2=cst[:, cb0 + c:cb0 + c + 1], op0=ALU.mult, op1=ALU.add), reads=rd, writes=[B_a])
                    yield
                    for k in range(1, CW - 1):
                        P.op("vector", I("scalar_tensor_tensor", out=a[:, :], in0=upad[:, c, base + k:base + k + 512], scalar=wcol(k),
                                         in1=a[:, :], op0=ALU.mult, op1=ALU.add), reads=rd, writes=[B_a])
                        yield
                    k = CW - 1
                    home = upad[:, c, base + k:base + k + 512]
                    P.op("vector", I("scalar_tensor_tensor", out=home, in0=home, scalar=wcol(k), in1=a[:, :], op0=ALU.mult, op1=ALU.add),
                         reads=[B_a, self.B_c], writes=[B_up[c][blk]])
                    yield

        conv = conv_items()

        P.barrier()
        A.off = mark
        vtp = [A.alloc([16, 128], BF16) for _ in range(1)]
        B_vtp = [[Buf() for _ in range(4)] for _ in range(1)]
        oT = A.alloc([4, T], BF16)
        B_o = [[Buf() for _ in range(4)] for _ in range(4)]
        qT = [A.alloc([T], BF16) for _ in range(1)]
        kT = [A.alloc([T], BF16) for _ in range(1)]
        B_q = [[Buf() for _ in range(4)] for _ in range(1)]
        B_k = [[Buf() for _ in range(4)] for _ in range(1)]
        mark2 = A.off
        NS = 4
        et = [A.alloc([512], F32) for _ in range(NS)]
        spb = [A.alloc([512], BF16) for _ in range(NS)]
        ssf = A.alloc([512], F32)
        ssb = [A.alloc([512], BF16) for _ in range(2)]
        At = [A.alloc([512], BF16) for _ in range(3)]
        B_et = [Buf() for _ in range(NS)]
        B_spb = [Buf() for _ in range(NS)]
        B_ssf = Buf()
        B_ssb = [Buf(), Buf()]
        B_At = [Buf(), Buf(), Buf()]
        st = dict(ss_cur=0, acnt=0)
        s2q = []
        s3q = []

        def advance():
            old3 = s3q.pop(0) if s3q else None
            if s2q:
                s2q.pop(0)()
            if old3:
                old3()

        def flush_all():
            while s2q or s3q:
                advance()

        cnt = 0
        ocnt = 0
        for pr in range(4):
            ps_ = 0
            flush_all()
            wq, B_wq = self.load_w("wc", wc, B_wc, self.abin_d[e * 20 + 8 + pr, :, :])
            wk, B_wk = self.load_w("wc", wc, B_wc, self.abin_d[e * 20 + 12 + pr, :, :])
            wv_, B_wv_ = self.load_w("wc", wc, B_wc, self.abin_d[e * 20 + 16 + pr, :, :])
            for g_ in range(4):
                vb = 6 + g_ % 2
                P.group("tensor", [I("matmul", out=bank[vb][:, i * 128:(i + 1) * 128], lhsT=hT[:, kc, (g_ * 4 + i) * 128:(g_ * 4 + i + 1) * 128], rhs=wv_[:, kc, :],
                                     start=(kc == 0), stop=(kc == 7)) for i in range(4) for kc in range(8)],
                        reads=[B_wv_] + hall(g_), writes=[B_bank[vb]])
                P.op("scalar", I("activation", out=vtp[ps_][:, g_ * 4:(g_ + 1) * 4, :], in_=bank[vb][:, :].rearrange("p (a b) -> p a b", b=128), func=AF.Copy),
                     reads=[B_bank[vb]], writes=[B_vtp[ps_][g_]])
            for blk in range(4):
                tsl = slice(blk * 512, (blk + 1) * 512)
                P.group("tensor", [I("matmul", out=bank[6][:, :], lhsT=wq[:, kc, :], rhs=hT[:, kc, tsl], start=(kc == 0), stop=(kc == 7))
                                   for kc in range(8)], reads=[B_wq] + hall(blk), writes=[B_bank[6]])
                P.op("scalar", I("activation", out=qT[ps_][:, tsl], in_=bank[6][:, :], func=AF.Copy, scale=0.125),
                     reads=[B_bank[6]], writes=[B_q[ps_][blk]])
                P.group("tensor", [I("matmul", out=bank[7][:, :], lhsT=wk[:, kc, :], rhs=hT[:, kc, tsl], start=(kc == 0), stop=(kc == 7))
                                   for kc in range(8)], reads=[B_wk] + hall(blk), writes=[B_bank[7]])
                P.op("scalar", I("activation", out=kT[ps_][:, tsl], in_=bank[7][:, :], func=AF.Copy), reads=[B_bank[7]], writes=[B_k[ps_][blk]])
            for half in range(2):
                psl = slice(half * 64, half * 64 + 64)
                for qb in range(4):
                    qs = slice(qb * 512, (qb + 1) * 512)
                    n = 4 * qb + 4
                    ob = 4 + (ocnt % 2)
                    ocnt += 1
                    for It in reversed(range(n)):
                        r = It - 4 * qb
                        b = cnt % 2
                        e4 = cnt % NS
                        cnt += 1
                        first = (It == n - 1)
                        ks = slice(It * 128, (It + 1) * 128)
                        qk = dict(lhsT=kT[ps_][psl, ks], rhs=qT[ps_][psl, qs])
                        qkr = [B_k[ps_][It // 4], B_q[ps_][qb]]
                        P.op("tensor", I("matmul", out=bank[b][:, :], start=True, stop=True, **qk), reads=qkr, writes=[B_bank[b]])
                        P.op("scalar", I("activation", out=et[e4][:, :], in_=bank[b][:, :], func=AF.Exp), reads=[B_bank[b]], writes=[B_et[e4]])
                        P.op("scalar", I("activation", out=spb[e4][:, :], in_=et[e4][:, :], func=AF.Ln, bias=self.onec[:, 0:1], scale=1.0),
                             reads=[B_et[e4], self.B_c], writes=[B_spb[e4]])
                        if r >= 0:
                            P.op("vector", I("tensor_tensor", out=spb[e4][:, :], in0=spb[e4][:, :], in1=self.msb[:, r, :], op=ALU.mult),
                                 reads=[self.B_c], writes=[B_spb[e4]])

                        def s2(It=It, r=r, b=b, e4=e4, first=first, qk=qk, qkr=qkr, ob=ob, n=n, ps_=ps_, psl=psl, pr=pr, qs=qs, qb=qb):
                            cur = st["ss_cur"]
                            a3 = st["acnt"] % 3
                            st["acnt"] += 1
                            mm = [I("matmul", out=bank[2 + b][:, :], start=True, stop=False, **qk),
                                  I("matmul", out=bank[2 + b][:, :], lhsT=self.negtri[:, :], rhs=spb[e4][:, :], start=False, stop=first)]
                            rd = qkr + [B_spb[e4], self.B_c]
                            if not first:
                                mm.append(I("matmul", out=bank[2 + b][:, :], lhsT=self.negones[:, :], rhs=ssb[cur][:, :], start=False, stop=True))
                                rd.append(B_ssb[cur])
                            P.group("tensor", mm, reads=rd, writes=[B_bank[2 + b]])
                            P.op("scalar", I("activation", out=At[a3][:, :], in_=bank[2 + b][:, :], func=AF.Exp), reads=[B_bank[2 + b]], writes=[B_At[a3]])
                            if r >= 0:
                                P.op("vector", I("tensor_tensor", out=At[a3][:, :], in0=At[a3][:, :], in1=self.msb[:, r, :], op=ALU.mult),
                                     reads=[self.B_c], writes=[B_At[a3]])
                            if It > 0:
                                sn = 1 - cur
                                if first:
                                    P.op("vector", I("tensor_copy", out=ssb[sn][:, :], in_=spb[e4][:, :]), reads=[B_spb[e4]], writes=[B_ssb[sn]])
                                    P.op("vector", I("tensor_copy", out=ssf[:, :], in_=spb[e4][:, :]), reads=[B_spb[e4]], writes=[B_ssf])
                                else:
                                    P.op("vector", I("tensor_tensor", out=ssb[sn][:, :], in0=ssf[:, :], in1=spb[e4][:, :], op=ALU.add),
                                         reads=[B_ssf, B_spb[e4]], writes=[B_ssb[sn]])
                                    P.op("vector", I("tensor_tensor", out=ssf[:, :], in0=ssf[:, :], in1=spb[e4][:, :], op=ALU.add),
                                         reads=[B_spb[e4]], writes=[B_ssf])
                                st["ss_cur"] = sn

                            def s3():
                                P.op("tensor", I("matmul", out=bank[ob][:, :], lhsT=vtp[ps_][:, It, :], rhs=At[a3][:, :], start=first, stop=(It == 0)),
                                     reads=[B_vtp[ps_][It // 4], B_At[a3]], writes=[B_bank[ob]])
                                if It == 0:
                                    P.op("scalar", I("activation", out=oT[psl, pr, qs], in_=bank[ob][psl, :], func=AF.Copy), reads=[B_bank[ob]], writes=[B_o[pr][qb]])
                            s3q.append(s3)

                        advance()
                        s2q.append(s2)
                        next(conv, None)
                        next(conv, None)
        flush_all()
        for _ in conv:
            pass

        P.barrier()
        A.off = mark2
        ysq = A.alloc([4, 512], F32)
        B_ysq = Buf()
        m2 = A.alloc([512], F32)
        var = A.alloc([512], F32)
        B_m2 = Buf()
        B_var = Buf()
        t1 = [A.alloc([512], F32) for _ in range(2)]
        B_t1 = [Buf(), Buf()]
        lg0, lb0 = CO[("lg", e)], CO[("lb", e)]
        for blk in range(4):
            ysl = slice(CW - 1 + blk * 512, CW - 1 + (blk + 1) * 512)
            B_yb = [B_up[c][blk] for c in range(4)]
            for c in range(4):
                P.op("scalar", I("activation", out=ysq[:, c, :], in_=upad[:, c, ysl], func=AF.Square), reads=[B_yb[c]], writes=[B_ysq])
            P.group("tensor", [I("matmul", out=bank[0][:, :], lhsT=self.ones_ln[:, :], rhs=upad[:, c, ysl], start=(c == 0), stop=(c == 3))
                               for c in range(4)], reads=B_yb + [self.B_c], writes=[B_bank[0]])
            P.group("tensor", [I("matmul", out=bank[1][:, :], lhsT=self.ones_ln[:, :], rhs=ysq[:, c, :], start=(c == 0), stop=(c == 3))
                               for c in range(4)], reads=[B_ysq, self.B_c], writes=[B_bank[1]])
            P.op("scalar", I("activation", out=m2[:, :], in_=bank[0][:, :], func=AF.Square), reads=[B_bank[0]], writes=[B_m2])
            P.op("vector", I("scalar_tensor_tensor", out=var[:, :], in0=m2[:, :], scalar=-1.0, in1=bank[1][:, :], op0=ALU.mult, op1=ALU.add),
                 reads=[B_m2, B_bank[1]], writes=[B_var])
            P.op("vector", I("tensor_scalar", out=var[:, :], in0=var[:, :], scalar1=0.0, scalar2=None, op0=ALU.max), reads=[B_var], writes=[B_var])
            P.op("scalar", I("activation", out=var[:, :], in_=var[:, :], func=AF.Sqrt, bias=self.epsc[:, 0:1], scale=1.0),
                 reads=[B_var, self.B_c], writes=[B_var])
            P.op("vector", I("reciprocal", out=var[:, :], in_=var[:, :]), reads=[B_var], writes=[B_var])
            for c in range(4):
                b = c % 2
                P.op("vector", I("tensor_tensor", out=t1[b][:, :], in0=upad[:, c, ysl], in1=bank[0][:, :], op=ALU.subtract),
                     reads=[B_yb[c], B_bank[0]], writes=[B_t1[b]])
                P.op("vector", I("tensor_tensor", out=t1[b][:, :], in0=t1[b][:, :], in1=var[:, :], op=ALU.mult),
                     reads=[B_var], writes=[B_t1[b]])
                P.op("scalar", I("activation", out=ufin(c, blk), in_=t1[b][:, :], func=AF.Silu,
                                 scale=cst[:, lg0 + c:lg0 + c + 1], bias=cst[:, lb0 + c:lb0 + c + 1]),
                     reads=[B_t1[b], self.B_c], writes=[B_yb[c]])

        for m in range(8):
            w, B_w = self.load_w("wc", wc, B_wc, self.about_d[e * 8 + m, :, :])
            for blk in range(4):
                b = 6 + blk % 2
                tsl = slice(blk * 512, (blk + 1) * 512)
                mm = [I("matmul", out=bank[b][:, :], lhsT=w[:, kc, :], rhs=(ufin(kc, blk) if kc < 4 else oT[:, kc - 4, tsl]),
                        start=(kc == 0), stop=(kc == 7)) for kc in range(8)]
                P.group("tensor", mm, reads=[B_w] + [B_up[c][blk] for c in range(4)] + [B_o[p_][blk] for p_ in range(4)], writes=[B_bank[b]])
                xs = self.xT[:, m, tsl]
                P.op("vector", I("tensor_tensor", out=xs, in0=bank[b][:, :], in1=xs, op=ALU.add), reads=[B_bank[b]], writes=[self.B_x[m][blk]])

    def mixer_odd(self, L):
        P = self.P
        A = self.arena
        o = L // 2
        P.barrier()
        A.reset()
        bank, B_bank = self.bank, self.B_bank
        cst = self.cst
        hT = A.alloc([8, T], BF16)
        B_h = [[Buf() for _ in range(4)] for _ in range(8)]
        mark = A.off
        self.sq = [A.alloc([8, 512], BF16) for _ in range(2)]
        self.rstd = [A.alloc([512], F32) for _ in range(2)]
        self.B_sq = [Buf(), Buf()]
        self.B_rstd = [Buf(), Buf()]
        for blk in range(4):
            self.norm_block(blk, CO[("nm", L)], hT, slice(blk * 512, (blk + 1) * 512), [B_h[kc][blk] for kc in range(8)])
        hall = lambda blk: [B_h[kc][blk] for kc in range(8)]
        P.barrier()
        A.off = mark
        wf = A.alloc([8, 16], BF16)
        B_wf = Buf()
        lf = A.alloc([T], F32)
        cpos = A.alloc([T], F32)
        cp3 = A.alloc([3, T], BF16)
        r1 = lf
        cposT = A.alloc([16, 16], F32)
        nfb = A.alloc([1], F32)
        tmpf = A.alloc([512], F32)
        B_lf, B_cpos, B_cp3, B_cposT, B_nfb, B_tmpf = (Buf() for _ in range(6))
        B_r1 = B_lf
        P.dma("gpsimd", I("dma_start", out=wf.rearrange("p a b -> p (a b)"), in_=self.fxf_d[o, :, :]), writes=[B_wf], key="d_wf")
        P.op("vector", I("tensor_scalar", out=nfb[0:16, :], in0=cst[0:16, CO[("fb", o)]:CO[("fb", o)] + 1], scalar1=-1.0, scalar2=None, op0=ALU.mult),
             reads=[self.B_c], writes=[B_nfb])
        for blk in range(4):
            tsl = slice(blk * 512, (blk + 1) * 512)
            P.group("tensor", [I("matmul", out=bank[6][0:16, :], lhsT=wf[:, kc, :], rhs=hT[:, kc, tsl], start=(kc == 0), stop=(kc == 7))
                               for kc in range(8)], reads=[B_wf] + hall(blk), writes=[B_bank[6]])
            P.op("scalar", I("activation", out=tmpf[0:16, :], in_=bank[6][0:16, :], func=AF.Exp, scale=-1.0, bias=nfb[0:16, 0:1]),
                 reads=[B_bank[6], B_nfb], writes=[B_tmpf])
            P.op("scalar", I("activation", out=lf[0:16, tsl], in_=tmpf[0:16, :], func=AF.Ln, bias=self.onec[0:16, 0:1], scale=1.0),
                 reads=[B_tmpf, self.B_c], writes=[B_lf])
        P.op("vector", I("tensor_tensor_scan", out=cpos[0:16, :], data0=lf[0:16, :], data1=lf[0:16, :], initial=0.0, op0=ALU.add, op1=ALU.max),
             reads=[B_lf], writes=[B_cpos])
        P.op("vector", I("tensor_scalar", out=r1[0:16, :], in0=cpos[0:16, :], scalar1=-1.0, scalar2=None, op0=ALU.mult), reads=[B_cpos], writes=[B_r1])
        P.op("vector", I("tensor_copy", out=cp3[0:16, 0, :], in_=r1[0:16, :]), reads=[B_r1], writes=[B_cp3])
        P.op("vector", I("tensor_tensor", out=r1[0:16, :], in0=r1[0:16, :], in1=cp3[0:16, 0, :], op=ALU.subtract), reads=[B_cp3], writes=[B_r1])
        P.op("vector", I("tensor_copy", out=cp3[0:16, 1, :], in_=r1[0:16, :]), reads=[B_r1], writes=[B_cp3])
        P.op("vector", I("tensor_tensor", out=r1[0:16, :], in0=r1[0:16, :], in1=cp3[0:16, 1, :], op=ALU.subtract), reads=[B_cp3], writes=[B_r1])
        P.op("vector", I("tensor_copy", out=cp3[0:16, 2, :], in_=r1[0:16, :]), reads=[B_r1], writes=[B_cp3])
        for It in range(16):
            b = 6 + It % 2
            P.op("tensor", I("transpose", out=bank[b][:, 0:16], in_=cpos[0:16, It * 128:(It + 1) * 128], identity=self.ident[0:16, 0:16]),
                 reads=[B_cpos, self.B_c], writes=[B_bank[b]])
            P.op("vector", I("tensor_copy", out=cposT[:, It, :], in_=bank[b][:, 0:16]), reads=[B_bank[b]], writes=[B_cposT])
        wq4 = [A.alloc([8, 64], BF16) for _ in range(6)]
        B_wq4 = [Buf() for _ in range(6)]
        vtp = [A.alloc([16, 128], BF16) for _ in range(2)]
        B_vtp = [[Buf() for _ in range(4)] for _ in range(2)]
        wout = [A.alloc([D], BF16) for _ in range(2)]
        B_wout = [Buf(), Buf()]
        qa = [A.alloc([T], BF16) for _ in range(2)]
        ka = [A.alloc([T], BF16) for _ in range(2)]
        B_qa = [[Buf() for _ in range(5)] for _ in range(2)]
        B_ka = [[Buf() for _ in range(5)] for _ in range(2)]
        oTp = [A.alloc([T], BF16) for _ in range(2)]
        B_oTp = [[Buf() for _ in range(4)] for _ in range(2)]
        sqh = [A.alloc([512], BF16) for _ in range(2)]
        rsh = [A.alloc([512], F32) for _ in range(2)]
        B_sqh = [Buf(), Buf()]
        B_rsh = [Buf(), Buf()]
        At = [A.alloc([512], BF16) for _ in range(4)]
        B_At = [Buf() for _ in range(4)]
        rden = [A.alloc([512], F32) for _ in range(2)]
        B_rden = [Buf(), Buf()]
        for s in range(2):
            P.op("gpsimd", I("memset", ap=ka[s][64:128, :], constant=0.0), writes=[B_ka[s][4]])
            P.op("gpsimd", I("memset", ap=qa[s][64:128, :], constant=0.0), writes=[B_qa[s][4]])
            P.op("gpsimd", I("memset", ap=ka[s][64:67, :], constant=1.0), writes=[B_ka[s][4]])
        P.op("gpsimd", I("memset", ap=vtp[0][:, :, 64:128], constant=1.0), writes=B_vtp[0])
        P.op("gpsimd", I("memset", ap=vtp[1][:, :, 0:64], constant=1.0), writes=B_vtp[1])
        self._pcnt = 0

        wld = {}

        def preload(h):
            wld[h] = [self.load_w("wq4", wq4, B_wq4, self.fxqk_d[(o * 3 + j) * 16 + h, :, :]) for j in range(3)]

        def proj(h):
            s = h % 2
            (wq, B_wq), (wk, B_wk) = wld[h][0], wld[h][1]
            for j in range(3):
                P.dma("sync", I("dma_start", out=qa[s][64 + j:65 + j, :], in_=cp3[h:h + 1, j, :]), reads=[B_cp3], writes=[B_qa[s][4]], key=f"d_aug{s}")
            gq, gk = CO[("gq", o)], CO[("gk", o)]
            tails = []
            for blk in range(4):
                tsl = slice(blk * 512, (blk + 1) * 512)
                u = self._pcnt % 2
                pb = 5 + self._pcnt % 2
                self._pcnt += 1
                P.group("tensor", [I("matmul", out=bank[pb][0:64, :], lhsT=wq[:, kc, :], rhs=hT[:, kc, tsl], start=(kc == 0), stop=(kc == 7)) for kc in range(8)]
                        + [I("matmul", out=bank[pb][64:128, :], lhsT=wk[:, kc, :], rhs=hT[:, kc, tsl], start=(kc == 0), stop=(kc == 7)) for kc in range(8)],
                        reads=[B_wq, B_wk] + hall(blk), writes=[B_bank[pb]])
                P.op("scalar", I("activation", out=sqh[u][:, :], in_=bank[pb][:, :], func=AF.Square), reads=[B_bank[pb]], writes=[B_sqh[u]])

                def tail(u=u, pb=pb, tsl=tsl, blk=blk):
                    P.op("tensor", I("matmul", out=bank[7][:, :], lhsT=self.ones64bd[:, :], rhs=sqh[u][:, :], start=True, stop=True),
                         reads=[B_sqh[u], self.B_c], writes=[B_bank[7]])
                    P.op("scalar", I("activation", out=rsh[u][:, :], in_=bank[7][:, :], func=AF.Ln, bias=self.qkbias[:, 0:1], scale=self.qkscale[:, 0:1]),
                         reads=[B_bank[7], self.B_c], writes=[B_rsh[u]])
                    P.op("scalar", I("activation", out=rsh[u][:, :], in_=rsh[u][:, :], func=AF.Exp, scale=-0.5), reads=[B_rsh[u]], writes=[B_rsh[u]])
                    P.op("vector", I("scalar_tensor_tensor", out=qa[s][0:64, tsl], in0=bank[pb][0:64, :], scalar=cst[0:64, gq:gq + 1], in1=rsh[u][0:64, :],
                                     op0=ALU.mult, op1=ALU.mult), reads=[B_bank[pb], B_rsh[u], self.B_c], writes=[B_qa[s][blk]])
                    P.op("vector", I("scalar_tensor_tensor", out=ka[s][0:64, tsl], in0=bank[pb][64:128, :], scalar=cst[64:128, gk:gk + 1], in1=rsh[u][64:128, :],
                                     op0=ALU.mult, op1=ALU.mult), reads=[B_bank[pb], B_rsh[u], self.B_c], writes=[B_ka[s][blk]])

                if tails:
                    tails.pop(0)()
                tails.append(tail)
                yield
            while tails:
                tails.pop(0)()
            yield

        def vproj(h):
            s = h % 2
            vc = slice(s * 64, s * 64 + 64)
            w, B_w = wld[h][2]
            for g in range(4):
                P.group("tensor", [I("matmul", out=bank[7][:, i * 64:(i + 1) * 64], lhsT=hT[:, kc, (g * 4 + i) * 128:(g * 4 + i + 1) * 128], rhs=w[:, kc, :],
                                     start=(kc == 0), stop=(kc == 7)) for i in range(4) for kc in range(8)],
                        reads=[B_w] + hall(g), writes=[B_bank[7]])
                P.op("vector", I("tensor_copy", out=vtp[s][:, g * 4:(g + 1) * 4, vc], in_=bank[7][:, 0:256].rearrange("p (a b) -> p a b", b=64)),
                     reads=[B_bank[7]], writes=[B_vtp[s][g]])
                yield

        pend = []

        def flush():
            while pend:
                pend.pop(0)()

        cnt = 0
        ocnt = 0
        preload(0)
        preload(1)
        for _ in proj(0):
            pass
        for _ in vproj(0):
            pass
        for pr in range(8):
            vs = pr % 2
            wo_, B_wo_ = self.load_w("wfo", wout, B_wout, self.fxout_d[o * 8 + pr, :, :])
            for half in range(2):
                h = 2 * pr + half
                s = h % 2
                flush()
                work = iter(())
                if h + 2 < 16:
                    preload(h + 2)
                if h + 1 < 16:
                    def both(hn=h + 1):
                        yield from proj(hn)
                        yield from vproj(hn)
                    work = both()
                tix = 0
                psl = slice(half * 64, half * 64 + 64)
                osl = slice((1 - half) * 64, (1 - half) * 64 + 64)
                for qb in range(4):
                    n = 4 * qb + 4
                    ob = 3 + (ocnt % 2)
                    ocnt += 1
                    order = list(range(n)) if qb == 0 else [0] + list(range(4 * qb, n)) + list(range(1, 4 * qb))
                    for oi, It in enumerate(order):
                        r = It - 4 * qb
                        c0 = max(r, 0) * 128
                        b = cnt % 3
                        is_first = (oi == 0)
                        is_last = (oi == n - 1)
                        a3 = cnt % 4
                        cnt += 1
                        ks = slice(It * 128, (It + 1) * 128)
                        qs = slice(qb * 512 + c0, (qb + 1) * 512)
                        cs = slice(c0, 512)
                        P.op("tensor", I("matmul", out=bank[b][:, cs], lhsT=ka[s][:, ks], rhs=qa[s][:, qs], start=True, stop=True),
                             reads=[B_ka[s][It // 4], B_ka[s][4], B_qa[s][qb], B_qa[s][4]], writes=[B_bank[b]])
                        P.op("scalar", I("activation", out=At[a3][:, cs], in_=bank[b][:, cs], func=AF.Exp, bias=cposT[:, It, h:h + 1], scale=1.0),
                             reads=[B_bank[b], B_cposT], writes=[B_At[a3]])
                        if r >= 0:
                            P.op("gpsimd", I("affine_select", out=At[a3][:, cs], in_=At[a3][:, cs], pattern=[[1, 512 - c0]], compare_op=ALU.is_ge,
                                             fill=0.0, base=0, channel_multiplier=-1), reads=[], writes=[B_At[a3]])

                        def pv(It=It, a3=a3, cs=cs, ob=ob, n=n, qb=qb, s=s, psl=psl, osl=osl, vs=vs, is_first=is_first, is_last=is_last):
                            P.op("tensor", I("matmul", out=bank[ob][:, cs], lhsT=vtp[s][:, It, :], rhs=At[a3][:, cs], start=is_first, stop=is_last),
                                 reads=[B_vtp[s][It // 4], B_At[a3]], writes=[B_bank[ob]])
                            if is_last:
                                u = qb % 2
                                P.op("vector", I("reciprocal", out=rden[u][psl, :], in_=bank[ob][osl, :]), reads=[B_bank[ob]], writes=[B_rden[u]])
                                P.op("vector", I("tensor_tensor", out=oTp[vs][psl, qb * 512:(qb + 1) * 512], in0=bank[ob][psl, :], in1=rden[u][psl, :], op=ALU.mult),
                                     reads=[B_bank[ob], B_rden[u]], writes=[B_oTp[vs][qb]])

                        if len(pend) >= 2:
                            pend.pop(0)()
                        pend.append(pv)
                        tix += 1
                        if tix % 4 == 0:
                            next(work, None)
                for _ in work:
                    pass
            flush()
            for m in range(8):
                for blk in range(4):
                    b = 6 + (m * 4 + blk) % 2
                    tsl = slice(blk * 512, (blk + 1) * 512)
                    P.op("tensor", I("matmul", out=bank[b][:, :], lhsT=wo_[:, m * 128:(m + 1) * 128], rhs=oTp[vs][:, tsl], start=True, stop=True),
                         reads=[B_wo_, B_oTp[vs][blk]], writes=[B_bank[b]])
                    xs = self.xT[:, m, tsl]
                    P.op("vector", I("tensor_tensor", out=xs, in0=bank[b][:, :], in1=xs, op=ALU.add), reads=[B_bank[b]], writes=[self.B_x[m][blk]])

    def build(self):
        nc, P = self.nc, self.P
        ns = self.nseq
        dt_in = lambda name, shape: nc.dram_tensor(name, shape, F32, kind="ExternalInput").ap()
        self.xT_d = dt_in("xT", [ns, D, T])
        self.cst_d = dt_in("cst", [128, NCST])
        self.win_d = dt_in("win", [8 * NJ, 128, 8 * 256])
        self.wout_d = dt_in("wout", [8 * 8, 128, NJ * 128])
        self.abin_d = dt_in("abin", [2 * 20, 128, 8 * 128])
        self.about_d = dt_in("about", [2 * 8, 128, 8 * 128])
        self.fxqk_d = dt_in("fxqk", [2 * 3 * 16, 128, 8 * 64])
        self.fxf_d = dt_in("fxf", [2, 128, 8 * 16])
        self.fxout_d = dt_in("fxout", [2 * 8, 128, D])
        self.yT_d = nc.dram_tensor("yT", [ns, D, T], F32, kind="ExternalOutput").ap()
        with contextlib.ExitStack() as es:
            sb = lambda name, shape, dt: es.enter_context(nc.sbuf_tensor(name, shape, dt))
            self.xT = sb("xTs", [128, 8, T], F32)
            self.cst = sb("csts", [128, NCST], F32)
            self.ones_d = sb("ones_d", [128, 128], BF16)
            self.ones_ln = sb("ones_ln", [128, 128], F32)
            self.ones64bd = sb("ones64bd", [128, 128], BF16)
            self.qkscale = sb("qkscale", [128, 1], F32)
            self.qkbias = sb("qkbias", [128, 1], F32)
            self.ones_den = sb("ones_den", [128, 128], BF16)
            self.negones = sb("negones", [128, 128], BF16)
            self.negtri = sb("negtri", [128, 128], BF16)
            self.ident = sb("ident", [128, 128], F32)
            self.onesf = sb("onesf", [128, 128], F32)
            self.epsc = sb("epsc", [128, 1], F32)
            self.eps64c = sb("eps64c", [128, 1], F32)
            self.onec = sb("onec", [128, 1], F32)
            self.msb = sb("msb", [128, 4, 512], BF16)
            self.onesb = sb("onesb", [128, 512], BF16)
            SCRW = 33400
            scr = sb("scr", [128, SCRW], F32)
            self.arena = Arena(scr, SCRW)
            self.bank = [es.enter_context(nc.psum_tensor(f"bank{i}", [128, 512], F32)) for i in range(8)]
            self.B_bank = [Buf() for _ in range(8)]
            self.B_x = [[Buf() for _ in range(4)] for _ in range(8)]
            self.B_c = Buf()
            B_c = self.B_c
            P.dma("sync", I("dma_start", out=self.cst[:, :], in_=self.cst_d[:, :]), writes=[B_c], key="d_cst")
            g = "gpsimd"
            P.op(g, I("memset", ap=self.ones_d[:, :], constant=1.0 / D), writes=[B_c])
            P.op(g, I("memset", ap=self.ones_ln[:, :], constant=1.0 / 512), writes=[B_c])
            P.op(g, I("memset", ap=self.ones64bd[:, :], constant=0.0), writes=[B_c])
            P.op(g, I("memset", ap=self.ones64bd[0:64, 0:64], constant=1.0 / 64), writes=[B_c])
            P.op(g, I("memset", ap=self.ones64bd[64:128, 64:128], constant=1.0 / 64), writes=[B_c])
            P.op(g, I("memset", ap=self.qkscale[0:64, :], constant=64.0), writes=[B_c])
            P.op(g, I("memset", ap=self.qkscale[64:128, :], constant=1.0), writes=[B_c])
            P.op(g, I("memset", ap=self.qkbias[0:64, :], constant=64.0 * EPS), writes=[B_c])
            P.op(g, I("memset", ap=self.qkbias[64:128, :], constant=EPS), writes=[B_c])
            P.op(g, I("memset", ap=self.ones_den[:, :], constant=1.0), writes=[B_c])
            P.op(g, I("memset", ap=self.negones[:, :], constant=-1.0), writes=[B_c])
            P.op(g, I("memset", ap=self.onesf[:, :], constant=1.0), writes=[B_c])
            P.op(g, I("memset", ap=self.onesb[:, :], constant=1.0), writes=[B_c])
            P.op(g, I("memset", ap=self.epsc[:, :], constant=EPS), writes=[B_c])
            P.op(g, I("memset", ap=self.eps64c[:, :], constant=64.0 * EPS), writes=[B_c])
            P.op(g, I("memset", ap=self.onec[:, :], constant=1.0), writes=[B_c])
            P.op(g, I("affine_select", out=self.negtri[:, :], in_=self.negones[:, :], pattern=[[-1, 128]], compare_op=ALU.is_ge, fill=0.0,
                      base=0, channel_multiplier=1), writes=[B_c])
            P.op(g, I("affine_select", out=self.ident[:, :], in_=self.onesf[:, :], pattern=[[-1, 128]], compare_op=ALU.is_equal, fill=0.0,
                      base=0, channel_multiplier=1), writes=[B_c])
            for r in range(4):
                P.op(g, I("affine_select", out=self.msb[:, r, :], in_=self.onesb[:, :], pattern=[[1, 512]], compare_op=ALU.is_gt, fill=0.0,
                          base=-r * 128, channel_multiplier=-1), writes=[B_c])
            B_out = Buf()
            for sq_ in range(ns):
                xv = self.xT_d[sq_].rearrange("(kc p) t -> p kc t", p=128)
                for kc in range(8):
                    P.dma("sync", I("dma_start", out=self.xT[:, kc, :], in_=xv[:, kc, :]), writes=self.B_x[kc], key="d_x")
                for L in self.layers:
                    if "f1" in self.parts:
                        self.ffn(L, 0)
                    if "mix" in self.parts:
                        if L % 2 == 0:
                            self.mixer_even(L)
                        else:
                            self.mixer_odd(L)
                    if "f2" in self.parts:
                        self.ffn(L, 1)
                yv = self.yT_d[sq_].rearrange("(kc p) t -> p kc t", p=128)
                for kc in range(8):
                    P.dma("sync", I("dma_start", out=yv[:, kc, :], in_=self.xT[:, kc, :]), reads=self.B_x[kc], writes=[B_out], key="d_out")
            P.final_wait("sync", [B_out])
            sems = {k: es.enter_context(nc.semaphore(k)) for k in P.cnt.keys()}
            with nc.Block() as block:
                @block.sync
                def _(e):
                    P.replay("sync", e, sems)

                @block.tensor
                def _(e):
                    P.replay("tensor", e, sems)

                @block.scalar
                def _(e):
                    P.replay("scalar", e, sems)

                @block.vector
                def _(e):
                    P.replay("vector", e, sems)

                @block.gpsimd
                def _(e):
                    P.replay("gpsimd", e, sems)
        return nc


def chunk_k(W, c0, w):
    return np.ascontiguousarray(W[:, c0:c0 + w].reshape(8, 128, w).transpose(1, 0, 2)).reshape(128, 8 * w)


def pack_weights(inp):
    f32 = np.float32
    g = lambda k: np.asarray(inp[k], dtype=f32)
    win = np.empty((8 * NJ, 128, 2048), f32)
    wout = np.empty((64, 128, NJ * 128), f32)
    for L in range(DEPTH):
        for wh, (ki, ko) in enumerate((("ffn1_w_in", "ffn1_w_out"), ("ffn2_w_in", "ffn2_w_out"))):
            Wi = g(ki)[L]
            Wo = g(ko)[L]
            fi = L * 2 + wh
            gg = Wi[:, :DFF].reshape(8, 128, NJ, 128)
            uu = Wi[:, DFF:].reshape(8, 128, NJ, 128)
            cat = np.concatenate([gg, uu], axis=3)
            win[fi * NJ:(fi + 1) * NJ] = cat.transpose(2, 1, 0, 3).reshape(NJ, 128, 2048)
            wout[fi * 8:(fi + 1) * 8] = Wo.reshape(NJ, 128, 8, 128).transpose(2, 1, 0, 3).reshape(8, 128, NJ * 128)
    abin = np.empty((40, 128, 1024), f32)
    about = np.empty((16, 128, 1024), f32)
    for e in range(2):
        W = g("ab_w_in")[e]
        for cc in range(20):
            abin[e * 20 + cc] = chunk_k(W, cc * 128, 128)
        Wo = g("ab_w_out")[e]
        for m in range(8):
            about[e * 8 + m] = chunk_k(Wo, m * 128, 128)
    fxqk = np.empty((96, 128, 512), f32)
    fxf = np.empty((2, 128, 128), f32)
    fxout = np.empty((16, 128, D), f32)
    for o in range(2):
        W = g("fox_w_in")[o]
        for wh in range(3):
            for h in range(16):
                fxqk[(o * 3 + wh) * 16 + h] = chunk_k(W, wh * 1024 + h * 64, 64)
        fxf[o] = chunk_k(W, 3072, 16)
        fxout[o * 8:(o + 1) * 8] = g("fox_w_out")[o].reshape(8, 128, D)
    cst = np.zeros((128, NCST), f32)
    col8 = lambda v: v.reshape(8, 128).T
    for L in range(DEPTH):
        cst[:, CO[("n1", L)]:CO[("n1", L)] + 8] = col8(g("ffn1_norm")[L])
        cst[:, CO[("nm", L)]:CO[("nm", L)] + 8] = col8(g("mix_norm")[L])
        cst[:, CO[("n2", L)]:CO[("n2", L)] + 8] = col8(g("ffn2_norm")[L])
    for e in range(2):
        cw = g("conv_w")[e]
        c0 = CO[("cw", e)]
        cst[:, c0:c0 + 4 * CW] = cw.reshape(CW, 4, 128).transpose(2, 1, 0).reshape(128, 4 * CW)
        for nm, key in (("cb", "conv_b"), ("lg", "conv_ln_g"), ("lb", "conv_ln_b")):
            cst[:, CO[(nm, e)]:CO[(nm, e)] + 4] = g(key)[e].reshape(4, 128).T
    for o in range(2):
        cst[:, CO[("gq", o)]] = np.tile(g("fox_q_norm")[o], 2)
        cst[:, CO[("gk", o)]] = np.tile(g("fox_k_norm")[o], 2)
        cst[0:16, CO[("fb", o)]] = g("fox_f_bias")[o]
    return dict(cst=cst, win=win, wout=wout, abin=abin, about=about, fxqk=fxqk, fxf=fxf, fxout=fxout)


def kernel(**inputs):
    x = np.asarray(inputs["x"], dtype=np.float32)
    wts = pack_weights(inputs)
    mk = MK(nseq=2)
    nc = mk.build()
    in_maps = []
    for c in range(NCORES):
        m = dict(wts)
        m["xT"] = np.ascontiguousarray(x[2 * c:2 * c + 2].transpose(0, 2, 1))
        in_maps.append(m)
    res = run_bass_kernel_spmd(nc, in_maps, core_ids=list(range(NCORES)))
    out = np.empty_like(x)
    for c in range(NCORES):
        out[2 * c:2 * c + 2] = np.asarray(res.results[c]["yT"]).transpose(0, 2, 1)
    return out
```

```python
import contextlib
import numpy as np
import concourse.bass as bass
import concourse.mybir as mybir
from concourse.bass_utils import run_bass_kernel_spmd

F32 = mybir.dt.float32
BF16 = mybir.dt.bfloat16
AF = mybir.ActivationFunctionType
ALU = mybir.AluOpType
ENGS = ["sync", "tensor", "scalar", "vector", "gpsimd"]

D = 1024
T = 2048
DFF = 2816
NJ = DFF // 128
DEPTH = 4
CW = 31
EPS = 1e-6
NCORES = 8


class Ev:
    __slots__ = ("key", "val")

    def __init__(self, key, val):
        self.key = key
        self.val = val


class Buf:
    __slots__ = ("w", "r")

    def __init__(self):
        self.w = None
        self.r = {}


def I(meth, **kw):
    return (meth, kw)


class Prog:
    def __init__(self):
        self.q = {e: [] for e in ENGS}
        self.cnt = {}
        self.seen = {e: {} for e in ENGS}
        self.nops = {e: 0 for e in ENGS}

    def _wait(self, eng, key, val):
        if self.seen[eng].get(key, 0) >= val:
            return
        self.seen[eng][key] = val
        self.q[eng].append(("wait", key, val))

    def _waits(self, eng, reads, writes):
        for b in reads:
            if b.w is not None:
                self._wait(eng, b.w.key, b.w.val)
        for b in writes:
            if b.w is not None:
                self._wait(eng, b.w.key, b.w.val)
            for ev in b.r.values():
                self._wait(eng, ev.key, ev.val)

    def _mark(self, ev, reads, writes):
        for b in reads:
            o = b.r.get(ev.key)
            if o is None or o.val < ev.val:
                b.r[ev.key] = ev
        for b in writes:
            b.w = ev
            b.r = {}

    def op(self, eng, ins, reads=(), writes=(), key=None, inc=1):
        key = key or eng
        self._waits(eng, reads, writes)
        self.cnt[key] = self.cnt.get(key, 0) + inc
        ev = Ev(key, self.cnt[key])
        self.q[eng].append(("op", ins, key, inc))
        self._mark(ev, reads, writes)
        self.nops[eng] += 1
        return ev

    def group(self, eng, inss, reads=(), writes=(), key=None):
        key = key or eng
        self._waits(eng, reads, writes)
        self.cnt[key] = self.cnt.get(key, 0) + 1
        ev = Ev(key, self.cnt[key])
        for f in inss[:-1]:
            self.q[eng].append(("op", f, None, 0))
        self.q[eng].append(("op", inss[-1], key, 1))
        self._mark(ev, reads, writes)
        self.nops[eng] += len(inss)
        return ev

    def dma(self, eng, ins, reads=(), writes=(), key=None):
        return self.op(eng, ins, reads, writes, key=key, inc=16)

    def barrier(self):
        for e in ENGS:
            for k, v in self.cnt.items():
                self._wait(e, k, v)

    def final_wait(self, eng, bufs):
        self._waits(eng, bufs, ())

    def replay(self, eng_name, eng, sems):
        pend = []
        for it in self.q[eng_name]:
            if it[0] == "wait":
                pend.append(it)
            else:
                for w in pend[:-1]:
                    eng.wait_ge(sems[w[1]], w[2])
                ins = getattr(eng, it[1][0])(**it[1][1])
                if pend:
                    ins._wait_ge(sems[pend[-1][1]], pend[-1][2])
                pend = []
                if it[2] is not None:
                    ins.then_inc(sems[it[2]], it[3])
        for w in pend:
            eng.wait_ge(sems[w[1]], w[2])


class Arena:
    def __init__(self, tile, nwords):
        self.tile = tile
        self.n = nwords
        self.off = 0

    def reset(self):
        self.off = 0

    def alloc(self, shape, dt):
        n = int(np.prod(shape))
        nw = n if dt == F32 else (n + 1) // 2
        nw = (nw + 7) // 8 * 8
        assert self.off + nw <= self.n, f"arena overflow {self.off}+{nw}>{self.n}"
        v = self.tile[:, self.off:self.off + nw]
        self.off += nw
        if dt != F32:
            v = v.bitcast(dt)
        v = v[:, 0:n]
        if len(shape) == 2:
            v = v.rearrange("p (a b) -> p a b", b=shape[1])
        elif len(shape) == 3:
            v = v.rearrange("p (a b c) -> p a b c", b=shape[1], c=shape[2])
        return v


def cst_layout():
    off = {}
    c = 0
    for L in range(DEPTH):
        for nm in ("n1", "nm", "n2"):
            off[(nm, L)] = c
            c += 8
    for e in range(2):
        off[("cw", e)] = c
        c += 4 * CW
        for nm in ("cb", "lg", "lb"):
            off[(nm, e)] = c
            c += 4
    for o in range(2):
        for nm in ("gq", "gk", "fb"):
            off[(nm, o)] = c
            c += 1
    return off, c


CO, NCST = cst_layout()


class MK:
    def __init__(self, nseq=2, layers=(0, 1, 2, 3), parts=("f1", "mix", "f2")):
        self.nseq = nseq
        self.layers = layers
        self.parts = parts
        self.nc = bass.Bass("TRN2", target_bir_lowering=False)
        self.P = Prog()
        self.slotn = {}

    def wslot(self, name, nslots):
        n = self.slotn.get(name, 0)
        self.slotn[name] = n + 1
        return n % nslots

    def load_w(self, name, tiles, bufs, src, maxlast=None):
        s = self.wslot(name, len(tiles))
        t = tiles[s]
        nd = len(t.shape)
        if nd == 3:
            o = t.rearrange("p a b -> p (a b)")
        else:
            o = t
        kw = dict(out=o, in_=src)
        if maxlast:
            kw["max_dma_last_dim"] = maxlast
        self.P.dma("gpsimd", I("dma_start", **kw), writes=[bufs[s]], key=f"d_{name}{s}")
        return t, bufs[s]

    def norm_block(self, blk, gcol, dst, dst_sl, B_dst):
        P = self.P
        s = self.wslot("nrm", 2)
        sq, B_sq = self.sq[s], self.B_sq[s]
        rstd, B_rstd = self.rstd[s], self.B_rstd[s]
        pn, B_pn = self.bank[6], self.B_bank[6]
        t0 = blk * 512
        for kc in range(8):
            P.op("scalar", I("activation", out=sq[:, kc, :], in_=self.xT[:, kc, t0:t0 + 512], func=AF.Square),
                 reads=[self.B_x[kc][blk]], writes=[B_sq])
        P.group("tensor", [I("matmul", out=pn[:, :], lhsT=self.ones_d[:, :], rhs=sq[:, kc, :], start=(kc == 0), stop=(kc == 7))
                           for kc in range(8)], reads=[B_sq, self.B_c], writes=[B_pn])
        P.op("scalar", I("activation", out=rstd[:, :], in_=pn[:, :], func=AF.Sqrt, bias=self.epsc[:, 0:1], scale=1.0),
             reads=[B_pn, self.B_c], writes=[B_rstd])
        P.op("vector", I("reciprocal", out=rstd[:, :], in_=rstd[:, :]), reads=[B_rstd], writes=[B_rstd])
        for kc in range(8):
            P.op("vector", I("scalar_tensor_tensor", out=dst[:, kc, dst_sl], in0=self.xT[:, kc, t0:t0 + 512],
                             scalar=self.cst[:, gcol + kc:gcol + kc + 1], in1=rstd[:, :], op0=ALU.mult, op1=ALU.mult),
                 reads=[self.B_x[kc][blk], B_rstd, self.B_c], writes=[B_dst[kc]])

    def ffn(self, L, which):
        P = self.P
        A = self.arena
        P.barrier()
        A.reset()
        gcol = CO[("n1" if which == 0 else "n2", L)]
        fidx = L * 2 + which
        hT = A.alloc([8, 1024], BF16)
        aT = A.alloc([NJ, 1024], BF16)
        self.sq = [A.alloc([8, 512], BF16) for _ in range(2)]
        self.rstd = [A.alloc([512], F32) for _ in range(2)]
        self.B_sq = [Buf(), Buf()]
        self.B_rstd = [Buf(), Buf()]
        sg = [A.alloc([512], F32) for _ in range(2)]
        wi = [A.alloc([8, 256], BF16) for _ in range(3)]
        wo = [A.alloc([NJ, 128], BF16) for _ in range(2)]
        B_sg = [Buf(), Buf()]
        B_wi = [Buf() for _ in range(3)]
        B_wo = [Buf() for _ in range(2)]
        bank, B_bank = self.bank, self.B_bank
        for tb in range(2):
            B_h = [[Buf() for _ in range(2)] for _ in range(8)]
            B_a = [[Buf() for _ in range(2)] for _ in range(NJ)]
            for sub in range(2):
                self.norm_block(tb * 2 + sub, gcol, hT, slice(sub * 512, (sub + 1) * 512), [B_h[kc][sub] for kc in range(8)])
            for j in range(NJ):
                w, B_w = self.load_w("wi", wi, B_wi, self.win_d[fidx * NJ + j, :, :])
                for sub in range(2):
                    b = (j * 2 + sub) % 2
                    tsl = slice(sub * 512, (sub + 1) * 512)
                    hreads = [B_w] + [B_h[kc][sub] for kc in range(8)]
                    P.group("tensor", [I("matmul", out=bank[b][:, :], lhsT=w[:, kc, 0:128], rhs=hT[:, kc, tsl], start=(kc == 0), stop=(kc == 7))
                                       for kc in range(8)], reads=hreads, writes=[B_bank[b]])
                    P.group("tensor", [I("matmul", out=bank[2 + b][:, :], lhsT=w[:, kc, 128:256], rhs=hT[:, kc, tsl], start=(kc == 0), stop=(kc == 7))
                                       for kc in range(8)], reads=hreads, writes=[B_bank[2 + b]])
                    P.op("scalar", I("activation", out=sg[b][:, :], in_=bank[b][:, :], func=AF.Silu),
                         reads=[B_bank[b]], writes=[B_sg[b]])
                    P.op("vector", I("tensor_tensor", out=aT[:, j, tsl], in0=sg[b][:, :], in1=bank[2 + b][:, :], op=ALU.mult),
                         reads=[B_sg[b], B_bank[2 + b]], writes=[B_a[j][sub]])
            for m in range(8):
                w, B_w = self.load_w("wo", wo, B_wo, self.wout_d[fidx * 8 + m, :, :], maxlast=4096)
                for sub in range(2):
                    blk = tb * 2 + sub
                    b = 4 + (m * 2 + sub) % 2
                    tsl = slice(sub * 512, (sub + 1) * 512)
                    P.group("tensor", [I("matmul", out=bank[b][:, :], lhsT=w[:, kc, :], rhs=aT[:, kc, tsl], start=(kc == 0), stop=(kc == NJ - 1))
                                       for kc in range(NJ)], reads=[B_w] + [B_a[kc][sub] for kc in range(NJ)], writes=[B_bank[b]])
                    xs = self.xT[:, m, blk * 512:(blk + 1) * 512]
                    P.op("vector", I("scalar_tensor_tensor", out=xs, in0=bank[b][:, :], scalar=0.5, in1=xs, op0=ALU.mult, op1=ALU.add),
                         reads=[B_bank[b]], writes=[self.B_x[m][blk]])

    def mixer_even(self, L):
        P = self.P
        A = self.arena
        e = L // 2
        P.barrier()
        A.reset()
        bank, B_bank = self.bank, self.B_bank
        cst = self.cst
        TP = T + CW - 1
        wc = [A.alloc([8, 128], BF16) for _ in range(4)]
        B_wc = [Buf() for _ in range(4)]
        uraw = A.alloc([4 * TP], F32)
        upad = uraw.rearrange("p (c t) -> p c t", c=4)
        ubf = uraw.bitcast(BF16).rearrange("p (c t) -> p c t", c=4)

        def ufin(c, blk):
            o = 2 * (CW - 1 + blk * 512)
            return ubf[:, c, o:o + 512]

        B_up = [[Buf() for _ in range(4)] for _ in range(4)]
        B_pad = [Buf() for _ in range(4)]
        acc = [A.alloc([512], F32) for _ in range(2)]
        B_acc = [Buf(), Buf()]
        hT = A.alloc([8, T], BF16)
        B_h = [[Buf() for _ in range(4)] for _ in range(8)]
        mark = A.off
        self.sq = [A.alloc([8, 512], BF16) for _ in range(2)]
        self.rstd = [A.alloc([512], F32) for _ in range(2)]
        self.B_sq = [Buf(), Buf()]
        self.B_rstd = [Buf(), Buf()]
        tmp = [A.alloc([512], F32) for _ in range(2)]
        B_tmp = [Buf(), Buf()]
        for blk in range(4):
            self.norm_block(blk, CO[("nm", L)], hT, slice(blk * 512, (blk + 1) * 512), [B_h[kc][blk] for kc in range(8)])
        hall = lambda blk: [B_h[kc][blk] for kc in range(8)]

        for c in range(4):
            P.op("gpsimd", I("memset", ap=upad[:, c, 0:CW - 1], constant=0.0), writes=[B_pad[c]])
        for c in range(4):
            wv, B_wv = self.load_w("wc", wc, B_wc, self.abin_d[e * 20 + c, :, :])
            wg, B_wg = self.load_w("wc", wc, B_wc, self.abin_d[e * 20 + 4 + c, :, :])
            for blk in range(4):
                b = blk % 2
                tsl = slice(blk * 512, (blk + 1) * 512)
                P.group("tensor", [I("matmul", out=bank[b][:, :], lhsT=wv[:, kc, :], rhs=hT[:, kc, tsl], start=(kc == 0), stop=(kc == 7))
                                   for kc in range(8)], reads=[B_wv] + hall(blk), writes=[B_bank[b]])
                P.group("tensor", [I("matmul", out=bank[2 + b][:, :], lhsT=wg[:, kc, :], rhs=hT[:, kc, tsl], start=(kc == 0), stop=(kc == 7))
                                   for kc in range(8)], reads=[B_wg] + hall(blk), writes=[B_bank[2 + b]])
                P.op("scalar", I("activation", out=tmp[b][:, :], in_=bank[2 + b][:, :], func=AF.Sigmoid),
                     reads=[B_bank[2 + b]], writes=[B_tmp[b]])
                P.op("vector", I("tensor_tensor", out=upad[:, c, CW - 1 + blk * 512:CW - 1 + (blk + 1) * 512], in0=tmp[b][:, :], in1=bank[b][:, :], op=ALU.mult),
                     reads=[B_tmp[b], B_bank[b]], writes=[B_up[c][blk]])

        cw0 = CO[("cw", e)]
        cb0 = CO[("cb", e)]

        def conv_items():
            i = 0
            for blk in (3, 2, 1, 0):
                base = blk * 512
                for c in range(4):
                    a, B_a = acc[i % 2], B_acc[i % 2]
                    i += 1
                    rd = [B_up[c][blk], B_up[c][blk - 1] if blk > 0 else B_pad[c], self.B_c]
                    wcol = lambda k: cst[:, cw0 + c * CW + k:cw0 + c * CW + k + 1]
                    P.op("vector", I("tensor_scalar", out=a[:, :], in0=upad[:, c, base:base + 512], scalar1=wcol(0),
                                     scalar2=cst[:, cb0 + c:cb0 + c + 1], op0=ALU.mult, op1=ALU.add), reads=rd, writes=[B_a])
                    yield
                    for k in range(1, CW - 1):
                        P.op("vector", I("scalar_tensor_tensor", out=a[:, :], in0=upad[:, c, base + k:base + k + 512], scalar=wcol(k),
                                         in1=a[:, :], op0=ALU.mult, op1=ALU.add), reads=rd, writes=[B_a])
                        yield
                    k = CW - 1
                    home = upad[:, c, base + k:base + k + 512]
                    P.op("vector", I("scalar_tensor_tensor", out=home, in0=home, scalar=wcol(k), in1=a[:, :], op0=ALU.mult, op1=ALU.add),
                         reads=[B_a, self.B_c], writes=[B_up[c][blk]])
                    yield

        conv = conv_items()

        P.barrier()
        A.off = mark
        vtp = [A.alloc([16, 128], BF16) for _ in range(1)]
        B_vtp = [[Buf() for _ in range(4)] for _ in range(1)]
        oT = A.alloc([4, T], BF16)
        B_o = [[Buf() for _ in range(4)] for _ in range(4)]
        qT = [A.alloc([T], BF16) for _ in range(1)]
        kT = [A.alloc([T], BF16) for _ in range(1)]
        B_q = [[Buf() for _ in range(4)] for _ in range(1)]
        B_k = [[Buf() for _ in range(4)] for _ in range(1)]
        mark2 = A.off
        NS = 6
        et = [A.alloc([512], F32) for _ in range(NS)]
        spb = [A.alloc([512], BF16) for _ in range(NS)]
        ssf = A.alloc([512], F32)
        ssb = [A.alloc([512], BF16) for _ in range(2)]
        At = [A.alloc([512], BF16) for _ in range(3)]
        B_et = [Buf() for _ in range(NS)]
        B_spb = [Buf() for _ in range(NS)]
        B_ssf = Buf()
        B_ssb = [Buf(), Buf()]
        B_At = [Buf(), Buf(), Buf()]
        st = dict(ss_cur=0, acnt=0)
        s2q = []
        s3q = []

        def advance(depth=2):
            old3 = s3q.pop(0) if s3q else None
            if len(s2q) >= depth:
                s2q.pop(0)()
            if old3:
                old3()

        def flush_all():
            while s2q or s3q:
                advance(1)

        cnt = 0
        ocnt = 0
        for pr in range(4):
            ps_ = 0
            flush_all()
            wq, B_wq = self.load_w("wc", wc, B_wc, self.abin_d[e * 20 + 8 + pr, :, :])
            wk, B_wk = self.load_w("wc", wc, B_wc, self.abin_d[e * 20 + 12 + pr, :, :])
            wv_, B_wv_ = self.load_w("wc", wc, B_wc, self.abin_d[e * 20 + 16 + pr, :, :])
            for g_ in range(4):
                vb = 6 + g_ % 2
                P.group("tensor", [I("matmul", out=bank[vb][:, i * 128:(i + 1) * 128], lhsT=hT[:, kc, (g_ * 4 + i) * 128:(g_ * 4 + i + 1) * 128], rhs=wv_[:, kc, :],
                                     start=(kc == 0), stop=(kc == 7)) for i in range(4) for kc in range(8)],
                        reads=[B_wv_] + hall(g_), writes=[B_bank[vb]])
                P.op("scalar", I("activation", out=vtp[ps_][:, g_ * 4:(g_ + 1) * 4, :], in_=bank[vb][:, :].rearrange("p (a b) -> p a b", b=128), func=AF.Copy),
                     reads=[B_bank[vb]], writes=[B_vtp[ps_][g_]])
            for blk in range(4):
                tsl = slice(blk * 512, (blk + 1) * 512)
                P.group("tensor", [I("matmul", out=bank[6][:, :], lhsT=wq[:, kc, :], rhs=hT[:, kc, tsl], start=(kc == 0), stop=(kc == 7))
                                   for kc in range(8)], reads=[B_wq] + hall(blk), writes=[B_bank[6]])
                P.op("scalar", I("activation", out=qT[ps_][:, tsl], in_=bank[6][:, :], func=AF.Copy, scale=0.125),
                     reads=[B_bank[6]], writes=[B_q[ps_][blk]])
                P.group("tensor", [I("matmul", out=bank[7][:, :], lhsT=wk[:, kc, :], rhs=hT[:, kc, tsl], start=(kc == 0), stop=(kc == 7))
                                   for kc in range(8)], reads=[B_wk] + hall(blk), writes=[B_bank[7]])
                P.op("scalar", I("activation", out=kT[ps_][:, tsl], in_=bank[7][:, :], func=AF.Copy), reads=[B_bank[7]], writes=[B_k[ps_][blk]])
            for half in range(2):
                psl = slice(half * 64, half * 64 + 64)
                for qb in range(4):
                    qs = slice(qb * 512, (qb + 1) * 512)
                    n = 4 * qb + 4
                    ob = 4 + (ocnt % 2)
                    ocnt += 1
                    for It in reversed(range(n)):
                        r = It - 4 * qb
                        b = cnt % 2
                        e4 = cnt % NS
                        cnt += 1
                        first = (It == n - 1)
                        ks = slice(It * 128, (It + 1) * 128)
                        qk = dict(lhsT=kT[ps_][psl, ks], rhs=qT[ps_][psl, qs])
                        qkr = [B_k[ps_][It // 4], B_q[ps_][qb]]
                        P.op("tensor", I("matmul", out=bank[b][:, :], start=True, stop=True, **qk), reads=qkr, writes=[B_bank[b]])
                        P.op("scalar", I("activation", out=et[e4][:, :], in_=bank[b][:, :], func=AF.Exp), reads=[B_bank[b]], writes=[B_et[e4]])
                        P.op("scalar", I("activation", out=spb[e4][:, :], in_=et[e4][:, :], func=AF.Ln, bias=self.onec[:, 0:1], scale=1.0),
                             reads=[B_et[e4], self.B_c], writes=[B_spb[e4]])
                        if r >= 0:
                            P.op("vector", I("tensor_tensor", out=spb[e4][:, :], in0=spb[e4][:, :], in1=self.msb[:, r, :], op=ALU.mult),
                                 reads=[self.B_c], writes=[B_spb[e4]])

                        def s2(It=It, r=r, b=b, e4=e4, first=first, qk=qk, qkr=qkr, ob=ob, n=n, ps_=ps_, psl=psl, pr=pr, qs=qs, qb=qb):
                            cur = st["ss_cur"]
                            a3 = st["acnt"] % 3
                            st["acnt"] += 1
                            lb = 2 + b % 2
                            mm = [I("matmul", out=bank[lb][:, :], start=True, stop=False, **qk),
                                  I("matmul", out=bank[lb][:, :], lhsT=self.negtri[:, :], rhs=spb[e4][:, :], start=False, stop=first)]
                            rd = qkr + [B_spb[e4], self.B_c]
                            if not first:
                                mm.append(I("matmul", out=bank[lb][:, :], lhsT=self.negones[:, :], rhs=ssb[cur][:, :], start=False, stop=True))
                                rd.append(B_ssb[cur])
                            P.group("tensor", mm, reads=rd, writes=[B_bank[lb]])
                            P.op("scalar", I("activation", out=At[a3][:, :], in_=bank[lb][:, :], func=AF.Exp), reads=[B_bank[lb]], writes=[B_At[a3]])
                            if r >= 0:
                                P.op("vector", I("tensor_tensor", out=At[a3][:, :], in0=At[a3][:, :], in1=self.msb[:, r, :], op=ALU.mult),
                                     reads=[self.B_c], writes=[B_At[a3]])
                            if It > 0:
                                sn = 1 - cur
                                if first:
                                    P.op("vector", I("tensor_copy", out=ssb[sn][:, :], in_=spb[e4][:, :]), reads=[B_spb[e4]], writes=[B_ssb[sn]])
                                    P.op("vector", I("tensor_copy", out=ssf[:, :], in_=spb[e4][:, :]), reads=[B_spb[e4]], writes=[B_ssf])
                                else:
                                    P.op("vector", I("tensor_tensor", out=ssb[sn][:, :], in0=ssf[:, :], in1=spb[e4][:, :], op=ALU.add),
                                         reads=[B_ssf, B_spb[e4]], writes=[B_ssb[sn]])
                                    P.op("vector", I("tensor_tensor", out=ssf[:, :], in0=ssf[:, :], in1=spb[e4][:, :], op=ALU.add),
                                         reads=[B_spb[e4]], writes=[B_ssf])
                                st["ss_cur"] = sn

                            def s3():
                                P.op("tensor", I("matmul", out=bank[ob][:, :], lhsT=vtp[ps_][:, It, :], rhs=At[a3][:, :], start=first, stop=(It == 0)),
                                     reads=[B_vtp[ps_][It // 4], B_At[a3]], writes=[B_bank[ob]])
                                if It == 0:
                                    P.op("scalar", I("activation", out=oT[psl, pr, qs], in_=bank[ob][psl, :], func=AF.Copy), reads=[B_bank[ob]], writes=[B_o[pr][qb]])
                            s3q.append(s3)

                        s2q.append(s2)
                        advance(3)
                        next(conv, None)
                        next(conv, None)
        flush_all()
        for _ in conv:
            pass

        P.barrier()
        A.off = mark2
        ysq = A.alloc([4, 512], F32)
        B_ysq = Buf()
        m2 = A.alloc([512], F32)
        var = A.alloc([512], F32)
        B_m2 = Buf()
        B_var = Buf()
        t1 = [A.alloc([512], F32) for _ in range(2)]
        B_t1 = [Buf(), Buf()]
        lg0, lb0 = CO[("lg", e)], CO[("lb", e)]
        for blk in range(4):
            ysl = slice(CW - 1 + blk * 512, CW - 1 + (blk + 1) * 512)
            B_yb = [B_up[c][blk] for c in range(4)]
            for c in range(4):
                P.op("scalar", I("activation", out=ysq[:, c, :], in_=upad[:, c, ysl], func=AF.Square), reads=[B_yb[c]], writes=[B_ysq])
            P.group("tensor", [I("matmul", out=bank[0][:, :], lhsT=self.ones_ln[:, :], rhs=upad[:, c, ysl], start=(c == 0), stop=(c == 3))
                               for c in range(4)], reads=B_yb + [self.B_c], writes=[B_bank[0]])
            P.group("tensor", [I("matmul", out=bank[1][:, :], lhsT=self.ones_ln[:, :], rhs=ysq[:, c, :], start=(c == 0), stop=(c == 3))
                               for c in range(4)], reads=[B_ysq, self.B_c], writes=[B_bank[1]])
            P.op("scalar", I("activation", out=m2[:, :], in_=bank[0][:, :], func=AF.Square), reads=[B_bank[0]], writes=[B_m2])
            P.op("vector", I("scalar_tensor_tensor", out=var[:, :], in0=m2[:, :], scalar=-1.0, in1=bank[1][:, :], op0=ALU.mult, op1=ALU.add),
                 reads=[B_m2, B_bank[1]], writes=[B_var])
            P.op("vector", I("tensor_scalar", out=var[:, :], in0=var[:, :], scalar1=0.0, scalar2=None, op0=ALU.max), reads=[B_var], writes=[B_var])
            P.op("scalar", I("activation", out=var[:, :], in_=var[:, :], func=AF.Sqrt, bias=self.epsc[:, 0:1], scale=1.0),
                 reads=[B_var, self.B_c], writes=[B_var])
            P.op("vector", I("reciprocal", out=var[:, :], in_=var[:, :]), reads=[B_var], writes=[B_var])
            for c in range(4):
                b = c % 2
                P.op("vector", I("tensor_tensor", out=t1[b][:, :], in0=upad[:, c, ysl], in1=bank[0][:, :], op=ALU.subtract),
                     reads=[B_yb[c], B_bank[0]], writes=[B_t1[b]])
                P.op("vector", I("tensor_tensor", out=t1[b][:, :], in0=t1[b][:, :], in1=var[:, :], op=ALU.mult),
                     reads=[B_var], writes=[B_t1[b]])
                P.op("scalar", I("activation", out=ufin(c, blk), in_=t1[b][:, :], func=AF.Silu,
                                 scale=cst[:, lg0 + c:lg0 + c + 1], bias=cst[:, lb0 + c:lb0 + c + 1]),
                     reads=[B_t1[b], self.B_c], writes=[B_yb[c]])

        for m in range(8):
            w, B_w = self.load_w("wc", wc, B_wc, self.about_d[e * 8 + m, :, :])
            for blk in range(4):
                b = 6 + blk % 2
                tsl = slice(blk * 512, (blk + 1) * 512)
                mm = [I("matmul", out=bank[b][:, :], lhsT=w[:, kc, :], rhs=(ufin(kc, blk) if kc < 4 else oT[:, kc - 4, tsl]),
                        start=(kc == 0), stop=(kc == 7)) for kc in range(8)]
                P.group("tensor", mm, reads=[B_w] + [B_up[c][blk] for c in range(4)] + [B_o[p_][blk] for p_ in range(4)], writes=[B_bank[b]])
                xs = self.xT[:, m, tsl]
                P.op("vector", I("tensor_tensor", out=xs, in0=bank[b][:, :], in1=xs, op=ALU.add), reads=[B_bank[b]], writes=[self.B_x[m][blk]])

    def mixer_odd(self, L):
        P = self.P
        A = self.arena
        o = L // 2
        P.barrier()
        A.reset()
        bank, B_bank = self.bank, self.B_bank
        cst = self.cst
        hT = A.alloc([8, T], BF16)
        B_h = [[Buf() for _ in range(4)] for _ in range(8)]
        mark = A.off
        self.sq = [A.alloc([8, 512], BF16) for _ in range(2)]
        self.rstd = [A.alloc([512], F32) for _ in range(2)]
        self.B_sq = [Buf(), Buf()]
        self.B_rstd = [Buf(), Buf()]
        for blk in range(4):
            self.norm_block(blk, CO[("nm", L)], hT, slice(blk * 512, (blk + 1) * 512), [B_h[kc][blk] for kc in range(8)])
        hall = lambda blk: [B_h[kc][blk] for kc in range(8)]
        P.barrier()
        A.off = mark
        wf = A.alloc([8, 16], BF16)
        B_wf = Buf()
        lf = A.alloc([T], F32)
        cpos = A.alloc([T], F32)
        cp3 = A.alloc([3, T], BF16)
        r1 = lf
        cposT = A.alloc([16, 16], F32)
        nfb = A.alloc([1], F32)
        tmpf = A.alloc([512], F32)
        B_lf, B_cpos, B_cp3, B_cposT, B_nfb, B_tmpf = (Buf() for _ in range(6))
        B_r1 = B_lf
        P.dma("gpsimd", I("dma_start", out=wf.rearrange("p a b -> p (a b)"), in_=self.fxf_d[o, :, :]), writes=[B_wf], key="d_wf")
        P.op("vector", I("tensor_scalar", out=nfb[0:16, :], in0=cst[0:16, CO[("fb", o)]:CO[("fb", o)] + 1], scalar1=-1.0, scalar2=None, op0=ALU.mult),
             reads=[self.B_c], writes=[B_nfb])
        for blk in range(4):
            tsl = slice(blk * 512, (blk + 1) * 512)
            P.group("tensor", [I("matmul", out=bank[6][0:16, :], lhsT=wf[:, kc, :], rhs=hT[:, kc, tsl], start=(kc == 0), stop=(kc == 7))
                               for kc in range(8)], reads=[B_wf] + hall(blk), writes=[B_bank[6]])
            P.op("scalar", I("activation", out=tmpf[0:16, :], in_=bank[6][0:16, :], func=AF.Exp, scale=-1.0, bias=nfb[0:16, 0:1]),
                 reads=[B_bank[6], B_nfb], writes=[B_tmpf])
            P.op("scalar", I("activation", out=lf[0:16, tsl], in_=tmpf[0:16, :], func=AF.Ln, bias=self.onec[0:16, 0:1], scale=1.0),
                 reads=[B_tmpf, self.B_c], writes=[B_lf])
        P.op("vector", I("tensor_tensor_scan", out=cpos[0:16, :], data0=lf[0:16, :], data1=lf[0:16, :], initial=0.0, op0=ALU.add, op1=ALU.max),
             reads=[B_lf], writes=[B_cpos])
        P.op("vector", I("tensor_scalar", out=r1[0:16, :], in0=cpos[0:16, :], scalar1=-1.0, scalar2=None, op0=ALU.mult), reads=[B_cpos], writes=[B_r1])
        P.op("vector", I("tensor_copy", out=cp3[0:16, 0, :], in_=r1[0:16, :]), reads=[B_r1], writes=[B_cp3])
        P.op("vector", I("tensor_tensor", out=r1[0:16, :], in0=r1[0:16, :], in1=cp3[0:16, 0, :], op=ALU.subtract), reads=[B_cp3], writes=[B_r1])
        P.op("vector", I("tensor_copy", out=cp3[0:16, 1, :], in_=r1[0:16, :]), reads=[B_r1], writes=[B_cp3])
        P.op("vector", I("tensor_tensor", out=r1[0:16, :], in0=r1[0:16, :], in1=cp3[0:16, 1, :], op=ALU.subtract), reads=[B_cp3], writes=[B_r1])
        P.op("vector", I("tensor_copy", out=cp3[0:16, 2, :], in_=r1[0:16, :]), reads=[B_r1], writes=[B_cp3])
        for It in range(16):
            b = 6 + It % 2
            P.op("tensor", I("transpose", out=bank[b][:, 0:16], in_=cpos[0:16, It * 128:(It + 1) * 128], identity=self.ident[0:16, 0:16]),
                 reads=[B_cpos, self.B_c], writes=[B_bank[b]])
            P.op("vector", I("tensor_copy", out=cposT[:, It, :], in_=bank[b][:, 0:16]), reads=[B_bank[b]], writes=[B_cposT])
        wq4 = [A.alloc([8, 64], BF16) for _ in range(6)]
        B_wq4 = [Buf() for _ in range(6)]
        vtp = [A.alloc([16, 128], BF16) for _ in range(2)]
        B_vtp = [[Buf() for _ in range(4)] for _ in range(2)]
        wout = [A.alloc([D], BF16) for _ in range(2)]
        B_wout = [Buf(), Buf()]
        qa = [A.alloc([T], BF16) for _ in range(2)]
        ka = [A.alloc([T], BF16) for _ in range(2)]
        B_qa = [[Buf() for _ in range(5)] for _ in range(2)]
        B_ka = [[Buf() for _ in range(5)] for _ in range(2)]
        oTp = [A.alloc([T], BF16) for _ in range(2)]
        B_oTp = [[Buf() for _ in range(4)] for _ in range(2)]
        sqh = [A.alloc([512], BF16) for _ in range(2)]
        rsh = [A.alloc([512], F32) for _ in range(2)]
        B_sqh = [Buf(), Buf()]
        B_rsh = [Buf(), Buf()]
        At = [A.alloc([512], BF16) for _ in range(4)]
        B_At = [Buf() for _ in range(4)]
        rden = [A.alloc([512], F32) for _ in range(2)]
        B_rden = [Buf(), Buf()]
        for s in range(2):
            P.op("gpsimd", I("memset", ap=ka[s][64:128, :], constant=0.0), writes=[B_ka[s][4]])
            P.op("gpsimd", I("memset", ap=qa[s][64:128, :], constant=0.0), writes=[B_qa[s][4]])
            P.op("gpsimd", I("memset", ap=ka[s][64:67, :], constant=1.0), writes=[B_ka[s][4]])
        P.op("gpsimd", I("memset", ap=vtp[0][:, :, 64:128], constant=1.0), writes=B_vtp[0])
        P.op("gpsimd", I("memset", ap=vtp[1][:, :, 0:64], constant=1.0), writes=B_vtp[1])
        self._pcnt = 0

        wld = {}

        def preload(h):
            wld[h] = [self.load_w("wq4", wq4, B_wq4, self.fxqk_d[(o * 3 + j) * 16 + h, :, :]) for j in range(3)]

        def proj(h):
            s = h % 2
            (wq, B_wq), (wk, B_wk) = wld[h][0], wld[h][1]
            for j in range(3):
                P.dma("sync", I("dma_start", out=qa[s][64 + j:65 + j, :], in_=cp3[h:h + 1, j, :]), reads=[B_cp3], writes=[B_qa[s][4]], key=f"d_aug{s}")
            gq, gk = CO[("gq", o)], CO[("gk", o)]
            tails = []
            for blk in range(4):
                tsl = slice(blk * 512, (blk + 1) * 512)
                u = self._pcnt % 2
                pb = 5 + self._pcnt % 2
                self._pcnt += 1
                P.group("tensor", [I("matmul", out=bank[pb][0:64, :], lhsT=wq[:, kc, :], rhs=hT[:, kc, tsl], start=(kc == 0), stop=(kc == 7)) for kc in range(8)]
                        + [I("matmul", out=bank[pb][64:128, :], lhsT=wk[:, kc, :], rhs=hT[:, kc, tsl], start=(kc == 0), stop=(kc == 7)) for kc in range(8)],
                        reads=[B_wq, B_wk] + hall(blk), writes=[B_bank[pb]])
                P.op("scalar", I("activation", out=sqh[u][:, :], in_=bank[pb][:, :], func=AF.Square), reads=[B_bank[pb]], writes=[B_sqh[u]])

                def tail(u=u, pb=pb, tsl=tsl, blk=blk):
                    P.op("tensor", I("matmul", out=bank[7][:, :], lhsT=self.ones64bd[:, :], rhs=sqh[u][:, :], start=True, stop=True),
                         reads=[B_sqh[u], self.B_c], writes=[B_bank[7]])
                    P.op("scalar", I("activation", out=rsh[u][:, :], in_=bank[7][:, :], func=AF.Ln, bias=self.qkbias[:, 0:1], scale=self.qkscale[:, 0:1]),
                         reads=[B_bank[7], self.B_c], writes=[B_rsh[u]])
                    P.op("scalar", I("activation", out=rsh[u][:, :], in_=rsh[u][:, :], func=AF.Exp, scale=-0.5), reads=[B_rsh[u]], writes=[B_rsh[u]])
                    P.op("vector", I("scalar_tensor_tensor", out=qa[s][0:64, tsl], in0=bank[pb][0:64, :], scalar=cst[0:64, gq:gq + 1], in1=rsh[u][0:64, :],
                                     op0=ALU.mult, op1=ALU.mult), reads=[B_bank[pb], B_rsh[u], self.B_c], writes=[B_qa[s][blk]])
                    P.op("vector", I("scalar_tensor_tensor", out=ka[s][0:64, tsl], in0=bank[pb][64:128, :], scalar=cst[64:128, gk:gk + 1], in1=rsh[u][64:128, :],
                                     op0=ALU.mult, op1=ALU.mult), reads=[B_bank[pb], B_rsh[u], self.B_c], writes=[B_ka[s][blk]])

                if tails:
                    tails.pop(0)()
                tails.append(tail)
                yield
            while tails:
                tails.pop(0)()
            yield

        def vproj(h):
            s = h % 2
            vc = slice(s * 64, s * 64 + 64)
            w, B_w = wld[h][2]
            for g in range(4):
                P.group("tensor", [I("matmul", out=bank[7][:, i * 64:(i + 1) * 64], lhsT=hT[:, kc, (g * 4 + i) * 128:(g * 4 + i + 1) * 128], rhs=w[:, kc, :],
                                     start=(kc == 0), stop=(kc == 7)) for i in range(4) for kc in range(8)],
                        reads=[B_w] + hall(g), writes=[B_bank[7]])
                P.op("vector", I("tensor_copy", out=vtp[s][:, g * 4:(g + 1) * 4, vc], in_=bank[7][:, 0:256].rearrange("p (a b) -> p a b", b=64)),
                     reads=[B_bank[7]], writes=[B_vtp[s][g]])
                yield

        pend = []
        outq = []

        def flush():
            while pend:
                pend.pop(0)()

        cnt = 0
        ocnt = 0
        preload(0)
        preload(1)
        for _ in proj(0):
            pass
        for _ in vproj(0):
            pass
        for pr in range(8):
            vs = pr % 2
            wo_, B_wo_ = self.load_w("wfo", wout, B_wout, self.fxout_d[o * 8 + pr, :, :])
            for half in range(2):
                h = 2 * pr + half
                s = h % 2
                flush()
                if h + 2 < 16:
                    preload(h + 2)

                def both(hn=h + 1, oq=list(outq)):
                    for g_ in oq:
                        yield from g_
                    if hn < 16:
                        yield from proj(hn)
                        yield from vproj(hn)
                outq.clear()
                work = both()
                tix = 0
                psl = slice(half * 64, half * 64 + 64)
                osl = slice((1 - half) * 64, (1 - half) * 64 + 64)
                for qb in range(4):
                    n = 4 * qb + 4
                    ob = 3 + (ocnt % 2)
                    ocnt += 1
                    order = list(range(n)) if qb == 0 else [0] + list(range(4 * qb, n)) + list(range(1, 4 * qb))
                    for oi, It in enumerate(order):
                        r = It - 4 * qb
                        c0 = max(r, 0) * 128
                        b = cnt % 3
                        is_first = (oi == 0)
                        is_last = (oi == n - 1)
                        a3 = cnt % 4
                        cnt += 1
                        ks = slice(It * 128, (It + 1) * 128)
                        qs = slice(qb * 512 + c0, (qb + 1) * 512)
                        cs = slice(c0, 512)
                        P.op("tensor", I("matmul", out=bank[b][:, cs], lhsT=ka[s][:, ks], rhs=qa[s][:, qs], start=True, stop=True),
                             reads=[B_ka[s][It // 4], B_ka[s][4], B_qa[s][qb], B_qa[s][4]], writes=[B_bank[b]])
                        P.op("scalar", I("activation", out=At[a3][:, cs], in_=bank[b][:, cs], func=AF.Exp, bias=cposT[:, It, h:h + 1], scale=1.0),
                             reads=[B_bank[b], B_cposT], writes=[B_At[a3]])
                        if r >= 0:
                            P.op("gpsimd", I("affine_select", out=At[a3][:, cs], in_=At[a3][:, cs], pattern=[[1, 512 - c0]], compare_op=ALU.is_ge,
                                             fill=0.0, base=0, channel_multiplier=-1), reads=[], writes=[B_At[a3]])

                        def pv(It=It, a3=a3, cs=cs, ob=ob, n=n, qb=qb, s=s, psl=psl, osl=osl, vs=vs, is_first=is_first, is_last=is_last):
                            P.op("tensor", I("matmul", out=bank[ob][:, cs], lhsT=vtp[s][:, It, :], rhs=At[a3][:, cs], start=is_first, stop=is_last),
                                 reads=[B_vtp[s][It // 4], B_At[a3]], writes=[B_bank[ob]])
                            if is_last:
                                u = qb % 2
                                P.op("vector", I("reciprocal", out=rden[u][psl, :], in_=bank[ob][osl, :]), reads=[B_bank[ob]], writes=[B_rden[u]])
                                P.op("vector", I("tensor_tensor", out=oTp[vs][psl, qb * 512:(qb + 1) * 512], in0=bank[ob][psl, :], in1=rden[u][psl, :], op=ALU.mult),
                                     reads=[B_bank[ob], B_rden[u]], writes=[B_oTp[vs][qb]])

                        if len(pend) >= 2:
                            pend.pop(0)()
                        pend.append(pv)
                        tix += 1
                        if tix % 4 == 0:
                            next(work, None)
                for _ in work:
                    pass
            flush()

            def outproj(vs=vs, wo_=wo_, B_wo_=B_wo_):
                for m in range(8):
                    for blk in range(4):
                        b = 6 + (m * 4 + blk) % 2
                        tsl = slice(blk * 512, (blk + 1) * 512)
                        P.op("tensor", I("matmul", out=bank[b][:, :], lhsT=wo_[:, m * 128:(m + 1) * 128], rhs=oTp[vs][:, tsl], start=True, stop=True),
                             reads=[B_wo_, B_oTp[vs][blk]], writes=[B_bank[b]])
                        xs = self.xT[:, m, tsl]
                        P.op("vector", I("tensor_tensor", out=xs, in0=bank[b][:, :], in1=xs, op=ALU.add), reads=[B_bank[b]], writes=[self.B_x[m][blk]])
                    yield

            for _ in outproj():
                pass
        for g_ in outq:
            for _ in g_:
                pass

    def build(self):
        nc, P = self.nc, self.P
        ns = self.nseq
        dt_in = lambda name, shape: nc.dram_tensor(name, shape, F32, kind="ExternalInput").ap()
        self.xT_d = dt_in("xT", [ns, D, T])
        self.cst_d = dt_in("cst", [128, NCST])
        self.win_d = dt_in("win", [8 * NJ, 128, 8 * 256])
        self.wout_d = dt_in("wout", [8 * 8, 128, NJ * 128])
        self.abin_d = dt_in("abin", [2 * 20, 128, 8 * 128])
        self.about_d = dt_in("about", [2 * 8, 128, 8 * 128])
        self.fxqk_d = dt_in("fxqk", [2 * 3 * 16, 128, 8 * 64])
        self.fxf_d = dt_in("fxf", [2, 128, 8 * 16])
        self.fxout_d = dt_in("fxout", [2 * 8, 128, D])
        self.yT_d = nc.dram_tensor("yT", [ns, D, T], F32, kind="ExternalOutput").ap()
        with contextlib.ExitStack() as es:
            sb = lambda name, shape, dt: es.enter_context(nc.sbuf_tensor(name, shape, dt))
            self.xT = sb("xTs", [128, 8, T], F32)
            self.cst = sb("csts", [128, NCST], F32)
            self.ones_d = sb("ones_d", [128, 128], BF16)
            self.ones_ln = sb("ones_ln", [128, 128], F32)
            self.ones64bd = sb("ones64bd", [128, 128], BF16)
            self.qkscale = sb("qkscale", [128, 1], F32)
            self.qkbias = sb("qkbias", [128, 1], F32)
            self.ones_den = sb("ones_den", [128, 128], BF16)
            self.negones = sb("negones", [128, 128], BF16)
            self.negtri = sb("negtri", [128, 128], BF16)
            self.ident = sb("ident", [128, 128], F32)
            self.onesf = sb("onesf", [128, 128], F32)
            self.epsc = sb("epsc", [128, 1], F32)
            self.eps64c = sb("eps64c", [128, 1], F32)
            self.onec = sb("onec", [128, 1], F32)
            self.msb = sb("msb", [128, 4, 512], BF16)
            self.onesb = sb("onesb", [128, 512], BF16)
            SCRW = 33400
            scr = sb("scr", [128, SCRW], F32)
            self.arena = Arena(scr, SCRW)
            self.bank = [es.enter_context(nc.psum_tensor(f"bank{i}", [128, 512], F32)) for i in range(8)]
            self.B_bank = [Buf() for _ in range(8)]
            self.B_x = [[Buf() for _ in range(4)] for _ in range(8)]
            self.B_c = Buf()
            B_c = self.B_c
            P.dma("sync", I("dma_start", out=self.cst[:, :], in_=self.cst_d[:, :]), writes=[B_c], key="d_cst")
            g = "gpsimd"
            P.op(g, I("memset", ap=self.ones_d[:, :], constant=1.0 / D), writes=[B_c])
            P.op(g, I("memset", ap=self.ones_ln[:, :], constant=1.0 / 512), writes=[B_c])
            P.op(g, I("memset", ap=self.ones64bd[:, :], constant=0.0), writes=[B_c])
            P.op(g, I("memset", ap=self.ones64bd[0:64, 0:64], constant=1.0 / 64), writes=[B_c])
            P.op(g, I("memset", ap=self.ones64bd[64:128, 64:128], constant=1.0 / 64), writes=[B_c])
            P.op(g, I("memset", ap=self.qkscale[0:64, :], constant=64.0), writes=[B_c])
            P.op(g, I("memset", ap=self.qkscale[64:128, :], constant=1.0), writes=[B_c])
            P.op(g, I("memset", ap=self.qkbias[0:64, :], constant=64.0 * EPS), writes=[B_c])
            P.op(g, I("memset", ap=self.qkbias[64:128, :], constant=EPS), writes=[B_c])
            P.op(g, I("memset", ap=self.ones_den[:, :], constant=1.0), writes=[B_c])
            P.op(g, I("memset", ap=self.negones[:, :], constant=-1.0), writes=[B_c])
            P.op(g, I("memset", ap=self.onesf[:, :], constant=1.0), writes=[B_c])
            P.op(g, I("memset", ap=self.onesb[:, :], constant=1.0), writes=[B_c])
            P.op(g, I("memset", ap=self.epsc[:, :], constant=EPS), writes=[B_c])
            P.op(g, I("memset", ap=self.eps64c[:, :], constant=64.0 * EPS), writes=[B_c])
            P.op(g, I("memset", ap=self.onec[:, :], constant=1.0), writes=[B_c])
            P.op(g, I("affine_select", out=self.negtri[:, :], in_=self.negones[:, :], pattern=[[-1, 128]], compare_op=ALU.is_ge, fill=0.0,
                      base=0, channel_multiplier=1), writes=[B_c])
            P.op(g, I("affine_select", out=self.ident[:, :], in_=self.onesf[:, :], pattern=[[-1, 128]], compare_op=ALU.is_equal, fill=0.0,
                      base=0, channel_multiplier=1), writes=[B_c])
            for r in range(4):
                P.op(g, I("affine_select", out=self.msb[:, r, :], in_=self.onesb[:, :], pattern=[[1, 512]], compare_op=ALU.is_gt, fill=0.0,
                          base=-r * 128, channel_multiplier=-1), writes=[B_c])
            B_out = Buf()
            for sq_ in range(ns):
                xv = self.xT_d[sq_].rearrange("(kc p) t -> p kc t", p=128)
                for kc in range(8):
                    P.dma("sync", I("dma_start", out=self.xT[:, kc, :], in_=xv[:, kc, :]), writes=self.B_x[kc], key=f"d_x{kc}")
                for L in self.layers:
                    if "f1" in self.parts:
                        self.ffn(L, 0)
                    if "mix" in self.parts:
                        if L % 2 == 0:
                            self.mixer_even(L)
                        else:
                            self.mixer_odd(L)
                    if "f2" in self.parts:
                        self.ffn(L, 1)
                yv = self.yT_d[sq_].rearrange("(kc p) t -> p kc t", p=128)
                for kc in range(8):
                    P.dma("sync", I("dma_start", out=yv[:, kc, :], in_=self.xT[:, kc, :]), reads=self.B_x[kc], writes=[B_out], key="d_out")
            P.final_wait("sync", [B_out])
            sems = {k: es.enter_context(nc.semaphore(k)) for k in P.cnt.keys()}
            with nc.Block() as block:
                @block.sync
                def _(e):
                    P.replay("sync", e, sems)

                @block.tensor
                def _(e):
                    P.replay("tensor", e, sems)

                @block.scalar
                def _(e):
                    P.replay("scalar", e, sems)

                @block.vector
                def _(e):
                    P.replay("vector", e, sems)

                @block.gpsimd
                def _(e):
                    P.replay("gpsimd", e, sems)
        return nc


def chunk_k(W, c0, w):
    return np.ascontiguousarray(W[:, c0:c0 + w].reshape(8, 128, w).transpose(1, 0, 2)).reshape(128, 8 * w)


def pack_weights(inp):
    f32 = np.float32
    g = lambda k: np.asarray(inp[k], dtype=f32)
    win = np.empty((8 * NJ, 128, 2048), f32)
    wout = np.empty((64, 128, NJ * 128), f32)
    for L in range(DEPTH):
        for wh, (ki, ko) in enumerate((("ffn1_w_in", "ffn1_w_out"), ("ffn2_w_in", "ffn2_w_out"))):
            Wi = g(ki)[L]
            Wo = g(ko)[L]
            fi = L * 2 + wh
            gg = Wi[:, :DFF].reshape(8, 128, NJ, 128)
            uu = Wi[:, DFF:].reshape(8, 128, NJ, 128)
            cat = np.concatenate([gg, uu], axis=3)
            win[fi * NJ:(fi + 1) * NJ] = cat.transpose(2, 1, 0, 3).reshape(NJ, 128, 2048)
            wout[fi * 8:(fi + 1) * 8] = Wo.reshape(NJ, 128, 8, 128).transpose(2, 1, 0, 3).reshape(8, 128, NJ * 128)
    abin = np.empty((40, 128, 1024), f32)
    about = np.empty((16, 128, 1024), f32)
    for e in range(2):
        W = g("ab_w_in")[e]
        for cc in range(20):
            abin[e * 20 + cc] = chunk_k(W, cc * 128, 128)
        Wo = g("ab_w_out")[e]
        for m in range(8):
            about[e * 8 + m] = chunk_k(Wo, m * 128, 128)
    fxqk = np.empty((96, 128, 512), f32)
    fxf = np.empty((2, 128, 128), f32)
    fxout = np.empty((16, 128, D), f32)
    for o in range(2):
        W = g("fox_w_in")[o]
        for wh in range(3):
            for h in range(16):
                fxqk[(o * 3 + wh) * 16 + h] = chunk_k(W, wh * 1024 + h * 64, 64)
        fxf[o] = chunk_k(W, 3072, 16)
        fxout[o * 8:(o + 1) * 8] = g("fox_w_out")[o].reshape(8, 128, D)
    cst = np.zeros((128, NCST), f32)
    col8 = lambda v: v.reshape(8, 128).T
    for L in range(DEPTH):
        cst[:, CO[("n1", L)]:CO[("n1", L)] + 8] = col8(g("ffn1_norm")[L])
        cst[:, CO[("nm", L)]:CO[("nm", L)] + 8] = col8(g("mix_norm")[L])
        cst[:, CO[("n2", L)]:CO[("n2", L)] + 8] = col8(g("ffn2_norm")[L])
    for e in range(2):
        cw = g("conv_w")[e]
        c0 = CO[("cw", e)]
        cst[:, c0:c0 + 4 * CW] = cw.reshape(CW, 4, 128).transpose(2, 1, 0).reshape(128, 4 * CW)
        for nm, key in (("cb", "conv_b"), ("lg", "conv_ln_g"), ("lb", "conv_ln_b")):
            cst[:, CO[(nm, e)]:CO[(nm, e)] + 4] = g(key)[e].reshape(4, 128).T
    for o in range(2):
        cst[:, CO[("gq", o)]] = np.tile(g("fox_q_norm")[o], 2)
        cst[:, CO[("gk", o)]] = np.tile(g("fox_k_norm")[o], 2)
        cst[0:16, CO[("fb", o)]] = g("fox_f_bias")[o]
    return dict(cst=cst, win=win, wout=wout, abin=abin, about=about, fxqk=fxqk, fxf=fxf, fxout=fxout)


def kernel(**inputs):
    x = np.asarray(inputs["x"], dtype=np.float32)
    wts = pack_weights(inputs)
    mk = MK(nseq=2)
    nc = mk.build()
    in_maps = []
    for c in range(NCORES):
        m = dict(wts)
        m["xT"] = np.ascontiguousarray(x[2 * c:2 * c + 2].transpose(0, 2, 1))
        in_maps.append(m)
    res = run_bass_kernel_spmd(nc, in_maps, core_ids=list(range(NCORES)))
    out = np.empty_like(x)
    for c in range(NCORES):
        out[2 * c:2 * c + 2] = np.asarray(res.results[c]["yT"]).transpose(0, 2, 1)
    return out
```

```python
import contextlib
import numpy as np
import concourse.bass as bass
import concourse.mybir as mybir
from concourse.bass_utils import run_bass_kernel_spmd

F32 = mybir.dt.float32
BF16 = mybir.dt.bfloat16
AF = mybir.ActivationFunctionType
ALU = mybir.AluOpType
ENGS = ["sync", "tensor", "scalar", "vector", "gpsimd"]

D = 1024
T = 2048
DFF = 2816
NJ = DFF // 128
DEPTH = 4
CW = 31
EPS = 1e-6
NCORES = 8


class Ev:
    __slots__ = ("key", "val")

    def __init__(self, key, val):
        self.key = key
        self.val = val


class Buf:
    __slots__ = ("w", "r")

    def __init__(self):
        self.w = None
        self.r = {}


def I(meth, **kw):
    return (meth, kw)


class Prog:
    def __init__(self):
        self.q = {e: [] for e in ENGS}
        self.cnt = {}
        self.seen = {e: {} for e in ENGS}
        self.nops = {e: 0 for e in ENGS}

    def _wait(self, eng, key, val):
        if self.seen[eng].get(key, 0) >= val:
            return
        self.seen[eng][key] = val
        self.q[eng].append(("wait", key, val))

    def _waits(self, eng, reads, writes):
        for b in reads:
            if b.w is not None:
                self._wait(eng, b.w.key, b.w.val)
        for b in writes:
            if b.w is not None:
                self._wait(eng, b.w.key, b.w.val)
            for ev in b.r.values():
                self._wait(eng, ev.key, ev.val)

    def _mark(self, ev, reads, writes):
        for b in reads:
            o = b.r.get(ev.key)
            if o is None or o.val < ev.val:
                b.r[ev.key] = ev
        for b in writes:
            b.w = ev
            b.r = {}

    def op(self, eng, ins, reads=(), writes=(), key=None, inc=1):
        key = key or eng
        self._waits(eng, reads, writes)
        self.cnt[key] = self.cnt.get(key, 0) + inc
        ev = Ev(key, self.cnt[key])
        self.q[eng].append(("op", ins, key, inc))
        self._mark(ev, reads, writes)
        self.nops[eng] += 1
        return ev

    def group(self, eng, inss, reads=(), writes=(), key=None):
        key = key or eng
        self._waits(eng, reads, writes)
        self.cnt[key] = self.cnt.get(key, 0) + 1
        ev = Ev(key, self.cnt[key])
        for f in inss[:-1]:
            self.q[eng].append(("op", f, None, 0))
        self.q[eng].append(("op", inss[-1], key, 1))
        self._mark(ev, reads, writes)
        self.nops[eng] += len(inss)
        return ev

    def dma(self, eng, ins, reads=(), writes=(), key=None):
        return self.op(eng, ins, reads, writes, key=key, inc=16)

    def barrier(self):
        for e in ENGS:
            for k, v in self.cnt.items():
                self._wait(e, k, v)

    def final_wait(self, eng, bufs):
        self._waits(eng, bufs, ())

    def replay(self, eng_name, eng, sems):
        pend = []
        for it in self.q[eng_name]:
            if it[0] == "wait":
                pend.append(it)
            else:
                for w in pend[:-1]:
                    eng.wait_ge(sems[w[1]], w[2])
                ins = getattr(eng, it[1][0])(**it[1][1])
                if pend:
                    ins._wait_ge(sems[pend[-1][1]], pend[-1][2])
                pend = []
                if it[2] is not None:
                    ins.then_inc(sems[it[2]], it[3])
        for w in pend:
            eng.wait_ge(sems[w[1]], w[2])


class Arena:
    def __init__(self, tile, nwords):
        self.tile = tile
        self.n = nwords
        self.off = 0

    def reset(self):
        self.off = 0

    def alloc(self, shape, dt):
        n = int(np.prod(shape))
        nw = n if dt == F32 else (n + 1) // 2
        nw = (nw + 7) // 8 * 8
        assert self.off + nw <= self.n, f"arena overflow {self.off}+{nw}>{self.n}"
        v = self.tile[:, self.off:self.off + nw]
        self.off += nw
        if dt != F32:
            v = v.bitcast(dt)
        v = v[:, 0:n]
        if len(shape) == 2:
            v = v.rearrange("p (a b) -> p a b", b=shape[1])
        elif len(shape) == 3:
            v = v.rearrange("p (a b c) -> p a b c", b=shape[1], c=shape[2])
        return v


def cst_layout():
    off = {}
    c = 0
    for L in range(DEPTH):
        for nm in ("n1", "nm", "n2"):
            off[(nm, L)] = c
            c += 8
    for e in range(2):
        off[("cw", e)] = c
        c += 4 * CW
        for nm in ("cb", "lg", "lb"):
            off[(nm, e)] = c
            c += 4
    for o in range(2):
        for nm in ("gq", "gk", "fb"):
            off[(nm, o)] = c
            c += 1
    return off, c


CO, NCST = cst_layout()


class MK:
    def __init__(self, nseq=2, layers=(0, 1, 2, 3), parts=("f1", "mix", "f2")):
        self.nseq = nseq
        self.layers = layers
        self.parts = parts
        self.nc = bass.Bass("TRN2", target_bir_lowering=False)
        self.P = Prog()
        self.slotn = {}

    def wslot(self, name, nslots):
        n = self.slotn.get(name, 0)
        self.slotn[name] = n + 1
        return n % nslots

    def load_w(self, name, tiles, bufs, src, maxlast=None):
        s = self.wslot(name, len(tiles))
        t = tiles[s]
        nd = len(t.shape)
        if nd == 3:
            o = t.rearrange("p a b -> p (a b)")
        else:
            o = t
        kw = dict(out=o, in_=src)
        if maxlast:
            kw["max_dma_last_dim"] = maxlast
        self.P.dma("gpsimd", I("dma_start", **kw), writes=[bufs[s]], key=f"d_{name}{s}")
        return t, bufs[s]

    def norm_block(self, blk, gcol, dst, dst_sl, B_dst):
        P = self.P
        s = self.wslot("nrm", 2)
        sq, B_sq = self.sq[s], self.B_sq[s]
        rstd, B_rstd = self.rstd[s], self.B_rstd[s]
        pn, B_pn = self.bank[6], self.B_bank[6]
        t0 = blk * 512
        for kc in range(8):
            P.op("scalar", I("activation", out=sq[:, kc, :], in_=self.xT[:, kc, t0:t0 + 512], func=AF.Square),
                 reads=[self.B_x[kc][blk]], writes=[B_sq])
        P.group("tensor", [I("matmul", out=pn[:, :], lhsT=self.ones_d[:, :], rhs=sq[:, kc, :], start=(kc == 0), stop=(kc == 7))
                           for kc in range(8)], reads=[B_sq, self.B_c], writes=[B_pn])
        P.op("scalar", I("activation", out=rstd[:, :], in_=pn[:, :], func=AF.Sqrt, bias=self.epsc[:, 0:1], scale=1.0),
             reads=[B_pn, self.B_c], writes=[B_rstd])
        P.op("vector", I("reciprocal", out=rstd[:, :], in_=rstd[:, :]), reads=[B_rstd], writes=[B_rstd])
        for kc in range(8):
            P.op("vector", I("scalar_tensor_tensor", out=dst[:, kc, dst_sl], in0=self.xT[:, kc, t0:t0 + 512],
                             scalar=self.cst[:, gcol + kc:gcol + kc + 1], in1=rstd[:, :], op0=ALU.mult, op1=ALU.mult),
                 reads=[self.B_x[kc][blk], B_rstd, self.B_c], writes=[B_dst[kc]])

    def ffn(self, L, which):
        P = self.P
        A = self.arena
        P.barrier()
        A.reset()
        gcol = CO[("n1" if which == 0 else "n2", L)]
        fidx = L * 2 + which
        hT = A.alloc([8, 1024], BF16)
        aT = A.alloc([NJ, 1024], BF16)
        self.sq = [A.alloc([8, 512], BF16) for _ in range(2)]
        self.rstd = [A.alloc([512], F32) for _ in range(2)]
        self.B_sq = [Buf(), Buf()]
        self.B_rstd = [Buf(), Buf()]
        sg = [A.alloc([512], F32) for _ in range(2)]
        wi = [A.alloc([8, 256], BF16) for _ in range(3)]
        wo = [A.alloc([NJ, 128], BF16) for _ in range(2)]
        B_sg = [Buf(), Buf()]
        B_wi = [Buf() for _ in range(3)]
        B_wo = [Buf() for _ in range(2)]
        bank, B_bank = self.bank, self.B_bank
        for tb in range(2):
            B_h = [[Buf() for _ in range(2)] for _ in range(8)]
            B_a = [[Buf() for _ in range(2)] for _ in range(NJ)]
            for sub in range(2):
                self.norm_block(tb * 2 + sub, gcol, hT, slice(sub * 512, (sub + 1) * 512), [B_h[kc][sub] for kc in range(8)])
            for j in range(NJ):
                w, B_w = self.load_w("wi", wi, B_wi, self.win_d[fidx * NJ + j, :, :])
                for sub in range(2):
                    b = (j * 2 + sub) % 2
                    tsl = slice(sub * 512, (sub + 1) * 512)
                    hreads = [B_w] + [B_h[kc][sub] for kc in range(8)]
                    P.group("tensor", [I("matmul", out=bank[b][:, :], lhsT=w[:, kc, 0:128], rhs=hT[:, kc, tsl], start=(kc == 0), stop=(kc == 7))
                                       for kc in range(8)], reads=hreads, writes=[B_bank[b]])
                    P.group("tensor", [I("matmul", out=bank[2 + b][:, :], lhsT=w[:, kc, 128:256], rhs=hT[:, kc, tsl], start=(kc == 0), stop=(kc == 7))
                                       for kc in range(8)], reads=hreads, writes=[B_bank[2 + b]])
                    P.op("scalar", I("activation", out=sg[b][:, :], in_=bank[b][:, :], func=AF.Silu),
                         reads=[B_bank[b]], writes=[B_sg[b]])
                    P.op("vector", I("tensor_tensor", out=aT[:, j, tsl], in0=sg[b][:, :], in1=bank[2 + b][:, :], op=ALU.mult),
                         reads=[B_sg[b], B_bank[2 + b]], writes=[B_a[j][sub]])
            for m in range(8):
                w, B_w = self.load_w("wo", wo, B_wo, self.wout_d[fidx * 8 + m, :, :], maxlast=4096)
                for sub in range(2):
                    blk = tb * 2 + sub
                    b = 4 + (m * 2 + sub) % 2
                    tsl = slice(sub * 512, (sub + 1) * 512)
                    P.group("tensor", [I("matmul", out=bank[b][:, :], lhsT=w[:, kc, :], rhs=aT[:, kc, tsl], start=(kc == 0), stop=(kc == NJ - 1))
                                       for kc in range(NJ)], reads=[B_w] + [B_a[kc][sub] for kc in range(NJ)], writes=[B_bank[b]])
                    xs = self.xT[:, m, blk * 512:(blk + 1) * 512]
                    P.op("vector", I("scalar_tensor_tensor", out=xs, in0=bank[b][:, :], scalar=0.5, in1=xs, op0=ALU.mult, op1=ALU.add),
                         reads=[B_bank[b]], writes=[self.B_x[m][blk]])

    def mixer_even(self, L):
        P = self.P
        A = self.arena
        e = L // 2
        P.barrier()
        A.reset()
        bank, B_bank = self.bank, self.B_bank
        cst = self.cst
        TP = T + CW - 1
        wc = [A.alloc([8, 128], BF16) for _ in range(4)]
        B_wc = [Buf() for _ in range(4)]
        uraw = A.alloc([4 * TP], F32)
        upad = uraw.rearrange("p (c t) -> p c t", c=4)
        ubf = uraw.bitcast(BF16).rearrange("p (c t) -> p c t", c=4)

        def ufin(c, blk):
            o = 2 * (CW - 1 + blk * 512)
            return ubf[:, c, o:o + 512]

        B_up = [[Buf() for _ in range(4)] for _ in range(4)]
        B_pad = [Buf() for _ in range(4)]
        acc = [A.alloc([1024], F32) for _ in range(2)]
        B_acc = [Buf(), Buf()]
        hT = A.alloc([8, T], BF16)
        B_h = [[Buf() for _ in range(4)] for _ in range(8)]
        mark = A.off
        self.sq = [A.alloc([8, 512], BF16) for _ in range(2)]
        self.rstd = [A.alloc([512], F32) for _ in range(2)]
        self.B_sq = [Buf(), Buf()]
        self.B_rstd = [Buf(), Buf()]
        tmp = [A.alloc([512], F32) for _ in range(2)]
        B_tmp = [Buf(), Buf()]
        for blk in range(4):
            self.norm_block(blk, CO[("nm", L)], hT, slice(blk * 512, (blk + 1) * 512), [B_h[kc][blk] for kc in range(8)])
        hall = lambda blk: [B_h[kc][blk] for kc in range(8)]

        for c in range(4):
            P.op("gpsimd", I("memset", ap=upad[:, c, 0:CW - 1], constant=0.0), writes=[B_pad[c]])
        for c in range(4):
            wv, B_wv = self.load_w("wc", wc, B_wc, self.abin_d[e * 20 + c, :, :])
            wg, B_wg = self.load_w("wc", wc, B_wc, self.abin_d[e * 20 + 4 + c, :, :])
            for blk in range(4):
                b = blk % 2
                tsl = slice(blk * 512, (blk + 1) * 512)
                P.group("tensor", [I("matmul", out=bank[b][:, :], lhsT=wv[:, kc, :], rhs=hT[:, kc, tsl], start=(kc == 0), stop=(kc == 7))
                                   for kc in range(8)], reads=[B_wv] + hall(blk), writes=[B_bank[b]])
                P.group("tensor", [I("matmul", out=bank[2 + b][:, :], lhsT=wg[:, kc, :], rhs=hT[:, kc, tsl], start=(kc == 0), stop=(kc == 7))
                                   for kc in range(8)], reads=[B_wg] + hall(blk), writes=[B_bank[2 + b]])
                P.op("scalar", I("activation", out=tmp[b][:, :], in_=bank[2 + b][:, :], func=AF.Sigmoid),
                     reads=[B_bank[2 + b]], writes=[B_tmp[b]])
                P.op("vector", I("tensor_tensor", out=upad[:, c, CW - 1 + blk * 512:CW - 1 + (blk + 1) * 512], in0=tmp[b][:, :], in1=bank[b][:, :], op=ALU.mult),
                     reads=[B_tmp[b], B_bank[b]], writes=[B_up[c][blk]])

        cw0 = CO[("cw", e)]
        cb0 = CO[("cb", e)]

        def conv_items():
            i = 0
            for hb in (1, 0):
                base = hb * 1024
                for c in range(4):
                    a, B_a = acc[i % 2], B_acc[i % 2]
                    i += 1
                    rd = [B_up[c][2 * hb], B_up[c][2 * hb + 1], B_up[c][2 * hb - 1] if hb > 0 else B_pad[c], self.B_c]
                    wcol = lambda k: cst[:, cw0 + c * CW + k:cw0 + c * CW + k + 1]
                    P.op("vector", I("tensor_scalar", out=a[:, :], in0=upad[:, c, base:base + 1024], scalar1=wcol(0),
                                     scalar2=cst[:, cb0 + c:cb0 + c + 1], op0=ALU.mult, op1=ALU.add), reads=rd, writes=[B_a])
                    yield
                    for k in range(1, CW - 1):
                        P.op("vector", I("scalar_tensor_tensor", out=a[:, :], in0=upad[:, c, base + k:base + k + 1024], scalar=wcol(k),
                                         in1=a[:, :], op0=ALU.mult, op1=ALU.add), reads=rd, writes=[B_a])
                        yield
                    k = CW - 1
                    home = upad[:, c, base + k:base + k + 1024]
                    P.op("vector", I("scalar_tensor_tensor", out=home, in0=home, scalar=wcol(k), in1=a[:, :], op0=ALU.mult, op1=ALU.add),
                         reads=[B_a, self.B_c], writes=[B_up[c][2 * hb], B_up[c][2 * hb + 1]])
                    yield

        conv = conv_items()

        P.barrier()
        A.off = mark
        vtp = [A.alloc([16, 128], BF16) for _ in range(1)]
        B_vtp = [[Buf() for _ in range(4)] for _ in range(1)]
        oT = A.alloc([4, T], BF16)
        B_o = [[Buf() for _ in range(4)] for _ in range(4)]
        qT = [A.alloc([T], BF16) for _ in range(1)]
        kT = [A.alloc([T], BF16) for _ in range(1)]
        B_q = [[Buf() for _ in range(4)] for _ in range(1)]
        B_k = [[Buf() for _ in range(4)] for _ in range(1)]
        mark2 = A.off
        NS = 4
        et = [A.alloc([512], F32) for _ in range(NS)]
        spb = [A.alloc([512], BF16) for _ in range(NS)]
        ssf = A.alloc([512], F32)
        ssb = [A.alloc([512], BF16) for _ in range(2)]
        At = [A.alloc([512], BF16) for _ in range(3)]
        B_et = [Buf() for _ in range(NS)]
        B_spb = [Buf() for _ in range(NS)]
        B_ssf = Buf()
        B_ssb = [Buf(), Buf()]
        B_At = [Buf(), Buf(), Buf()]
        st = dict(ss_cur=0, acnt=0)
        s2q = []
        s3q = []

        def advance(depth=2):
            old3 = s3q.pop(0) if s3q else None
            if len(s2q) >= depth:
                s2q.pop(0)()
            if old3:
                old3()

        def flush_all():
            while s2q or s3q:
                advance(1)

        cnt = 0
        ocnt = 0
        for pr in range(4):
            ps_ = 0
            flush_all()
            wq, B_wq = self.load_w("wc", wc, B_wc, self.abin_d[e * 20 + 8 + pr, :, :])
            wk, B_wk = self.load_w("wc", wc, B_wc, self.abin_d[e * 20 + 12 + pr, :, :])
            wv_, B_wv_ = self.load_w("wc", wc, B_wc, self.abin_d[e * 20 + 16 + pr, :, :])
            for g_ in range(4):
                vb = 6 + g_ % 2
                P.group("tensor", [I("matmul", out=bank[vb][:, i * 128:(i + 1) * 128], lhsT=hT[:, kc, (g_ * 4 + i) * 128:(g_ * 4 + i + 1) * 128], rhs=wv_[:, kc, :],
                                     start=(kc == 0), stop=(kc == 7)) for i in range(4) for kc in range(8)],
                        reads=[B_wv_] + hall(g_), writes=[B_bank[vb]])
                P.op("scalar", I("activation", out=vtp[ps_][:, g_ * 4:(g_ + 1) * 4, :], in_=bank[vb][:, :].rearrange("p (a b) -> p a b", b=128), func=AF.Copy),
                     reads=[B_bank[vb]], writes=[B_vtp[ps_][g_]])
            for blk in range(4):
                tsl = slice(blk * 512, (blk + 1) * 512)
                P.group("tensor", [I("matmul", out=bank[6][:, :], lhsT=wq[:, kc, :], rhs=hT[:, kc, tsl], start=(kc == 0), stop=(kc == 7))
                                   for kc in range(8)], reads=[B_wq] + hall(blk), writes=[B_bank[6]])
                P.op("scalar", I("activation", out=qT[ps_][:, tsl], in_=bank[6][:, :], func=AF.Copy, scale=0.125),
                     reads=[B_bank[6]], writes=[B_q[ps_][blk]])
                P.group("tensor", [I("matmul", out=bank[7][:, :], lhsT=wk[:, kc, :], rhs=hT[:, kc, tsl], start=(kc == 0), stop=(kc == 7))
                                   for kc in range(8)], reads=[B_wk] + hall(blk), writes=[B_bank[7]])
                P.op("scalar", I("activation", out=kT[ps_][:, tsl], in_=bank[7][:, :], func=AF.Copy), reads=[B_bank[7]], writes=[B_k[ps_][blk]])
            for half in range(2):
                psl = slice(half * 64, half * 64 + 64)
                for qb in range(4):
                    qs = slice(qb * 512, (qb + 1) * 512)
                    n = 4 * qb + 4
                    ob = 4 + (ocnt % 2)
                    ocnt += 1
                    for It in reversed(range(n)):
                        r = It - 4 * qb
                        b = cnt % 4
                        e4 = cnt % NS
                        cnt += 1
                        first = (It == n - 1)
                        ks = slice(It * 128, (It + 1) * 128)
                        qk = dict(lhsT=kT[ps_][psl, ks], rhs=qT[ps_][psl, qs])
                        qkr = [B_k[ps_][It // 4], B_q[ps_][qb]]
                        P.op("tensor", I("matmul", out=bank[b][:, :], start=True, stop=True, **qk), reads=qkr, writes=[B_bank[b]])
                        P.op("scalar", I("activation", out=et[e4][:, :], in_=bank[b][:, :], func=AF.Exp), reads=[B_bank[b]], writes=[B_et[e4]])
                        P.op("scalar", I("activation", out=spb[e4][:, :], in_=et[e4][:, :], func=AF.Ln, bias=self.onec[:, 0:1], scale=1.0),
                             reads=[B_et[e4], self.B_c], writes=[B_spb[e4]])
                        if r >= 0:
                            P.op("gpsimd", I("affine_select", out=spb[e4][:, :], in_=spb[e4][:, :], pattern=[[1, 512]], compare_op=ALU.is_gt,
                                             fill=0.0, base=-r * 128, channel_multiplier=-1), reads=[], writes=[B_spb[e4]])

                        def s2(It=It, r=r, b=b, e4=e4, first=first, qk=qk, qkr=qkr, ob=ob, n=n, ps_=ps_, psl=psl, pr=pr, qs=qs, qb=qb):
                            cur = st["ss_cur"]
                            a3 = st["acnt"] % 3
                            st["acnt"] += 1
                            mm = [I("matmul", out=bank[b][:, :], lhsT=self.negtri[:, :], rhs=spb[e4][:, :], start=False, stop=first)]
                            rd = [B_spb[e4], self.B_c]
                            if not first:
                                mm.append(I("matmul", out=bank[b][:, :], lhsT=self.negones[:, :], rhs=ssb[cur][:, :], start=False, stop=True))
                                rd.append(B_ssb[cur])
                            P.group("tensor", mm, reads=rd, writes=[B_bank[b]])
                            P.op("scalar", I("activation", out=At[a3][:, :], in_=bank[b][:, :], func=AF.Exp), reads=[B_bank[b]], writes=[B_At[a3]])
                            if r >= 0:
                                P.op("gpsimd", I("affine_select", out=At[a3][:, :], in_=At[a3][:, :], pattern=[[1, 512]], compare_op=ALU.is_gt,
                                                 fill=0.0, base=-r * 128, channel_multiplier=-1), reads=[], writes=[B_At[a3]])
                            if It > 0:
                                sn = 1 - cur
                                if first:
                                    P.op("vector", I("tensor_copy", out=ssb[sn][:, :], in_=spb[e4][:, :]), reads=[B_spb[e4]], writes=[B_ssb[sn]])
                                    P.op("vector", I("tensor_copy", out=ssf[:, :], in_=spb[e4][:, :]), reads=[B_spb[e4]], writes=[B_ssf])
                                else:
                                    P.op("vector", I("tensor_tensor", out=ssb[sn][:, :], in0=ssf[:, :], in1=spb[e4][:, :], op=ALU.add),
                                         reads=[B_ssf, B_spb[e4]], writes=[B_ssb[sn]])
                                    P.op("vector", I("tensor_tensor", out=ssf[:, :], in0=ssf[:, :], in1=spb[e4][:, :], op=ALU.add),
                                         reads=[B_spb[e4]], writes=[B_ssf])
                                st["ss_cur"] = sn

                            def s3():
                                P.op("tensor", I("matmul", out=bank[ob][:, :], lhsT=vtp[ps_][:, It, :], rhs=At[a3][:, :], start=first, stop=(It == 0)),
                                     reads=[B_vtp[ps_][It // 4], B_At[a3]], writes=[B_bank[ob]])
                                if It == 0:
                                    P.op("scalar", I("activation", out=oT[psl, pr, qs], in_=bank[ob][psl, :], func=AF.Copy), reads=[B_bank[ob]], writes=[B_o[pr][qb]])
                            s3q.append(s3)

                        s2q.append(s2)
                        advance(3)
                        next(conv, None)
        flush_all()
        for _ in conv:
            pass

        P.barrier()
        A.off = mark2
        ysq = A.alloc([4, 512], F32)
        B_ysq = Buf()
        m2 = A.alloc([512], F32)
        var = A.alloc([512], F32)
        B_m2 = Buf()
        B_var = Buf()
        t1 = [A.alloc([512], F32) for _ in range(2)]
        B_t1 = [Buf(), Buf()]
        lg0, lb0 = CO[("lg", e)], CO[("lb", e)]
        for blk in range(4):
            ysl = slice(CW - 1 + blk * 512, CW - 1 + (blk + 1) * 512)
            B_yb = [B_up[c][blk] for c in range(4)]
            for c in range(4):
                P.op("scalar", I("activation", out=ysq[:, c, :], in_=upad[:, c, ysl], func=AF.Square), reads=[B_yb[c]], writes=[B_ysq])
            P.group("tensor", [I("matmul", out=bank[0][:, :], lhsT=self.ones_ln[:, :], rhs=upad[:, c, ysl], start=(c == 0), stop=(c == 3))
                               for c in range(4)], reads=B_yb + [self.B_c], writes=[B_bank[0]])
            P.group("tensor", [I("matmul", out=bank[1][:, :], lhsT=self.ones_ln[:, :], rhs=ysq[:, c, :], start=(c == 0), stop=(c == 3))
                               for c in range(4)], reads=[B_ysq, self.B_c], writes=[B_bank[1]])
            P.op("scalar", I("activation", out=m2[:, :], in_=bank[0][:, :], func=AF.Square), reads=[B_bank[0]], writes=[B_m2])
            P.op("vector", I("scalar_tensor_tensor", out=var[:, :], in0=m2[:, :], scalar=-1.0, in1=bank[1][:, :], op0=ALU.mult, op1=ALU.add),
                 reads=[B_m2, B_bank[1]], writes=[B_var])
            P.op("vector", I("tensor_scalar", out=var[:, :], in0=var[:, :], scalar1=0.0, scalar2=None, op0=ALU.max), reads=[B_var], writes=[B_var])
            P.op("scalar", I("activation", out=var[:, :], in_=var[:, :], func=AF.Sqrt, bias=self.epsc[:, 0:1], scale=1.0),
                 reads=[B_var, self.B_c], writes=[B_var])
            P.op("vector", I("reciprocal", out=var[:, :], in_=var[:, :]), reads=[B_var], writes=[B_var])
            for c in range(4):
                b = c % 2
                P.op("vector", I("tensor_tensor", out=t1[b][:, :], in0=upad[:, c, ysl], in1=bank[0][:, :], op=ALU.subtract),
                     reads=[B_yb[c], B_bank[0]], writes=[B_t1[b]])
                P.op("vector", I("tensor_tensor", out=t1[b][:, :], in0=t1[b][:, :], in1=var[:, :], op=ALU.mult),
                     reads=[B_var], writes=[B_t1[b]])
                P.op("scalar", I("activation", out=ufin(c, blk), in_=t1[b][:, :], func=AF.Silu,
                                 scale=cst[:, lg0 + c:lg0 + c + 1], bias=cst[:, lb0 + c:lb0 + c + 1]),
                     reads=[B_t1[b], self.B_c], writes=[B_yb[c]])

        for m in range(8):
            w, B_w = self.load_w("wc", wc, B_wc, self.about_d[e * 8 + m, :, :])
            for blk in range(4):
                b = 6 + blk % 2
                tsl = slice(blk * 512, (blk + 1) * 512)
                mm = [I("matmul", out=bank[b][:, :], lhsT=w[:, kc, :], rhs=(ufin(kc, blk) if kc < 4 else oT[:, kc - 4, tsl]),
                        start=(kc == 0), stop=(kc == 7)) for kc in range(8)]
                P.group("tensor", mm, reads=[B_w] + [B_up[c][blk] for c in range(4)] + [B_o[p_][blk] for p_ in range(4)], writes=[B_bank[b]])
                xs = self.xT[:, m, tsl]
                P.op("vector", I("tensor_tensor", out=xs, in0=bank[b][:, :], in1=xs, op=ALU.add), reads=[B_bank[b]], writes=[self.B_x[m][blk]])

    def mixer_odd(self, L):
        P = self.P
        A = self.arena
        o = L // 2
        P.barrier()
        A.reset()
        bank, B_bank = self.bank, self.B_bank
        cst = self.cst
        hT = A.alloc([8, T], BF16)
        B_h = [[Buf() for _ in range(4)] for _ in range(8)]
        mark = A.off
        self.sq = [A.alloc([8, 512], BF16) for _ in range(2)]
        self.rstd = [A.alloc([512], F32) for _ in range(2)]
        self.B_sq = [Buf(), Buf()]
        self.B_rstd = [Buf(), Buf()]
        for blk in range(4):
            self.norm_block(blk, CO[("nm", L)], hT, slice(blk * 512, (blk + 1) * 512), [B_h[kc][blk] for kc in range(8)])
        hall = lambda blk: [B_h[kc][blk] for kc in range(8)]
        P.barrier()
        A.off = mark
        wf = A.alloc([8, 16], BF16)
        B_wf = Buf()
        lf = A.alloc([T], F32)
        cpos = A.alloc([T], F32)
        cp3 = A.alloc([3, T], BF16)
        r1 = lf
        cposT = A.alloc([16, 16], F32)
        nfb = A.alloc([1], F32)
        tmpf = A.alloc([512], F32)
        B_lf, B_cpos, B_cp3, B_cposT, B_nfb, B_tmpf = (Buf() for _ in range(6))
        B_r1 = B_lf
        P.dma("gpsimd", I("dma_start", out=wf.rearrange("p a b -> p (a b)"), in_=self.fxf_d[o, :, :]), writes=[B_wf], key="d_wf")
        P.op("vector", I("tensor_scalar", out=nfb[0:16, :], in0=cst[0:16, CO[("fb", o)]:CO[("fb", o)] + 1], scalar1=-1.0, scalar2=None, op0=ALU.mult),
             reads=[self.B_c], writes=[B_nfb])
        for blk in range(4):
            tsl = slice(blk * 512, (blk + 1) * 512)
            P.group("tensor", [I("matmul", out=bank[6][0:16, :], lhsT=wf[:, kc, :], rhs=hT[:, kc, tsl], start=(kc == 0), stop=(kc == 7))
                               for kc in range(8)], reads=[B_wf] + hall(blk), writes=[B_bank[6]])
            P.op("scalar", I("activation", out=tmpf[0:16, :], in_=bank[6][0:16, :], func=AF.Exp, scale=-1.0, bias=nfb[0:16, 0:1]),
                 reads=[B_bank[6], B_nfb], writes=[B_tmpf])
            P.op("scalar", I("activation", out=lf[0:16, tsl], in_=tmpf[0:16, :], func=AF.Ln, bias=self.onec[0:16, 0:1], scale=1.0),
                 reads=[B_tmpf, self.B_c], writes=[B_lf])
        P.op("vector", I("tensor_tensor_scan", out=cpos[0:16, :], data0=lf[0:16, :], data1=lf[0:16, :], initial=0.0, op0=ALU.add, op1=ALU.max),
             reads=[B_lf], writes=[B_cpos])
        P.op("vector", I("tensor_scalar", out=r1[0:16, :], in0=cpos[0:16, :], scalar1=-1.0, scalar2=None, op0=ALU.mult), reads=[B_cpos], writes=[B_r1])
        P.op("vector", I("tensor_copy", out=cp3[0:16, 0, :], in_=r1[0:16, :]), reads=[B_r1], writes=[B_cp3])
        P.op("vector", I("tensor_tensor", out=r1[0:16, :], in0=r1[0:16, :], in1=cp3[0:16, 0, :], op=ALU.subtract), reads=[B_cp3], writes=[B_r1])
        P.op("vector", I("tensor_copy", out=cp3[0:16, 1, :], in_=r1[0:16, :]), reads=[B_r1], writes=[B_cp3])
        P.op("vector", I("tensor_tensor", out=r1[0:16, :], in0=r1[0:16, :], in1=cp3[0:16, 1, :], op=ALU.subtract), reads=[B_cp3], writes=[B_r1])
        P.op("vector", I("tensor_copy", out=cp3[0:16, 2, :], in_=r1[0:16, :]), reads=[B_r1], writes=[B_cp3])
        for It in range(16):
            b = 6 + It % 2
            P.op("tensor", I("transpose", out=bank[b][:, 0:16], in_=cpos[0:16, It * 128:(It + 1) * 128], identity=self.ident[0:16, 0:16]),
                 reads=[B_cpos, self.B_c], writes=[B_bank[b]])
            P.op("vector", I("tensor_copy", out=cposT[:, It, :], in_=bank[b][:, 0:16]), reads=[B_bank[b]], writes=[B_cposT])
        wq4 = [A.alloc([8, 64], BF16) for _ in range(6)]
        B_wq4 = [Buf() for _ in range(6)]
        vtp = [A.alloc([16, 128], BF16) for _ in range(2)]
        B_vtp = [[Buf() for _ in range(4)] for _ in range(2)]
        wout = [A.alloc([D], BF16) for _ in range(2)]
        B_wout = [Buf(), Buf()]
        qa = [A.alloc([T], BF16) for _ in range(2)]
        ka = [A.alloc([T], BF16) for _ in range(2)]
        B_qa = [[Buf() for _ in range(5)] for _ in range(2)]
        B_ka = [[Buf() for _ in range(5)] for _ in range(2)]
        oTp = [A.alloc([T], BF16) for _ in range(2)]
        B_oTp = [[Buf() for _ in range(4)] for _ in range(2)]
        sqh = [A.alloc([512], BF16) for _ in range(2)]
        rsh = [A.alloc([512], F32) for _ in range(2)]
        B_sqh = [Buf(), Buf()]
        B_rsh = [Buf(), Buf()]
        At = [A.alloc([512], BF16) for _ in range(4)]
        B_At = [Buf() for _ in range(4)]
        rden = [A.alloc([512], F32) for _ in range(2)]
        B_rden = [Buf(), Buf()]
        for s in range(2):
            P.op("gpsimd", I("memset", ap=ka[s][64:128, :], constant=0.0), writes=[B_ka[s][4]])
            P.op("gpsimd", I("memset", ap=qa[s][64:128, :], constant=0.0), writes=[B_qa[s][4]])
            P.op("gpsimd", I("memset", ap=ka[s][64:67, :], constant=1.0), writes=[B_ka[s][4]])
        P.op("gpsimd", I("memset", ap=vtp[0][:, :, 64:128], constant=1.0), writes=B_vtp[0])
        P.op("gpsimd", I("memset", ap=vtp[1][:, :, 0:64], constant=1.0), writes=B_vtp[1])
        self._pcnt = 0

        wld = {}

        def preload(h):
            wld[h] = [self.load_w("wq4", wq4, B_wq4, self.fxqk_d[(o * 3 + j) * 16 + h, :, :]) for j in range(3)]

        def proj(h):
            s = h % 2
            (wq, B_wq), (wk, B_wk) = wld[h][0], wld[h][1]
            for j in range(3):
                P.dma("sync", I("dma_start", out=qa[s][64 + j:65 + j, :], in_=cp3[h:h + 1, j, :]), reads=[B_cp3], writes=[B_qa[s][4]], key=f"d_aug{s}")
            gq, gk = CO[("gq", o)], CO[("gk", o)]
            tails = []
            for blk in range(4):
                tsl = slice(blk * 512, (blk + 1) * 512)
                u = self._pcnt % 2
                pb = 5 + self._pcnt % 2
                self._pcnt += 1
                P.group("tensor", [I("matmul", out=bank[pb][0:64, :], lhsT=wq[:, kc, :], rhs=hT[:, kc, tsl], start=(kc == 0), stop=(kc == 7)) for kc in range(8)]
                        + [I("matmul", out=bank[pb][64:128, :], lhsT=wk[:, kc, :], rhs=hT[:, kc, tsl], start=(kc == 0), stop=(kc == 7)) for kc in range(8)],
                        reads=[B_wq, B_wk] + hall(blk), writes=[B_bank[pb]])
                P.op("scalar", I("activation", out=sqh[u][:, :], in_=bank[pb][:, :], func=AF.Square), reads=[B_bank[pb]], writes=[B_sqh[u]])

                def tail(u=u, pb=pb, tsl=tsl, blk=blk):
                    P.op("tensor", I("matmul", out=bank[7][:, :], lhsT=self.ones64bd[:, :], rhs=sqh[u][:, :], start=True, stop=True),
                         reads=[B_sqh[u], self.B_c], writes=[B_bank[7]])
                    P.op("scalar", I("activation", out=rsh[u][:, :], in_=bank[7][:, :], func=AF.Ln, bias=self.qkbias[:, 0:1], scale=self.qkscale[:, 0:1]),
                         reads=[B_bank[7], self.B_c], writes=[B_rsh[u]])
                    P.op("scalar", I("activation", out=rsh[u][:, :], in_=rsh[u][:, :], func=AF.Exp, scale=-0.5), reads=[B_rsh[u]], writes=[B_rsh[u]])
                    P.op("vector", I("scalar_tensor_tensor", out=qa[s][0:64, tsl], in0=bank[pb][0:64, :], scalar=cst[0:64, gq:gq + 1], in1=rsh[u][0:64, :],
                                     op0=ALU.mult, op1=ALU.mult), reads=[B_bank[pb], B_rsh[u], self.B_c], writes=[B_qa[s][blk]])
                    P.op("vector", I("scalar_tensor_tensor", out=ka[s][0:64, tsl], in0=bank[pb][64:128, :], scalar=cst[64:128, gk:gk + 1], in1=rsh[u][64:128, :],
                                     op0=ALU.mult, op1=ALU.mult), reads=[B_bank[pb], B_rsh[u], self.B_c], writes=[B_ka[s][blk]])

                if tails:
                    tails.pop(0)()
                tails.append(tail)
                yield
            while tails:
                tails.pop(0)()
            yield

        def vproj(h):
            s = h % 2
            vc = slice(s * 64, s * 64 + 64)
            w, B_w = wld[h][2]
            for g in range(4):
                P.group("tensor", [I("matmul", out=bank[7][:, i * 64:(i + 1) * 64], lhsT=hT[:, kc, (g * 4 + i) * 128:(g * 4 + i + 1) * 128], rhs=w[:, kc, :],
                                     start=(kc == 0), stop=(kc == 7)) for i in range(4) for kc in range(8)],
                        reads=[B_w] + hall(g), writes=[B_bank[7]])
                P.op("vector", I("tensor_copy", out=vtp[s][:, g * 4:(g + 1) * 4, vc], in_=bank[7][:, 0:256].rearrange("p (a b) -> p a b", b=64)),
                     reads=[B_bank[7]], writes=[B_vtp[s][g]])
                yield

        pend = []
        outq = []

        def flush():
            while pend:
                pend.pop(0)()

        cnt = 0
        ocnt = 0
        preload(0)
        preload(1)
        for _ in proj(0):
            pass
        for _ in vproj(0):
            pass
        for pr in range(8):
            vs = pr % 2
            wo_, B_wo_ = self.load_w("wfo", wout, B_wout, self.fxout_d[o * 8 + pr, :, :])
            for half in range(2):
                h = 2 * pr + half
                s = h % 2
                flush()
                if h + 2 < 16:
                    preload(h + 2)

                def both(hn=h + 1, oq=list(outq)):
                    for g_ in oq:
                        yield from g_
                    if hn < 16:
                        yield from proj(hn)
                        yield from vproj(hn)
                outq.clear()
                work = both()
                tix = 0
                psl = slice(half * 64, half * 64 + 64)
                osl = slice((1 - half) * 64, (1 - half) * 64 + 64)
                for qb in range(4):
                    n = 4 * qb + 4
                    ob = 3 + (ocnt % 2)
                    ocnt += 1
                    order = list(range(n)) if qb == 0 else [0] + list(range(4 * qb, n)) + list(range(1, 4 * qb))
                    for oi, It in enumerate(order):
                        r = It - 4 * qb
                        c0 = max(r, 0) * 128
                        b = cnt % 3
                        is_first = (oi == 0)
                        is_last = (oi == n - 1)
                        a3 = cnt % 4
                        cnt += 1
                        ks = slice(It * 128, (It + 1) * 128)
                        qs = slice(qb * 512 + c0, (qb + 1) * 512)
                        cs = slice(c0, 512)
                        P.op("tensor", I("matmul", out=bank[b][:, cs], lhsT=ka[s][:, ks], rhs=qa[s][:, qs], start=True, stop=True),
                             reads=[B_ka[s][It // 4], B_ka[s][4], B_qa[s][qb], B_qa[s][4]], writes=[B_bank[b]])
                        P.op("scalar", I("activation", out=At[a3][:, cs], in_=bank[b][:, cs], func=AF.Exp, bias=cposT[:, It, h:h + 1], scale=1.0),
                             reads=[B_bank[b], B_cposT], writes=[B_At[a3]])
                        if r >= 0:
                            P.op("gpsimd", I("affine_select", out=At[a3][:, cs], in_=At[a3][:, cs], pattern=[[1, 512 - c0]], compare_op=ALU.is_ge,
                                             fill=0.0, base=0, channel_multiplier=-1), reads=[], writes=[B_At[a3]])

                        def pv(It=It, a3=a3, cs=cs, ob=ob, n=n, qb=qb, s=s, psl=psl, osl=osl, vs=vs, is_first=is_first, is_last=is_last):
                            P.op("tensor", I("matmul", out=bank[ob][:, cs], lhsT=vtp[s][:, It, :], rhs=At[a3][:, cs], start=is_first, stop=is_last),
                                 reads=[B_vtp[s][It // 4], B_At[a3]], writes=[B_bank[ob]])
                            if is_last:
                                u = qb % 2
                                P.op("vector", I("reciprocal", out=rden[u][psl, :], in_=bank[ob][osl, :]), reads=[B_bank[ob]], writes=[B_rden[u]])
                                P.op("vector", I("tensor_tensor", out=oTp[vs][psl, qb * 512:(qb + 1) * 512], in0=bank[ob][psl, :], in1=rden[u][psl, :], op=ALU.mult),
                                     reads=[B_bank[ob], B_rden[u]], writes=[B_oTp[vs][qb]])

                        if len(pend) >= 2:
                            pend.pop(0)()
                        pend.append(pv)
                        tix += 1
                        if tix % 4 == 0:
                            next(work, None)
                for _ in work:
                    pass
            flush()

            def outproj(vs=vs, wo_=wo_, B_wo_=B_wo_):
                for m in range(8):
                    for blk in range(4):
                        b = 6 + (m * 4 + blk) % 2
                        tsl = slice(blk * 512, (blk + 1) * 512)
                        P.op("tensor", I("matmul", out=bank[b][:, :], lhsT=wo_[:, m * 128:(m + 1) * 128], rhs=oTp[vs][:, tsl], start=True, stop=True),
                             reads=[B_wo_, B_oTp[vs][blk]], writes=[B_bank[b]])
                        xs = self.xT[:, m, tsl]
                        P.op("vector", I("tensor_tensor", out=xs, in0=bank[b][:, :], in1=xs, op=ALU.add), reads=[B_bank[b]], writes=[self.B_x[m][blk]])
                    yield

            for _ in outproj():
                pass
        for g_ in outq:
            for _ in g_:
                pass

    def build(self):
        nc, P = self.nc, self.P
        ns = self.nseq
        dt_in = lambda name, shape: nc.dram_tensor(name, shape, F32, kind="ExternalInput").ap()
        self.xT_d = dt_in("xT", [ns, D, T])
        self.cst_d = dt_in("cst", [128, NCST])
        self.win_d = dt_in("win", [8 * NJ, 128, 8 * 256])
        self.wout_d = dt_in("wout", [8 * 8, 128, NJ * 128])
        self.abin_d = dt_in("abin", [2 * 20, 128, 8 * 128])
        self.about_d = dt_in("about", [2 * 8, 128, 8 * 128])
        self.fxqk_d = dt_in("fxqk", [2 * 3 * 16, 128, 8 * 64])
        self.fxf_d = dt_in("fxf", [2, 128, 8 * 16])
        self.fxout_d = dt_in("fxout", [2 * 8, 128, D])
        self.yT_d = nc.dram_tensor("yT", [ns, D, T], F32, kind="ExternalOutput").ap()
        with contextlib.ExitStack() as es:
            sb = lambda name, shape, dt: es.enter_context(nc.sbuf_tensor(name, shape, dt))
            self.xT = sb("xTs", [128, 8, T], F32)
            self.cst = sb("csts", [128, NCST], F32)
            self.ones_d = sb("ones_d", [128, 128], BF16)
            self.ones_ln = sb("ones_ln", [128, 128], F32)
            self.ones64bd = sb("ones64bd", [128, 128], BF16)
            self.qkscale = sb("qkscale", [128, 1], F32)
            self.qkbias = sb("qkbias", [128, 1], F32)
            self.ones_den = sb("ones_den", [128, 128], BF16)
            self.negones = sb("negones", [128, 128], BF16)
            self.negtri = sb("negtri", [128, 128], BF16)
            self.ident = sb("ident", [128, 128], F32)
            self.onesf = sb("onesf", [128, 128], F32)
            self.epsc = sb("epsc", [128, 1], F32)
            self.eps64c = sb("eps64c", [128, 1], F32)
            self.onec = sb("onec", [128, 1], F32)
            self.msb = sb("msb", [128, 4, 512], BF16)
            self.onesb = sb("onesb", [128, 512], BF16)
            SCRW = 33400
            scr = sb("scr", [128, SCRW], F32)
            self.arena = Arena(scr, SCRW)
            self.bank = [es.enter_context(nc.psum_tensor(f"bank{i}", [128, 512], F32)) for i in range(8)]
            self.B_bank = [Buf() for _ in range(8)]
            self.B_x = [[Buf() for _ in range(4)] for _ in range(8)]
            self.B_c = Buf()
            B_c = self.B_c
            P.dma("sync", I("dma_start", out=self.cst[:, :], in_=self.cst_d[:, :]), writes=[B_c], key="d_cst")
            g = "gpsimd"
            P.op(g, I("memset", ap=self.ones_d[:, :], constant=1.0 / D), writes=[B_c])
            P.op(g, I("memset", ap=self.ones_ln[:, :], constant=1.0 / 512), writes=[B_c])
            P.op(g, I("memset", ap=self.ones64bd[:, :], constant=0.0), writes=[B_c])
            P.op(g, I("memset", ap=self.ones64bd[0:64, 0:64], constant=1.0 / 64), writes=[B_c])
            P.op(g, I("memset", ap=self.ones64bd[64:128, 64:128], constant=1.0 / 64), writes=[B_c])
            P.op(g, I("memset", ap=self.qkscale[0:64, :], constant=64.0), writes=[B_c])
            P.op(g, I("memset", ap=self.qkscale[64:128, :], constant=1.0), writes=[B_c])
            P.op(g, I("memset", ap=self.qkbias[0:64, :], constant=64.0 * EPS), writes=[B_c])
            P.op(g, I("memset", ap=self.qkbias[64:128, :], constant=EPS), writes=[B_c])
            P.op(g, I("memset", ap=self.ones_den[:, :], constant=1.0), writes=[B_c])
            P.op(g, I("memset", ap=self.negones[:, :], constant=-1.0), writes=[B_c])
            P.op(g, I("memset", ap=self.onesf[:, :], constant=1.0), writes=[B_c])
            P.op(g, I("memset", ap=self.onesb[:, :], constant=1.0), writes=[B_c])
            P.op(g, I("memset", ap=self.epsc[:, :], constant=EPS), writes=[B_c])
            P.op(g, I("memset", ap=self.eps64c[:, :], constant=64.0 * EPS), writes=[B_c])
            P.op(g, I("memset", ap=self.onec[:, :], constant=1.0), writes=[B_c])
            P.op(g, I("affine_select", out=self.negtri[:, :], in_=self.negones[:, :], pattern=[[-1, 128]], compare_op=ALU.is_ge, fill=0.0,
                      base=0, channel_multiplier=1), writes=[B_c])
            P.op(g, I("affine_select", out=self.ident[:, :], in_=self.onesf[:, :], pattern=[[-1, 128]], compare_op=ALU.is_equal, fill=0.0,
                      base=0, channel_multiplier=1), writes=[B_c])
            for r in range(4):
                P.op(g, I("affine_select", out=self.msb[:, r, :], in_=self.onesb[:, :], pattern=[[1, 512]], compare_op=ALU.is_gt, fill=0.0,
                          base=-r * 128, channel_multiplier=-1), writes=[B_c])
            B_out = Buf()
            for sq_ in range(ns):
                xv = self.xT_d[sq_].rearrange("(kc p) t -> p kc t", p=128)
                for kc in range(8):
                    P.dma("sync", I("dma_start", out=self.xT[:, kc, :], in_=xv[:, kc, :]), writes=self.B_x[kc], key=f"d_x{kc}")
                for L in self.layers:
                    if "f1" in self.parts:
                        self.ffn(L, 0)
                    if "mix" in self.parts:
                        if L % 2 == 0:
                            self.mixer_even(L)
                        else:
                            self.mixer_odd(L)
                    if "f2" in self.parts:
                        self.ffn(L, 1)
                yv = self.yT_d[sq_].rearrange("(kc p) t -> p kc t", p=128)
                for kc in range(8):
                    P.dma("sync", I("dma_start", out=yv[:, kc, :], in_=self.xT[:, kc, :]), reads=self.B_x[kc], writes=[B_out], key="d_out")
            P.final_wait("sync", [B_out])
            sems = {k: es.enter_context(nc.semaphore(k)) for k in P.cnt.keys()}
            with nc.Block() as block:
                @block.sync
                def _(e):
                    P.replay("sync", e, sems)

                @block.tensor
                def _(e):
                    P.replay("tensor", e, sems)

                @block.scalar
                def _(e):
                    P.replay("scalar", e, sems)

                @block.vector
                def _(e):
                    P.replay("vector", e, sems)

                @block.gpsimd
                def _(e):
                    P.replay("gpsimd", e, sems)
        return nc


def chunk_k(W, c0, w):
    return np.ascontiguousarray(W[:, c0:c0 + w].reshape(8, 128, w).transpose(1, 0, 2)).reshape(128, 8 * w)


def pack_weights(inp):
    f32 = np.float32
    g = lambda k: np.asarray(inp[k], dtype=f32)
    win = np.empty((8 * NJ, 128, 2048), f32)
    wout = np.empty((64, 128, NJ * 128), f32)
    for L in range(DEPTH):
        for wh, (ki, ko) in enumerate((("ffn1_w_in", "ffn1_w_out"), ("ffn2_w_in", "ffn2_w_out"))):
            Wi = g(ki)[L]
            Wo = g(ko)[L]
            fi = L * 2 + wh
            gg = Wi[:, :DFF].reshape(8, 128, NJ, 128)
            uu = Wi[:, DFF:].reshape(8, 128, NJ, 128)
            cat = np.concatenate([gg, uu], axis=3)
            win[fi * NJ:(fi + 1) * NJ] = cat.transpose(2, 1, 0, 3).reshape(NJ, 128, 2048)
            wout[fi * 8:(fi + 1) * 8] = Wo.reshape(NJ, 128, 8, 128).transpose(2, 1, 0, 3).reshape(8, 128, NJ * 128)
    abin = np.empty((40, 128, 1024), f32)
    about = np.empty((16, 128, 1024), f32)
    for e in range(2):
        W = g("ab_w_in")[e]
        for cc in range(20):
            abin[e * 20 + cc] = chunk_k(W, cc * 128, 128)
        Wo = g("ab_w_out")[e]
        for m in range(8):
            about[e * 8 + m] = chunk_k(Wo, m * 128, 128)
    fxqk = np.empty((96, 128, 512), f32)
    fxf = np.empty((2, 128, 128), f32)
    fxout = np.empty((16, 128, D), f32)
    for o in range(2):
        W = g("fox_w_in")[o]
        for wh in range(3):
            for h in range(16):
                fxqk[(o * 3 + wh) * 16 + h] = chunk_k(W, wh * 1024 + h * 64, 64)
        fxf[o] = chunk_k(W, 3072, 16)
        fxout[o * 8:(o + 1) * 8] = g("fox_w_out")[o].reshape(8, 128, D)
    cst = np.zeros((128, NCST), f32)
    col8 = lambda v: v.reshape(8, 128).T
    for L in range(DEPTH):
        cst[:, CO[("n1", L)]:CO[("n1", L)] + 8] = col8(g("ffn1_norm")[L])
        cst[:, CO[("nm", L)]:CO[("nm", L)] + 8] = col8(g("mix_norm")[L])
        cst[:, CO[("n2", L)]:CO[("n2", L)] + 8] = col8(g("ffn2_norm")[L])
    for e in range(2):
        cw = g("conv_w")[e]
        c0 = CO[("cw", e)]
        cst[:, c0:c0 + 4 * CW] = cw.reshape(CW, 4, 128).transpose(2, 1, 0).reshape(128, 4 * CW)
        for nm, key in (("cb", "conv_b"), ("lg", "conv_ln_g"), ("lb", "conv_ln_b")):
            cst[:, CO[(nm, e)]:CO[(nm, e)] + 4] = g(key)[e].reshape(4, 128).T
    for o in range(2):
        cst[:, CO[("gq", o)]] = np.tile(g("fox_q_norm")[o], 2)
        cst[:, CO[("gk", o)]] = np.tile(g("fox_k_norm")[o], 2)
        cst[0:16, CO[("fb", o)]] = g("fox_f_bias")[o]
    return dict(cst=cst, win=win, wout=wout, abin=abin, about=about, fxqk=fxqk, fxf=fxf, fxout=fxout)


def kernel(**inputs):
    x = np.asarray(inputs["x"], dtype=np.float32)
    wts = pack_weights(inputs)
    mk = MK(nseq=2)
    nc = mk.build()
    in_maps = []
    for c in range(NCORES):
        m = dict(wts)
        m["xT"] = np.ascontiguousarray(x[2 * c:2 * c + 2].transpose(0, 2, 1))
        in_maps.append(m)
    res = run_bass_kernel_spmd(nc, in_maps, core_ids=list(range(NCORES)))
    out = np.empty_like(x)
    for c in range(NCORES):
        out[2 * c:2 * c + 2] = np.asarray(res.results[c]["yT"]).transpose(0, 2, 1)
    return out
```

```python
import contextlib
import numpy as np
import concourse.bass as bass
import concourse.mybir as mybir
from concourse.bass_utils import run_bass_kernel_spmd

F32 = mybir.dt.float32
BF16 = mybir.dt.bfloat16
AF = mybir.ActivationFunctionType
ALU = mybir.AluOpType
ENGS = ["sync", "tensor", "scalar", "vector", "gpsimd"]

D = 1024
T = 2048
DFF = 2816
NJ = DFF // 128
DEPTH = 4
CW = 31
EPS = 1e-6
NCORES = 8


class Ev:
    __slots__ = ("key", "val")

    def __init__(self, key, val):
        self.key = key
        self.val = val


class Buf:
    __slots__ = ("w", "r")

    def __init__(self):
        self.w = None
        self.r = {}


def I(meth, **kw):
    return (meth, kw)


class Prog:
    def __init__(self):
        self.q = {e: [] for e in ENGS}
        self.cnt = {}
        self.seen = {e: {} for e in ENGS}
        self.nops = {e: 0 for e in ENGS}

    def _wait(self, eng, key, val):
        if self.seen[eng].get(key, 0) >= val:
            return
        self.seen[eng][key] = val
        self.q[eng].append(("wait", key, val))

    def _waits(self, eng, reads, writes):
        for b in reads:
            if b.w is not None:
                self._wait(eng, b.w.key, b.w.val)
        for b in writes:
            if b.w is not None:
                self._wait(eng, b.w.key, b.w.val)
            for ev in b.r.values():
                self._wait(eng, ev.key, ev.val)

    def _mark(self, ev, reads, writes):
        for b in reads:
            o = b.r.get(ev.key)
            if o is None or o.val < ev.val:
                b.r[ev.key] = ev
        for b in writes:
            b.w = ev
            b.r = {}

    def op(self, eng, ins, reads=(), writes=(), key=None, inc=1):
        key = key or eng
        self._waits(eng, reads, writes)
        self.cnt[key] = self.cnt.get(key, 0) + inc
        ev = Ev(key, self.cnt[key])
        self.q[eng].append(("op", ins, key, inc))
        self._mark(ev, reads, writes)
        self.nops[eng] += 1
        return ev

    def group(self, eng, inss, reads=(), writes=(), key=None):
        key = key or eng
        self._waits(eng, reads, writes)
        self.cnt[key] = self.cnt.get(key, 0) + 1
        ev = Ev(key, self.cnt[key])
        for f in inss[:-1]:
            self.q[eng].append(("op", f, None, 0))
        self.q[eng].append(("op", inss[-1], key, 1))
        self._mark(ev, reads, writes)
        self.nops[eng] += len(inss)
        return ev

    def dma(self, eng, ins, reads=(), writes=(), key=None):
        return self.op(eng, ins, reads, writes, key=key, inc=16)

    def barrier(self):
        for e in ENGS:
            for k, v in self.cnt.items():
                self._wait(e, k, v)

    def final_wait(self, eng, bufs):
        self._waits(eng, bufs, ())

    def replay(self, eng_name, eng, sems):
        pend = []
        for it in self.q[eng_name]:
            if it[0] == "wait":
                pend.append(it)
            else:
                for w in pend[:-1]:
                    eng.wait_ge(sems[w[1]], w[2])
                ins = getattr(eng, it[1][0])(**it[1][1])
                if pend:
                    ins._wait_ge(sems[pend[-1][1]], pend[-1][2])
                pend = []
                if it[2] is not None:
                    ins.then_inc(sems[it[2]], it[3])
        for w in pend:
            eng.wait_ge(sems[w[1]], w[2])


class Arena:
    def __init__(self, tile, nwords):
        self.tile = tile
        self.n = nwords
        self.off = 0

    def reset(self):
        self.off = 0

    def alloc(self, shape, dt):
        n = int(np.prod(shape))
        nw = n if dt == F32 else (n + 1) // 2
        nw = (nw + 7) // 8 * 8
        assert self.off + nw <= self.n, f"arena overflow {self.off}+{nw}>{self.n}"
        v = self.tile[:, self.off:self.off + nw]
        self.off += nw
        if dt != F32:
            v = v.bitcast(dt)
        v = v[:, 0:n]
        if len(shape) == 2:
            v = v.rearrange("p (a b) -> p a b", b=shape[1])
        elif len(shape) == 3:
            v = v.rearrange("p (a b c) -> p a b c", b=shape[1], c=shape[2])
        return v


def cst_layout():
    off = {}
    c = 0
    for L in range(DEPTH):
        for nm in ("n1", "nm", "n2"):
            off[(nm, L)] = c
            c += 8
    for e in range(2):
        off[("cw", e)] = c
        c += 4 * CW
        for nm in ("cb", "lg", "lb"):
            off[(nm, e)] = c
            c += 4
    for o in range(2):
        for nm in ("gq", "gk", "fb"):
            off[(nm, o)] = c
            c += 1
    return off, c


CO, NCST = cst_layout()


class MK:
    def __init__(self, nseq=2, layers=(0, 1, 2, 3), parts=("f1", "mix", "f2")):
        self.nseq = nseq
        self.layers = layers
        self.parts = parts
        self.nc = bass.Bass("TRN2", target_bir_lowering=False)
        self.P = Prog()
        self.slotn = {}

    def wslot(self, name, nslots):
        n = self.slotn.get(name, 0)
        self.slotn[name] = n + 1
        return n % nslots

    def load_w(self, name, tiles, bufs, src, maxlast=None):
        s = self.wslot(name, len(tiles))
        t = tiles[s]
        nd = len(t.shape)
        if nd == 3:
            o = t.rearrange("p a b -> p (a b)")
        else:
            o = t
        kw = dict(out=o, in_=src)
        if maxlast:
            kw["max_dma_last_dim"] = maxlast
        self.P.dma("gpsimd", I("dma_start", **kw), writes=[bufs[s]], key=f"d_{name}{s}")
        return t, bufs[s]

    def norm_block(self, blk, gcol, dst, dst_sl, B_dst):
        P = self.P
        s = self.wslot("nrm", 2)
        sq, B_sq = self.sq[s], self.B_sq[s]
        rstd, B_rstd = self.rstd[s], self.B_rstd[s]
        pn, B_pn = self.bank[6], self.B_bank[6]
        t0 = blk * 512
        for kc in range(8):
            P.op("scalar", I("activation", out=sq[:, kc, :], in_=self.xT[:, kc, t0:t0 + 512], func=AF.Square),
                 reads=[self.B_x[kc][blk]], writes=[B_sq])
        P.group("tensor", [I("matmul", out=pn[:, :], lhsT=self.ones_d[:, :], rhs=sq[:, kc, :], start=(kc == 0), stop=(kc == 7))
                           for kc in range(8)], reads=[B_sq, self.B_c], writes=[B_pn])
        P.op("scalar", I("activation", out=rstd[:, :], in_=pn[:, :], func=AF.Sqrt, bias=self.epsc[:, 0:1], scale=1.0),
             reads=[B_pn, self.B_c], writes=[B_rstd])
        P.op("vector", I("reciprocal", out=rstd[:, :], in_=rstd[:, :]), reads=[B_rstd], writes=[B_rstd])
        for kc in range(8):
            P.op("vector", I("scalar_tensor_tensor", out=dst[:, kc, dst_sl], in0=self.xT[:, kc, t0:t0 + 512],
                             scalar=self.cst[:, gcol + kc:gcol + kc + 1], in1=rstd[:, :], op0=ALU.mult, op1=ALU.mult),
                 reads=[self.B_x[kc][blk], B_rstd, self.B_c], writes=[B_dst[kc]])

    def ffn(self, L, which):
        P = self.P
        A = self.arena
        P.barrier()
        A.reset()
        gcol = CO[("n1" if which == 0 else "n2", L)]
        fidx = L * 2 + which
        hT = A.alloc([8, 1024], BF16)
        aT = A.alloc([NJ, 1024], BF16)
        self.sq = [A.alloc([8, 512], BF16) for _ in range(2)]
        self.rstd = [A.alloc([512], F32) for _ in range(2)]
        self.B_sq = [Buf(), Buf()]
        self.B_rstd = [Buf(), Buf()]
        sg = [A.alloc([512], F32) for _ in range(2)]
        wi = [A.alloc([8, 256], BF16) for _ in range(3)]
        wo = [A.alloc([NJ, 128], BF16) for _ in range(2)]
        B_sg = [Buf(), Buf()]
        B_wi = [Buf() for _ in range(3)]
        B_wo = [Buf() for _ in range(2)]
        bank, B_bank = self.bank, self.B_bank
        for tb in range(2):
            B_h = [[Buf() for _ in range(2)] for _ in range(8)]
            B_a = [[Buf() for _ in range(2)] for _ in range(NJ)]
            for sub in range(2):
                self.norm_block(tb * 2 + sub, gcol, hT, slice(sub * 512, (sub + 1) * 512), [B_h[kc][sub] for kc in range(8)])
            for j in range(NJ):
                w, B_w = self.load_w("wi", wi, B_wi, self.win_d[fidx * NJ + j, :, :])
                for sub in range(2):
                    b = (j * 2 + sub) % 2
                    tsl = slice(sub * 512, (sub + 1) * 512)
                    hreads = [B_w] + [B_h[kc][sub] for kc in range(8)]
                    P.group("tensor", [I("matmul", out=bank[b][:, :], lhsT=w[:, kc, 0:128], rhs=hT[:, kc, tsl], start=(kc == 0), stop=(kc == 7))
                                       for kc in range(8)], reads=hreads, writes=[B_bank[b]])
                    P.group("tensor", [I("matmul", out=bank[2 + b][:, :], lhsT=w[:, kc, 128:256], rhs=hT[:, kc, tsl], start=(kc == 0), stop=(kc == 7))
                                       for kc in range(8)], reads=hreads, writes=[B_bank[2 + b]])
                    P.op("scalar", I("activation", out=sg[b][:, :], in_=bank[b][:, :], func=AF.Silu),
                         reads=[B_bank[b]], writes=[B_sg[b]])
                    P.op("vector", I("tensor_tensor", out=aT[:, j, tsl], in0=sg[b][:, :], in1=bank[2 + b][:, :], op=ALU.mult),
                         reads=[B_sg[b], B_bank[2 + b]], writes=[B_a[j][sub]])
            for m in range(8):
                w, B_w = self.load_w("wo", wo, B_wo, self.wout_d[fidx * 8 + m, :, :], maxlast=4096)
                for sub in range(2):
                    blk = tb * 2 + sub
                    b = 4 + (m * 2 + sub) % 2
                    tsl = slice(sub * 512, (sub + 1) * 512)
                    P.group("tensor", [I("matmul", out=bank[b][:, :], lhsT=w[:, kc, :], rhs=aT[:, kc, tsl], start=(kc == 0), stop=(kc == NJ - 1))
                                       for kc in range(NJ)], reads=[B_w] + [B_a[kc][sub] for kc in range(NJ)], writes=[B_bank[b]])
                    xs = self.xT[:, m, blk * 512:(blk + 1) * 512]
                    P.op("vector", I("scalar_tensor_tensor", out=xs, in0=bank[b][:, :], scalar=0.5, in1=xs, op0=ALU.mult, op1=ALU.add),
                         reads=[B_bank[b]], writes=[self.B_x[m][blk]])

    def mixer_even(self, L):
        P = self.P
        A = self.arena
        e = L // 2
        P.barrier()
        A.reset()
        bank, B_bank = self.bank, self.B_bank
        cst = self.cst
        TP = T + CW - 1
        wc = [A.alloc([8, 128], BF16) for _ in range(4)]
        B_wc = [Buf() for _ in range(4)]
        uraw = A.alloc([4 * TP], F32)
        upad = uraw.rearrange("p (c t) -> p c t", c=4)
        ubf = uraw.bitcast(BF16).rearrange("p (c t) -> p c t", c=4)

        def ufin(c, blk):
            o = 2 * (CW - 1 + blk * 512)
            return ubf[:, c, o:o + 512]

        B_up = [[Buf() for _ in range(4)] for _ in range(4)]
        B_pad = [Buf() for _ in range(4)]
        acc = [A.alloc([1024], F32) for _ in range(2)]
        B_acc = [Buf(), Buf()]
        hT = A.alloc([8, T], BF16)
        B_h = [[Buf() for _ in range(4)] for _ in range(8)]
        mark = A.off
        self.sq = [A.alloc([8, 512], BF16) for _ in range(2)]
        self.rstd = [A.alloc([512], F32) for _ in range(2)]
        self.B_sq = [Buf(), Buf()]
        self.B_rstd = [Buf(), Buf()]
        tmp = [A.alloc([512], F32) for _ in range(2)]
        B_tmp = [Buf(), Buf()]
        for blk in range(4):
            self.norm_block(blk, CO[("nm", L)], hT, slice(blk * 512, (blk + 1) * 512), [B_h[kc][blk] for kc in range(8)])
        hall = lambda blk: [B_h[kc][blk] for kc in range(8)]

        for c in range(4):
            P.op("gpsimd", I("memset", ap=upad[:, c, 0:CW - 1], constant=0.0), writes=[B_pad[c]])
        for c in range(4):
            wv, B_wv = self.load_w("wc", wc, B_wc, self.abin_d[e * 20 + c, :, :])
            wg, B_wg = self.load_w("wc", wc, B_wc, self.abin_d[e * 20 + 4 + c, :, :])
            for blk in range(4):
                b = blk % 2
                tsl = slice(blk * 512, (blk + 1) * 512)
                P.group("tensor", [I("matmul", out=bank[b][:, :], lhsT=wv[:, kc, :], rhs=hT[:, kc, tsl], start=(kc == 0), stop=(kc == 7))
                                   for kc in range(8)], reads=[B_wv] + hall(blk), writes=[B_bank[b]])
                P.group("tensor", [I("matmul", out=bank[2 + b][:, :], lhsT=wg[:, kc, :], rhs=hT[:, kc, tsl], start=(kc == 0), stop=(kc == 7))
                                   for kc in range(8)], reads=[B_wg] + hall(blk), writes=[B_bank[2 + b]])
                P.op("scalar", I("activation", out=tmp[b][:, :], in_=bank[2 + b][:, :], func=AF.Sigmoid),
                     reads=[B_bank[2 + b]], writes=[B_tmp[b]])
                P.op("vector", I("tensor_tensor", out=upad[:, c, CW - 1 + blk * 512:CW - 1 + (blk + 1) * 512], in0=tmp[b][:, :], in1=bank[b][:, :], op=ALU.mult),
                     reads=[B_tmp[b], B_bank[b]], writes=[B_up[c][blk]])

        cw0 = CO[("cw", e)]
        cb0 = CO[("cb", e)]

        def conv_items():
            i = 0
            for hb in (1, 0):
                base = hb * 1024
                for c in range(4):
                    a, B_a = acc[i % 2], B_acc[i % 2]
                    i += 1
                    rd = [B_up[c][2 * hb], B_up[c][2 * hb + 1], B_up[c][2 * hb - 1] if hb > 0 else B_pad[c], self.B_c]
                    wcol = lambda k: cst[:, cw0 + c * CW + k:cw0 + c * CW + k + 1]
                    P.op("vector", I("tensor_scalar", out=a[:, :], in0=upad[:, c, base:base + 1024], scalar1=wcol(0),
                                     scalar2=cst[:, cb0 + c:cb0 + c + 1], op0=ALU.mult, op1=ALU.add), reads=rd, writes=[B_a])
                    yield
                    for k in range(1, CW - 1):
                        P.op("vector", I("scalar_tensor_tensor", out=a[:, :], in0=upad[:, c, base + k:base + k + 1024], scalar=wcol(k),
                                         in1=a[:, :], op0=ALU.mult, op1=ALU.add), reads=rd, writes=[B_a])
                        yield
                    k = CW - 1
                    home = upad[:, c, base + k:base + k + 1024]
                    P.op("vector", I("scalar_tensor_tensor", out=home, in0=home, scalar=wcol(k), in1=a[:, :], op0=ALU.mult, op1=ALU.add),
                         reads=[B_a, self.B_c], writes=[B_up[c][2 * hb], B_up[c][2 * hb + 1]])
                    yield

        conv = conv_items()

        P.barrier()
        A.off = mark
        vtp = [A.alloc([16, 128], BF16) for _ in range(1)]
        B_vtp = [[Buf() for _ in range(4)] for _ in range(1)]
        oT = A.alloc([4, T], BF16)
        B_o = [[Buf() for _ in range(4)] for _ in range(4)]
        qT = [A.alloc([T], BF16) for _ in range(1)]
        kT = [A.alloc([T], BF16) for _ in range(1)]
        B_q = [[Buf() for _ in range(4)] for _ in range(1)]
        B_k = [[Buf() for _ in range(4)] for _ in range(1)]
        mark2 = A.off
        NS = 4
        et = [A.alloc([512], F32) for _ in range(NS)]
        spb = [A.alloc([512], BF16) for _ in range(NS)]
        ssf = A.alloc([512], F32)
        ssb = [A.alloc([512], BF16) for _ in range(2)]
        At = [A.alloc([512], BF16) for _ in range(3)]
        B_et = [Buf() for _ in range(NS)]
        B_spb = [Buf() for _ in range(NS)]
        B_ssf = Buf()
        B_ssb = [Buf(), Buf()]
        B_At = [Buf(), Buf(), Buf()]
        st = dict(ss_cur=0, acnt=0)
        s2q = []
        s3q = []

        def advance(depth=2):
            old3 = s3q.pop(0) if s3q else None
            if len(s2q) >= depth:
                s2q.pop(0)()
            if old3:
                old3()

        def flush_all():
            while s2q or s3q:
                advance(1)

        cnt = 0
        ocnt = 0
        for pr in range(4):
            ps_ = 0
            flush_all()
            wq, B_wq = self.load_w("wc", wc, B_wc, self.abin_d[e * 20 + 8 + pr, :, :])
            wk, B_wk = self.load_w("wc", wc, B_wc, self.abin_d[e * 20 + 12 + pr, :, :])
            wv_, B_wv_ = self.load_w("wc", wc, B_wc, self.abin_d[e * 20 + 16 + pr, :, :])
            for g_ in range(4):
                vb = 6 + g_ % 2
                P.group("tensor", [I("matmul", out=bank[vb][:, i * 128:(i + 1) * 128], lhsT=hT[:, kc, (g_ * 4 + i) * 128:(g_ * 4 + i + 1) * 128], rhs=wv_[:, kc, :],
                                     start=(kc == 0), stop=(kc == 7)) for i in range(4) for kc in range(8)],
                        reads=[B_wv_] + hall(g_), writes=[B_bank[vb]])
                P.op("scalar", I("activation", out=vtp[ps_][:, g_ * 4:(g_ + 1) * 4, :], in_=bank[vb][:, :].rearrange("p (a b) -> p a b", b=128), func=AF.Copy),
                     reads=[B_bank[vb]], writes=[B_vtp[ps_][g_]])
            for blk in range(4):
                tsl = slice(blk * 512, (blk + 1) * 512)
                P.group("tensor", [I("matmul", out=bank[6][:, :], lhsT=wq[:, kc, :], rhs=hT[:, kc, tsl], start=(kc == 0), stop=(kc == 7))
                                   for kc in range(8)], reads=[B_wq] + hall(blk), writes=[B_bank[6]])
                P.op("scalar", I("activation", out=qT[ps_][:, tsl], in_=bank[6][:, :], func=AF.Copy, scale=0.125),
                     reads=[B_bank[6]], writes=[B_q[ps_][blk]])
                P.group("tensor", [I("matmul", out=bank[7][:, :], lhsT=wk[:, kc, :], rhs=hT[:, kc, tsl], start=(kc == 0), stop=(kc == 7))
                                   for kc in range(8)], reads=[B_wk] + hall(blk), writes=[B_bank[7]])
                P.op("scalar", I("activation", out=kT[ps_][:, tsl], in_=bank[7][:, :], func=AF.Copy), reads=[B_bank[7]], writes=[B_k[ps_][blk]])
            for half in range(2):
                psl = slice(half * 64, half * 64 + 64)
                for qb in range(4):
                    qs = slice(qb * 512, (qb + 1) * 512)
                    n = 4 * qb + 4
                    ob = 4 + (ocnt % 2)
                    ocnt += 1
                    for It in reversed(range(n)):
                        r = It - 4 * qb
                        b = cnt % 4
                        e4 = cnt % NS
                        cnt += 1
                        first = (It == n - 1)
                        ks = slice(It * 128, (It + 1) * 128)
                        qk = dict(lhsT=kT[ps_][psl, ks], rhs=qT[ps_][psl, qs])
                        qkr = [B_k[ps_][It // 4], B_q[ps_][qb]]
                        P.op("tensor", I("matmul", out=bank[b][:, :], start=True, stop=True, **qk), reads=qkr, writes=[B_bank[b]])
                        P.op("scalar", I("activation", out=et[e4][:, :], in_=bank[b][:, :], func=AF.Exp), reads=[B_bank[b]], writes=[B_et[e4]])
                        P.op("scalar", I("activation", out=spb[e4][:, :], in_=et[e4][:, :], func=AF.Ln, bias=self.onec[:, 0:1], scale=1.0),
                             reads=[B_et[e4], self.B_c], writes=[B_spb[e4]])
                        if r >= 0:
                            P.op("gpsimd", I("affine_select", out=spb[e4][:, :], in_=spb[e4][:, :], pattern=[[1, 512]], compare_op=ALU.is_gt,
                                             fill=0.0, base=-r * 128, channel_multiplier=-1), reads=[], writes=[B_spb[e4]])

                        def s2(It=It, r=r, b=b, e4=e4, first=first, qk=qk, qkr=qkr, ob=ob, n=n, ps_=ps_, psl=psl, pr=pr, qs=qs, qb=qb):
                            cur = st["ss_cur"]
                            a3 = st["acnt"] % 3
                            st["acnt"] += 1
                            mm = [I("matmul", out=bank[b][:, :], lhsT=self.negtri[:, :], rhs=spb[e4][:, :], start=False, stop=first)]
                            rd = [B_spb[e4], self.B_c]
                            if not first:
                                mm.append(I("matmul", out=bank[b][:, :], lhsT=self.negones[:, :], rhs=ssb[cur][:, :], start=False, stop=True))
                                rd.append(B_ssb[cur])
                            P.group("tensor", mm, reads=rd, writes=[B_bank[b]])
                            P.op("scalar", I("activation", out=At[a3][:, :], in_=bank[b][:, :], func=AF.Exp), reads=[B_bank[b]], writes=[B_At[a3]])
                            if r >= 0:
                                P.op("gpsimd", I("affine_select", out=At[a3][:, :], in_=At[a3][:, :], pattern=[[1, 512]], compare_op=ALU.is_gt,
                                                 fill=0.0, base=-r * 128, channel_multiplier=-1), reads=[], writes=[B_At[a3]])
                            if It > 0:
                                sn = 1 - cur
                                if first:
                                    P.op("vector", I("tensor_copy", out=ssb[sn][:, :], in_=spb[e4][:, :]), reads=[B_spb[e4]], writes=[B_ssb[sn]])
                                    P.op("vector", I("tensor_copy", out=ssf[:, :], in_=spb[e4][:, :]), reads=[B_spb[e4]], writes=[B_ssf])
                                else:
                                    P.op("vector", I("tensor_tensor", out=ssb[sn][:, :], in0=ssf[:, :], in1=spb[e4][:, :], op=ALU.add),
                                         reads=[B_ssf, B_spb[e4]], writes=[B_ssb[sn]])
                                    P.op("vector", I("tensor_tensor", out=ssf[:, :], in0=ssf[:, :], in1=spb[e4][:, :], op=ALU.add),
                                         reads=[B_spb[e4]], writes=[B_ssf])
                                st["ss_cur"] = sn

                            def s3():
                                P.op("tensor", I("matmul", out=bank[ob][:, :], lhsT=vtp[ps_][:, It, :], rhs=At[a3][:, :], start=first, stop=(It == 0)),
                                     reads=[B_vtp[ps_][It // 4], B_At[a3]], writes=[B_bank[ob]])
                                if It == 0:
                                    P.op("scalar", I("activation", out=oT[psl, pr, qs], in_=bank[ob][psl, :], func=AF.Copy), reads=[B_bank[ob]], writes=[B_o[pr][qb]])
                            s3q.append(s3)

                        s2q.append(s2)
                        advance(3)
                        next(conv, None)
        flush_all()
        for _ in conv:
            pass

        P.barrier()
        A.off = mark2
        ysq = A.alloc([4, 512], F32)
        B_ysq = Buf()
        m2 = A.alloc([512], F32)
        var = A.alloc([512], F32)
        B_m2 = Buf()
        B_var = Buf()
        t1 = [A.alloc([512], F32) for _ in range(2)]
        B_t1 = [Buf(), Buf()]
        lg0, lb0 = CO[("lg", e)], CO[("lb", e)]
        for blk in range(4):
            ysl = slice(CW - 1 + blk * 512, CW - 1 + (blk + 1) * 512)
            B_yb = [B_up[c][blk] for c in range(4)]
            for c in range(4):
                P.op("scalar", I("activation", out=ysq[:, c, :], in_=upad[:, c, ysl], func=AF.Square), reads=[B_yb[c]], writes=[B_ysq])
            P.group("tensor", [I("matmul", out=bank[0][:, :], lhsT=self.ones_ln[:, :], rhs=upad[:, c, ysl], start=(c == 0), stop=(c == 3))
                               for c in range(4)], reads=B_yb + [self.B_c], writes=[B_bank[0]])
            P.group("tensor", [I("matmul", out=bank[1][:, :], lhsT=self.ones_ln[:, :], rhs=ysq[:, c, :], start=(c == 0), stop=(c == 3))
                               for c in range(4)], reads=[B_ysq, self.B_c], writes=[B_bank[1]])
            P.op("scalar", I("activation", out=m2[:, :], in_=bank[0][:, :], func=AF.Square), reads=[B_bank[0]], writes=[B_m2])
            P.op("vector", I("scalar_tensor_tensor", out=var[:, :], in0=m2[:, :], scalar=-1.0, in1=bank[1][:, :], op0=ALU.mult, op1=ALU.add),
                 reads=[B_m2, B_bank[1]], writes=[B_var])
            P.op("vector", I("tensor_scalar", out=var[:, :], in0=var[:, :], scalar1=0.0, scalar2=None, op0=ALU.max), reads=[B_var], writes=[B_var])
            P.op("scalar", I("activation", out=var[:, :], in_=var[:, :], func=AF.Sqrt, bias=self.epsc[:, 0:1], scale=1.0),
                 reads=[B_var, self.B_c], writes=[B_var])
            P.op("vector", I("reciprocal", out=var[:, :], in_=var[:, :]), reads=[B_var], writes=[B_var])
            for c in range(4):
                b = c % 2
                P.op("vector", I("tensor_tensor", out=t1[b][:, :], in0=upad[:, c, ysl], in1=bank[0][:, :], op=ALU.subtract),
                     reads=[B_yb[c], B_bank[0]], writes=[B_t1[b]])
                P.op("vector", I("tensor_tensor", out=t1[b][:, :], in0=t1[b][:, :], in1=var[:, :], op=ALU.mult),
                     reads=[B_var], writes=[B_t1[b]])
                P.op("scalar", I("activation", out=ufin(c, blk), in_=t1[b][:, :], func=AF.Silu,
                                 scale=cst[:, lg0 + c:lg0 + c + 1], bias=cst[:, lb0 + c:lb0 + c + 1]),
                     reads=[B_t1[b], self.B_c], writes=[B_yb[c]])

        for m in range(8):
            w, B_w = self.load_w("wc", wc, B_wc, self.about_d[e * 8 + m, :, :])
            for blk in range(4):
                b = 6 + blk % 2
                tsl = slice(blk * 512, (blk + 1) * 512)
                mm = [I("matmul", out=bank[b][:, :], lhsT=w[:, kc, :], rhs=(ufin(kc, blk) if kc < 4 else oT[:, kc - 4, tsl]),
                        start=(kc == 0), stop=(kc == 7)) for kc in range(8)]
                P.group("tensor", mm, reads=[B_w] + [B_up[c][blk] for c in range(4)] + [B_o[p_][blk] for p_ in range(4)], writes=[B_bank[b]])
                xs = self.xT[:, m, tsl]
                P.op("vector", I("tensor_tensor", out=xs, in0=bank[b][:, :], in1=xs, op=ALU.add), reads=[B_bank[b]], writes=[self.B_x[m][blk]])

    def mixer_odd(self, L):
        P = self.P
        A = self.arena
        o = L // 2
        P.barrier()
        A.reset()
        bank, B_bank = self.bank, self.B_bank
        cst = self.cst
        hT = A.alloc([8, T], BF16)
        B_h = [[Buf() for _ in range(4)] for _ in range(8)]
        mark = A.off
        self.sq = [A.alloc([8, 512], BF16) for _ in range(2)]
        self.rstd = [A.alloc([512], F32) for _ in range(2)]
        self.B_sq = [Buf(), Buf()]
        self.B_rstd = [Buf(), Buf()]
        for blk in range(4):
            self.norm_block(blk, CO[("nm", L)], hT, slice(blk * 512, (blk + 1) * 512), [B_h[kc][blk] for kc in range(8)])
        hall = lambda blk: [B_h[kc][blk] for kc in range(8)]
        P.barrier()
        A.off = mark
        wf = A.alloc([8, 16], BF16)
        B_wf = Buf()
        lf = A.alloc([T], F32)
        cpos = A.alloc([T], F32)
        cp3 = A.alloc([3, T], BF16)
        r1 = lf
        cposT = A.alloc([16, 16], F32)
        nfb = A.alloc([1], F32)
        tmpf = A.alloc([512], F32)
        B_lf, B_cpos, B_cp3, B_cposT, B_nfb, B_tmpf = (Buf() for _ in range(6))
        B_r1 = B_lf
        P.dma("gpsimd", I("dma_start", out=wf.rearrange("p a b -> p (a b)"), in_=self.fxf_d[o, :, :]), writes=[B_wf], key="d_wf")
        P.op("vector", I("tensor_scalar", out=nfb[0:16, :], in0=cst[0:16, CO[("fb", o)]:CO[("fb", o)] + 1], scalar1=-1.0, scalar2=None, op0=ALU.mult),
             reads=[self.B_c], writes=[B_nfb])
        for blk in range(4):
            tsl = slice(blk * 512, (blk + 1) * 512)
            P.group("tensor", [I("matmul", out=bank[6][0:16, :], lhsT=wf[:, kc, :], rhs=hT[:, kc, tsl], start=(kc == 0), stop=(kc == 7))
                               for kc in range(8)], reads=[B_wf] + hall(blk), writes=[B_bank[6]])
            P.op("scalar", I("activation", out=tmpf[0:16, :], in_=bank[6][0:16, :], func=AF.Exp, scale=-1.0, bias=nfb[0:16, 0:1]),
                 reads=[B_bank[6], B_nfb], writes=[B_tmpf])
            P.op("scalar", I("activation", out=lf[0:16, tsl], in_=tmpf[0:16, :], func=AF.Ln, bias=self.onec[0:16, 0:1], scale=1.0),
                 reads=[B_tmpf, self.B_c], writes=[B_lf])
        P.op("vector", I("tensor_tensor_scan", out=cpos[0:16, :], data0=lf[0:16, :], data1=lf[0:16, :], initial=0.0, op0=ALU.add, op1=ALU.max),
             reads=[B_lf], writes=[B_cpos])
        P.op("vector", I("tensor_scalar", out=r1[0:16, :], in0=cpos[0:16, :], scalar1=-1.0, scalar2=None, op0=ALU.mult), reads=[B_cpos], writes=[B_r1])
        P.op("vector", I("tensor_copy", out=cp3[0:16, 0, :], in_=r1[0:16, :]), reads=[B_r1], writes=[B_cp3])
        P.op("vector", I("tensor_tensor", out=r1[0:16, :], in0=r1[0:16, :], in1=cp3[0:16, 0, :], op=ALU.subtract), reads=[B_cp3], writes=[B_r1])
        P.op("vector", I("tensor_copy", out=cp3[0:16, 1, :], in_=r1[0:16, :]), reads=[B_r1], writes=[B_cp3])
        P.op("vector", I("tensor_tensor", out=r1[0:16, :], in0=r1[0:16, :], in1=cp3[0:16, 1, :], op=ALU.subtract), reads=[B_cp3], writes=[B_r1])
        P.op("vector", I("tensor_copy", out=cp3[0:16, 2, :], in_=r1[0:16, :]), reads=[B_r1], writes=[B_cp3])
        for It in range(16):
            b = 6 + It % 2
            P.op("tensor", I("transpose", out=bank[b][:, 0:16], in_=cpos[0:16, It * 128:(It + 1) * 128], identity=self.ident[0:16, 0:16]),
                 reads=[B_cpos, self.B_c], writes=[B_bank[b]])
            P.op("vector", I("tensor_copy", out=cposT[:, It, :], in_=bank[b][:, 0:16]), reads=[B_bank[b]], writes=[B_cposT])
        wq4 = [A.alloc([8, 64], BF16) for _ in range(6)]
        B_wq4 = [Buf() for _ in range(6)]
        vtp = [A.alloc([16, 128], BF16) for _ in range(2)]
        B_vtp = [[Buf() for _ in range(4)] for _ in range(2)]
        wout = [A.alloc([D], BF16) for _ in range(3)]
        B_wout = [Buf(), Buf(), Buf()]
        qa = [A.alloc([T], BF16) for _ in range(2)]
        ka = [A.alloc([T], BF16) for _ in range(2)]
        B_qa = [[Buf() for _ in range(5)] for _ in range(2)]
        B_ka = [[Buf() for _ in range(5)] for _ in range(2)]
        oTp = [A.alloc([T], BF16) for _ in range(4)]
        B_oTp = [[Buf() for _ in range(4)] for _ in range(4)]
        sqh = [A.alloc([512], BF16) for _ in range(2)]
        rsh = [A.alloc([512], F32) for _ in range(2)]
        B_sqh = [Buf(), Buf()]
        B_rsh = [Buf(), Buf()]
        At = [A.alloc([512], BF16) for _ in range(4)]
        B_At = [Buf() for _ in range(4)]
        rden = [A.alloc([512], F32) for _ in range(2)]
        B_rden = [Buf(), Buf()]
        for s in range(2):
            P.op("gpsimd", I("memset", ap=ka[s][64:128, :], constant=0.0), writes=[B_ka[s][4]])
            P.op("gpsimd", I("memset", ap=qa[s][64:128, :], constant=0.0), writes=[B_qa[s][4]])
            P.op("gpsimd", I("memset", ap=ka[s][64:67, :], constant=1.0), writes=[B_ka[s][4]])
        P.op("gpsimd", I("memset", ap=vtp[0][:, :, 64:128], constant=1.0), writes=B_vtp[0])
        P.op("gpsimd", I("memset", ap=vtp[1][:, :, 0:64], constant=1.0), writes=B_vtp[1])
        self._pcnt = 0

        wld = {}

        def preload(h):
            wld[h] = [self.load_w("wq4", wq4, B_wq4, self.fxqk_d[(o * 3 + j) * 16 + h, :, :]) for j in range(3)]

        def proj(h):
            s = h % 2
            (wq, B_wq), (wk, B_wk) = wld[h][0], wld[h][1]
            for j in range(3):
                P.dma("sync", I("dma_start", out=qa[s][64 + j:65 + j, :], in_=cp3[h:h + 1, j, :]), reads=[B_cp3], writes=[B_qa[s][4]], key=f"d_aug{s}")
            gq, gk = CO[("gq", o)], CO[("gk", o)]
            tails = []
            for blk in range(4):
                tsl = slice(blk * 512, (blk + 1) * 512)
                u = self._pcnt % 2
                pb = 5 + self._pcnt % 2
                self._pcnt += 1
                for kh in range(2):
                    mm = []
                    for kc in range(4 * kh, 4 * kh + 4):
                        mm.append(I("matmul", out=bank[pb][0:64, :], lhsT=wq[:, kc, :], rhs=hT[:, kc, tsl], start=(kc == 0), stop=(kc == 7)))
                        mm.append(I("matmul", out=bank[pb][64:128, :], lhsT=wk[:, kc, :], rhs=hT[:, kc, tsl], start=(kc == 0), stop=(kc == 7)))
                    P.group("tensor", mm, reads=[B_wq, B_wk] + hall(blk), writes=[B_bank[pb]])
                    if kh == 0:
                        yield
                P.op("scalar", I("activation", out=sqh[u][:, :], in_=bank[pb][:, :], func=AF.Square), reads=[B_bank[pb]], writes=[B_sqh[u]])

                def tail(u=u, pb=pb, tsl=tsl, blk=blk):
                    P.op("tensor", I("matmul", out=bank[7][:, :], lhsT=self.ones64bd[:, :], rhs=sqh[u][:, :], start=True, stop=True),
                         reads=[B_sqh[u], self.B_c], writes=[B_bank[7]])
                    P.op("scalar", I("activation", out=rsh[u][:, :], in_=bank[7][:, :], func=AF.Ln, bias=self.qkbias[:, 0:1], scale=self.qkscale[:, 0:1]),
                         reads=[B_bank[7], self.B_c], writes=[B_rsh[u]])
                    P.op("scalar", I("activation", out=rsh[u][:, :], in_=rsh[u][:, :], func=AF.Exp, scale=-0.5), reads=[B_rsh[u]], writes=[B_rsh[u]])
                    P.op("vector", I("scalar_tensor_tensor", out=qa[s][0:64, tsl], in0=bank[pb][0:64, :], scalar=cst[0:64, gq:gq + 1], in1=rsh[u][0:64, :],
                                     op0=ALU.mult, op1=ALU.mult), reads=[B_bank[pb], B_rsh[u], self.B_c], writes=[B_qa[s][blk]])
                    P.op("vector", I("scalar_tensor_tensor", out=ka[s][0:64, tsl], in0=bank[pb][64:128, :], scalar=cst[64:128, gk:gk + 1], in1=rsh[u][64:128, :],
                                     op0=ALU.mult, op1=ALU.mult), reads=[B_bank[pb], B_rsh[u], self.B_c], writes=[B_ka[s][blk]])

                if tails:
                    tails.pop(0)()
                tails.append(tail)
                yield
            while tails:
                tails.pop(0)()
            yield

        def vproj(h):
            s = h % 2
            vc = slice(s * 64, s * 64 + 64)
            w, B_w = wld[h][2]
            for g in range(4):
                P.group("tensor", [I("matmul", out=bank[7][:, i * 64:(i + 1) * 64], lhsT=hT[:, kc, (g * 4 + i) * 128:(g * 4 + i + 1) * 128], rhs=w[:, kc, :],
                                     start=(kc == 0), stop=(kc == 7)) for i in range(4) for kc in range(8)],
                        reads=[B_w] + hall(g), writes=[B_bank[7]])
                P.op("vector", I("tensor_copy", out=vtp[s][:, g * 4:(g + 1) * 4, vc], in_=bank[7][:, 0:256].rearrange("p (a b) -> p a b", b=64)),
                     reads=[B_bank[7]], writes=[B_vtp[s][g]])
                yield

        pend = []
        outq = []
        wos = {}

        def flush():
            while pend:
                pend.pop(0)()

        cnt = 0
        ocnt = 0
        preload(0)
        preload(1)
        for _ in proj(0):
            pass
        for _ in vproj(0):
            pass
        for pr in range(8):
            vs = pr % 4
            wo_, B_wo_ = self.load_w("wfo", wout, B_wout, self.fxout_d[o * 8 + pr, :, :])
            wos[pr] = (wo_, B_wo_)
            for half in range(2):
                h = 2 * pr + half
                s = h % 2
                flush()
                if h + 2 < 16:
                    preload(h + 2)

                def both(hn=h + 1, oq=list(outq)):
                    for g_ in oq:
                        yield from g_
                    if hn < 16:
                        yield from proj(hn)
                        yield from vproj(hn)
                outq.clear()
                work = both()
                tix = 0
                psl = slice(half * 64, half * 64 + 64)
                osl = slice((1 - half) * 64, (1 - half) * 64 + 64)
                for qb in range(4):
                    n = 4 * qb + 4
                    ob = 3 + (ocnt % 2)
                    ocnt += 1
                    order = list(range(n)) if qb == 0 else [0] + list(range(4 * qb, n)) + list(range(1, 4 * qb))
                    for oi, It in enumerate(order):
                        r = It - 4 * qb
                        c0 = max(r, 0) * 128
                        b = cnt % 3
                        is_first = (oi == 0)
                        is_last = (oi == n - 1)
                        a3 = cnt % 4
                        cnt += 1
                        ks = slice(It * 128, (It + 1) * 128)
                        qs = slice(qb * 512 + c0, (qb + 1) * 512)
                        cs = slice(c0, 512)
                        P.op("tensor", I("matmul", out=bank[b][:, cs], lhsT=ka[s][:, ks], rhs=qa[s][:, qs], start=True, stop=True),
                             reads=[B_ka[s][It // 4], B_ka[s][4], B_qa[s][qb], B_qa[s][4]], writes=[B_bank[b]])
                        P.op("scalar", I("activation", out=At[a3][:, cs], in_=bank[b][:, cs], func=AF.Exp, bias=cposT[:, It, h:h + 1], scale=1.0),
                             reads=[B_bank[b], B_cposT], writes=[B_At[a3]])
                        if r >= 0:
                            P.op("gpsimd", I("affine_select", out=At[a3][:, cs], in_=At[a3][:, cs], pattern=[[1, 512 - c0]], compare_op=ALU.is_ge,
                                             fill=0.0, base=0, channel_multiplier=-1), reads=[], writes=[B_At[a3]])

                        def pv(It=It, a3=a3, cs=cs, ob=ob, n=n, qb=qb, s=s, psl=psl, osl=osl, vs=vs, is_first=is_first, is_last=is_last):
                            P.op("tensor", I("matmul", out=bank[ob][:, cs], lhsT=vtp[s][:, It, :], rhs=At[a3][:, cs], start=is_first, stop=is_last),
                                 reads=[B_vtp[s][It // 4], B_At[a3]], writes=[B_bank[ob]])
                            if is_last:
                                u = qb % 2
                                P.op("vector", I("reciprocal", out=rden[u][psl, :], in_=bank[ob][osl, :]), reads=[B_bank[ob]], writes=[B_rden[u]])
                                P.op("vector", I("tensor_tensor", out=oTp[vs][psl, qb * 512:(qb + 1) * 512], in0=bank[ob][psl, :], in1=rden[u][psl, :], op=ALU.mult),
                                     reads=[B_bank[ob], B_rden[u]], writes=[B_oTp[vs][qb]])

                        if len(pend) >= 2:
                            pend.pop(0)()
                        pend.append(pv)
                        tix += 1
                        if tix % 2 == 0:
                            next(work, None)
                for _ in work:
                    pass
            flush()

            if pr % 2 == 1:
                (wa, B_wa), (wb, B_wb) = wos[pr - 1], wos[pr]
                va, vb_ = (pr - 1) % 4, pr % 4
                for m in range(8):
                    for blk in range(4):
                        b = 6 + (m * 4 + blk) % 2
                        tsl = slice(blk * 512, (blk + 1) * 512)
                        P.group("tensor", [I("matmul", out=bank[b][:, :], lhsT=wa[:, m * 128:(m + 1) * 128], rhs=oTp[va][:, tsl], start=True, stop=False),
                                           I("matmul", out=bank[b][:, :], lhsT=wb[:, m * 128:(m + 1) * 128], rhs=oTp[vb_][:, tsl], start=False, stop=True)],
                                reads=[B_wa, B_wb, B_oTp[va][blk], B_oTp[vb_][blk]], writes=[B_bank[b]])
                        xs = self.xT[:, m, tsl]
                        P.op("vector", I("tensor_tensor", out=xs, in0=bank[b][:, :], in1=xs, op=ALU.add), reads=[B_bank[b]], writes=[self.B_x[m][blk]])
        for g_ in outq:
            for _ in g_:
                pass

    def build(self):
        nc, P = self.nc, self.P
        ns = self.nseq
        dt_in = lambda name, shape: nc.dram_tensor(name, shape, F32, kind="ExternalInput").ap()
        self.xT_d = dt_in("xT", [ns, D, T])
        self.cst_d = dt_in("cst", [128, NCST])
        self.win_d = dt_in("win", [8 * NJ, 128, 8 * 256])
        self.wout_d = dt_in("wout", [8 * 8, 128, NJ * 128])
        self.abin_d = dt_in("abin", [2 * 20, 128, 8 * 128])
        self.about_d = dt_in("about", [2 * 8, 128, 8 * 128])
        self.fxqk_d = dt_in("fxqk", [2 * 3 * 16, 128, 8 * 64])
        self.fxf_d = dt_in("fxf", [2, 128, 8 * 16])
        self.fxout_d = dt_in("fxout", [2 * 8, 128, D])
        self.yT_d = nc.dram_tensor("yT", [ns, D, T], F32, kind="ExternalOutput").ap()
        with contextlib.ExitStack() as es:
            sb = lambda name, shape, dt: es.enter_context(nc.sbuf_tensor(name, shape, dt))
            self.xT = sb("xTs", [128, 8, T], F32)
            self.cst = sb("csts", [128, NCST], F32)
            self.ones_d = sb("ones_d", [128, 128], BF16)
            self.ones_ln = sb("ones_ln", [128, 128], F32)
            self.ones64bd = sb("ones64bd", [128, 128], BF16)
            self.qkscale = sb("qkscale", [128, 1], F32)
            self.qkbias = sb("qkbias", [128, 1], F32)
            self.ones_den = sb("ones_den", [128, 128], BF16)
            self.negones = sb("negones", [128, 128], BF16)
            self.negtri = sb("negtri", [128, 128], BF16)
            self.ident = sb("ident", [128, 128], F32)
            self.onesf = sb("onesf", [128, 128], F32)
            self.epsc = sb("epsc", [128, 1], F32)
            self.eps64c = sb("eps64c", [128, 1], F32)
            self.onec = sb("onec", [128, 1], F32)
            self.msb = sb("msb", [128, 4, 512], BF16)
            self.onesb = sb("onesb", [128, 512], BF16)
            SCRW = 33400
            scr = sb("scr", [128, SCRW], F32)
            self.arena = Arena(scr, SCRW)
            self.bank = [es.enter_context(nc.psum_tensor(f"bank{i}", [128, 512], F32)) for i in range(8)]
            self.B_bank = [Buf() for _ in range(8)]
            self.B_x = [[Buf() for _ in range(4)] for _ in range(8)]
            self.B_c = Buf()
            B_c = self.B_c
            P.dma("sync", I("dma_start", out=self.cst[:, :], in_=self.cst_d[:, :]), writes=[B_c], key="d_cst")
            g = "gpsimd"
            P.op(g, I("memset", ap=self.ones_d[:, :], constant=1.0 / D), writes=[B_c])
            P.op(g, I("memset", ap=self.ones_ln[:, :], constant=1.0 / 512), writes=[B_c])
            P.op(g, I("memset", ap=self.ones64bd[:, :], constant=0.0), writes=[B_c])
            P.op(g, I("memset", ap=self.ones64bd[0:64, 0:64], constant=1.0 / 64), writes=[B_c])
            P.op(g, I("memset", ap=self.ones64bd[64:128, 64:128], constant=1.0 / 64), writes=[B_c])
            P.op(g, I("memset", ap=self.qkscale[0:64, :], constant=64.0), writes=[B_c])
            P.op(g, I("memset", ap=self.qkscale[64:128, :], constant=1.0), writes=[B_c])
            P.op(g, I("memset", ap=self.qkbias[0:64, :], constant=64.0 * EPS), writes=[B_c])
            P.op(g, I("memset", ap=self.qkbias[64:128, :], constant=EPS), writes=[B_c])
            P.op(g, I("memset", ap=self.ones_den[:, :], constant=1.0), writes=[B_c])
            P.op(g, I("memset", ap=self.negones[:, :], constant=-1.0), writes=[B_c])
            P.op(g, I("memset", ap=self.onesf[:, :], constant=1.0), writes=[B_c])
            P.op(g, I("memset", ap=self.onesb[:, :], constant=1.0), writes=[B_c])
            P.op(g, I("memset", ap=self.epsc[:, :], constant=EPS), writes=[B_c])
            P.op(g, I("memset", ap=self.eps64c[:, :], constant=64.0 * EPS), writes=[B_c])
            P.op(g, I("memset", ap=self.onec[:, :], constant=1.0), writes=[B_c])
            P.op(g, I("affine_select", out=self.negtri[:, :], in_=self.negones[:, :], pattern=[[-1, 128]], compare_op=ALU.is_ge, fill=0.0,
                      base=0, channel_multiplier=1), writes=[B_c])
            P.op(g, I("affine_select", out=self.ident[:, :], in_=self.onesf[:, :], pattern=[[-1, 128]], compare_op=ALU.is_equal, fill=0.0,
                      base=0, channel_multiplier=1), writes=[B_c])
            for r in range(4):
                P.op(g, I("affine_select", out=self.msb[:, r, :], in_=self.onesb[:, :], pattern=[[1, 512]], compare_op=ALU.is_gt, fill=0.0,
                          base=-r * 128, channel_multiplier=-1), writes=[B_c])
            B_out = Buf()
            for sq_ in range(ns):
                xv = self.xT_d[sq_].rearrange("(kc p) t -> p kc t", p=128)
                for kc in range(8):
                    P.dma("sync", I("dma_start", out=self.xT[:, kc, :], in_=xv[:, kc, :]), writes=self.B_x[kc], key=f"d_x{kc}")
                for L in self.layers:
                    if "f1" in self.parts:
                        self.ffn(L, 0)
                    if "mix" in self.parts:
                        if L % 2 == 0:
                            self.mixer_even(L)
                        else:
                            self.mixer_odd(L)
                    if "f2" in self.parts:
                        self.ffn(L, 1)
                yv = self.yT_d[sq_].rearrange("(kc p) t -> p kc t", p=128)
                for kc in range(8):
                    P.dma("sync", I("dma_start", out=yv[:, kc, :], in_=self.xT[:, kc, :]), reads=self.B_x[kc], writes=[B_out], key="d_out")
            P.final_wait("sync", [B_out])
            sems = {k: es.enter_context(nc.semaphore(k)) for k in P.cnt.keys()}
            with nc.Block() as block:
                @block.sync
                def _(e):
                    P.replay("sync", e, sems)

                @block.tensor
                def _(e):
                    P.replay("tensor", e, sems)

                @block.scalar
                def _(e):
                    P.replay("scalar", e, sems)

                @block.vector
                def _(e):
                    P.replay("vector", e, sems)

                @block.gpsimd
                def _(e):
                    P.replay("gpsimd", e, sems)
        return nc


def chunk_k(W, c0, w):
    return np.ascontiguousarray(W[:, c0:c0 + w].reshape(8, 128, w).transpose(1, 0, 2)).reshape(128, 8 * w)


def pack_weights(inp):
    f32 = np.float32
    g = lambda k: np.asarray(inp[k], dtype=f32)
    win = np.empty((8 * NJ, 128, 2048), f32)
    wout = np.empty((64, 128, NJ * 128), f32)
    for L in range(DEPTH):
        for wh, (ki, ko) in enumerate((("ffn1_w_in", "ffn1_w_out"), ("ffn2_w_in", "ffn2_w_out"))):
            Wi = g(ki)[L]
            Wo = g(ko)[L]
            fi = L * 2 + wh
            gg = Wi[:, :DFF].reshape(8, 128, NJ, 128)
            uu = Wi[:, DFF:].reshape(8, 128, NJ, 128)
            cat = np.concatenate([gg, uu], axis=3)
            win[fi * NJ:(fi + 1) * NJ] = cat.transpose(2, 1, 0, 3).reshape(NJ, 128, 2048)
            wout[fi * 8:(fi + 1) * 8] = Wo.reshape(NJ, 128, 8, 128).transpose(2, 1, 0, 3).reshape(8, 128, NJ * 128)
    abin = np.empty((40, 128, 1024), f32)
    about = np.empty((16, 128, 1024), f32)
    for e in range(2):
        W = g("ab_w_in")[e]
        for cc in range(20):
            abin[e * 20 + cc] = chunk_k(W, cc * 128, 128)
        Wo = g("ab_w_out")[e]
        for m in range(8):
            about[e * 8 + m] = chunk_k(Wo, m * 128, 128)
    fxqk = np.empty((96, 128, 512), f32)
    fxf = np.empty((2, 128, 128), f32)
    fxout = np.empty((16, 128, D), f32)
    for o in range(2):
        W = g("fox_w_in")[o]
        for wh in range(3):
            for h in range(16):
                fxqk[(o * 3 + wh) * 16 + h] = chunk_k(W, wh * 1024 + h * 64, 64)
        fxf[o] = chunk_k(W, 3072, 16)
        fxout[o * 8:(o + 1) * 8] = g("fox_w_out")[o].reshape(8, 128, D)
    cst = np.zeros((128, NCST), f32)
    col8 = lambda v: v.reshape(8, 128).T
    for L in range(DEPTH):
        cst[:, CO[("n1", L)]:CO[("n1", L)] + 8] = col8(g("ffn1_norm")[L])
        cst[:, CO[("nm", L)]:CO[("nm", L)] + 8] = col8(g("mix_norm")[L])
        cst[:, CO[("n2", L)]:CO[("n2", L)] + 8] = col8(g("ffn2_norm")[L])
    for e in range(2):
        cw = g("conv_w")[e]
        c0 = CO[("cw", e)]
        cst[:, c0:c0 + 4 * CW] = cw.reshape(CW, 4, 128).transpose(2, 1, 0).reshape(128, 4 * CW)
        for nm, key in (("cb", "conv_b"), ("lg", "conv_ln_g"), ("lb", "conv_ln_b")):
            cst[:, CO[(nm, e)]:CO[(nm, e)] + 4] = g(key)[e].reshape(4, 128).T
    for o in range(2):
        cst[:, CO[("gq", o)]] = np.tile(g("fox_q_norm")[o], 2)
        cst[:, CO[("gk", o)]] = np.tile(g("fox_k_norm")[o], 2)
        cst[0:16, CO[("fb", o)]] = g("fox_f_bias")[o]
    return dict(cst=cst, win=win, wout=wout, abin=abin, about=about, fxqk=fxqk, fxf=fxf, fxout=fxout)


def kernel(**inputs):
    x = np.asarray(inputs["x"], dtype=np.float32)
    wts = pack_weights(inputs)
    mk = MK(nseq=2)
    nc = mk.build()
    in_maps = []
    for c in range(NCORES):
        m = dict(wts)
        m["xT"] = np.ascontiguousarray(x[2 * c:2 * c + 2].transpose(0, 2, 1))
        in_maps.append(m)
    res = run_bass_kernel_spmd(nc, in_maps, core_ids=list(range(NCORES)))
    out = np.empty_like(x)
    for c in range(NCORES):
        out[2 * c:2 * c + 2] = np.asarray(res.results[c]["yT"]).transpose(0, 2, 1)
    return out
```
